# Optimizing a Trainium2 kernel written in Bass

```python
import math
import jax, jax.numpy as jnp
from jax import lax
import numpy as np


D_MODEL = 1024
BATCH = 16
SEQ = 2048
DEPTH = 1
DEC_BATCH = 32
DEC_SEQ = 32
PAST_LEN = 2048

CHUNK = 64
Q_BLOCK = 128
FOX_HEADS = 8
FOX_HEAD_DIM = 64
DIFF_HEADS = 4
DIFF_HEAD_DIM = 64
FOX_WIDTH = FOX_HEADS * FOX_HEAD_DIM
DIFF_WIDTH = DIFF_HEADS * 2 * DIFF_HEAD_DIM
MIX_WIDTH = FOX_WIDTH + DIFF_WIDTH
IN_COLS = 3 * FOX_WIDTH + FOX_HEADS + 3 * DIFF_WIDTH
D_FF = 2816
CONV_WIDTH = 3
ROPE_THETA = 10000.0
LN_EPS = 1e-5
RMS_EPS = 1e-6
NEG_BIG = -1e30
FORGET_BIAS = 2.0
DEEPNORM_ALPHA = (2 * DEPTH) ** 0.25
DEEPNORM_BETA = (8 * DEPTH) ** -0.25

kernel_name = 'fox_diffattn_convffn_stream_step'


def _lambda_init(layer_idx):
    return 0.8 - 0.6 * math.exp(-0.3 * layer_idx)


def _rope(x, pos):
    d = x.shape[-1]
    inv = ROPE_THETA ** (-jnp.arange(0, d, 2, dtype=jnp.float32) / d)
    ang = pos.astype(jnp.float32)[:, None] * inv[None, :]
    cos = jnp.cos(ang)[None, :, None, :]
    sin = jnp.sin(ang)[None, :, None, :]
    x32 = x.astype(jnp.float32)
    x1, x2 = x32[..., : d // 2], x32[..., d // 2:]
    return jnp.concatenate([x1 * cos - x2 * sin, x2 * cos + x1 * sin], axis=-1).astype(x.dtype)


def _adaln(c, w_ada, b_ada):
    m = jnp.einsum('bd,de->be', jax.nn.silu(c), w_ada) + b_ada
    return [a[:, None, :] for a in jnp.split(m, 6, axis=-1)]


def _post_norm(x, h, gate, g, b):
    y = DEEPNORM_ALPHA * x.astype(jnp.float32) + gate.astype(jnp.float32) * h.astype(jnp.float32)
    mu = jnp.mean(y, axis=-1, keepdims=True)
    var = jnp.mean(jnp.square(y - mu), axis=-1, keepdims=True)
    return ((y - mu) * lax.rsqrt(var + LN_EPS) * g + b).astype(x.dtype)


def _project(u, w_in, b_f, pos):
    B, T, _ = u.shape
    z = jnp.einsum('btd,de->bte', u, w_in)
    sizes = [FOX_WIDTH, FOX_WIDTH, FOX_WIDTH, FOX_HEADS, DIFF_WIDTH, DIFF_WIDTH]
    idx = [sum(sizes[: i + 1]) for i in range(len(sizes))]
    fq, fk, fv, ff, dq, dk, dv = jnp.split(z, idx, axis=-1)
    fq = fq.reshape(B, T, FOX_HEADS, FOX_HEAD_DIM)
    fk = fk.reshape(B, T, FOX_HEADS, FOX_HEAD_DIM)
    fv = fv.reshape(B, T, FOX_HEADS, FOX_HEAD_DIM)
    logf = jax.nn.log_sigmoid((ff + b_f).astype(jnp.float32))
    dq = _rope(dq.reshape(B, T, 2 * DIFF_HEADS, DIFF_HEAD_DIM), pos)
    dk = _rope(dk.reshape(B, T, 2 * DIFF_HEADS, DIFF_HEAD_DIM), pos)
    dv = dv.reshape(B, T, DIFF_HEADS, 2 * DIFF_HEAD_DIM)
    return fq, fk, fv, logf, dq, dk, dv


def _fox_attend(q, k, v, cum_q, cum_k, q_pos, k_pos):
    s = jnp.einsum('bqhd,bkhd->bhqk', q, k).astype(jnp.float32) * (FOX_HEAD_DIM ** -0.5)
    s = s + jnp.transpose(cum_q, (0, 2, 1))[:, :, :, None] - jnp.transpose(cum_k, (0, 2, 1))[:, :, None, :]
    mask = k_pos[None, :] <= q_pos[:, None]
    p = jax.nn.softmax(jnp.where(mask, s, NEG_BIG), axis=-1)
    return jnp.einsum('bhqk,bkhd->bqhd', p.astype(v.dtype), v)


def _diff_attend(q, k, v, lam, q_pos, k_pos):
    B, Tq = q.shape[0], q.shape[1]
    S = k.shape[1]
    s = jnp.einsum('bqhd,bkhd->bhqk', q, k).astype(jnp.float32) * (DIFF_HEAD_DIM ** -0.5)
    mask = (k_pos[None, :] // CHUNK) <= (q_pos[:, None] // CHUNK)
    p = jax.nn.softmax(jnp.where(mask, s, NEG_BIG), axis=-1).reshape(B, DIFF_HEADS, 2, Tq, S)
    w = p[:, :, 0] - lam * p[:, :, 1]
    return jnp.einsum('bhqk,bkhe->bqhe', w.astype(v.dtype), v)


def _mix_output(fo, do, w_o, subln_g, lam_init):
    B, T = fo.shape[0], fo.shape[1]
    d32 = do.astype(jnp.float32)
    d32 = d32 * lax.rsqrt(jnp.mean(jnp.square(d32), axis=-1, keepdims=True) + RMS_EPS) * subln_g * (1.0 - lam_init)
    o = jnp.concatenate([fo.reshape(B, T, FOX_WIDTH), d32.astype(fo.dtype).reshape(B, T, DIFF_WIDTH)], axis=-1)
    return jnp.einsum('bte,ed->btd', o, w_o)


def _conv_ffn(u, conv_prev, w_up, conv_w, conv_b, w_down):
    T = u.shape[1]
    a, b = jnp.split(jnp.einsum('btd,df->btf', u, w_up), 2, axis=-1)
    a_pad = jnp.concatenate([conv_prev.astype(a.dtype), a], axis=1)
    conv = conv_b
    for j in range(CONV_WIDTH):
        conv = conv + a_pad[:, j:j + T] * conv_w[j]
    h = jax.nn.silu(conv) * b
    return jnp.einsum('btf,fd->btd', h, w_down), a_pad[:, -(CONV_WIDTH - 1):]


def _layer(x, c, past_fk, past_fv, past_logf, past_dk, past_dv, conv_prev,
           w_ada, b_ada, w_in, b_f, lambda_vecs, subln_g, w_o, ln1_g, ln1_b,
           w_up, conv_w, conv_b, w_down, ln2_g, ln2_b, lam_init):
    B, T, _ = x.shape
    P = past_fk.shape[1]
    q_pos = P + jnp.arange(T)
    k_pos = jnp.arange(P + T)
    sh1, sc1, g1, sh2, sc2, g2 = _adaln(c, w_ada, b_ada)
    u = x * (1.0 + sc1) + sh1
    fq, fk, fv, logf, dq, dk, dv = _project(u, w_in, b_f, q_pos)
    fk_all = jnp.concatenate([past_fk.astype(fk.dtype), fk], axis=1)
    fv_all = jnp.concatenate([past_fv.astype(fv.dtype), fv], axis=1)
    dk_all = jnp.concatenate([past_dk.astype(dk.dtype), dk], axis=1)
    dv_all = jnp.concatenate([past_dv.astype(dv.dtype), dv], axis=1)
    cum_k = jnp.cumsum(jnp.concatenate([past_logf.astype(jnp.float32), logf], axis=1), axis=1)
    cum_q = cum_k[:, P:]
    lv = lambda_vecs.astype(jnp.float32)
    lam = jnp.exp(jnp.sum(lv[0] * lv[1])) - jnp.exp(jnp.sum(lv[2] * lv[3])) + lam_init

    def attend(fq_b, dq_b, cq_b, qp_b):
        return (_fox_attend(fq_b, fk_all, fv_all, cq_b, cum_k, qp_b, k_pos),
                _diff_attend(dq_b, dk_all, dv_all, lam, qp_b, k_pos))

    if T > Q_BLOCK:
        nb = T // Q_BLOCK

        def blk(a):
            return a.reshape((B, nb, Q_BLOCK) + a.shape[2:]).swapaxes(0, 1)

        def unblk(a):
            return a.swapaxes(0, 1).reshape((B, T) + a.shape[3:])

        fo, do = lax.map(lambda args: attend(*args),
                         (blk(fq), blk(dq), blk(cum_q), q_pos.reshape(nb, Q_BLOCK)))
        fo, do = unblk(fo), unblk(do)
    else:
        fo, do = attend(fq, dq, cum_q, q_pos)

    h = _mix_output(fo, do, w_o, subln_g, lam_init)
    x = _post_norm(x, h, g1, ln1_g, ln1_b)
    u2 = x * (1.0 + sc2) + sh2
    f_out, conv_state = _conv_ffn(u2, conv_prev, w_up, conv_w, conv_b, w_down)
    x = _post_norm(x, f_out, g2, ln2_g, ln2_b)
    return x, (fk, fv, logf, dk, dv, conv_state)


def setup_inputs(seed: int = 0) -> dict:
    key = jax.random.key(seed)
    ks = jax.random.split(key, 32)
    f32 = jnp.float32
    n = lambda k, s: jax.random.normal(k, s, f32)
    L = DEPTH
    col_scale = jnp.concatenate([
        jnp.ones((2 * FOX_WIDTH,), f32), jnp.full((FOX_WIDTH,), DEEPNORM_BETA, f32),
        jnp.ones((FOX_HEADS + 2 * DIFF_WIDTH,), f32), jnp.full((DIFF_WIDTH,), DEEPNORM_BETA, f32)])
    return {
        'x_prompt': n(ks[0], (BATCH, SEQ, D_MODEL)),
        'x_sample': n(ks[1], (DEC_BATCH, DEC_SEQ, D_MODEL)),
        'c_prompt': n(ks[2], (BATCH, D_MODEL)),
        'c_sample': n(ks[3], (DEC_BATCH, D_MODEL)),
        'cache_fox_k': n(ks[4], (L, DEC_BATCH, PAST_LEN, FOX_HEADS, FOX_HEAD_DIM)),
        'cache_fox_v': n(ks[5], (L, DEC_BATCH, PAST_LEN, FOX_HEADS, FOX_HEAD_DIM)),
        'cache_fox_logf': jax.nn.log_sigmoid(FORGET_BIAS + n(ks[6], (L, DEC_BATCH, PAST_LEN, FOX_HEADS))),
        'cache_diff_k': n(ks[7], (L, DEC_BATCH, PAST_LEN, 2 * DIFF_HEADS, DIFF_HEAD_DIM)),
        'cache_diff_v': n(ks[8], (L, DEC_BATCH, PAST_LEN, DIFF_HEADS, 2 * DIFF_HEAD_DIM)),
        'state_ffn_conv': n(ks[9], (L, DEC_BATCH, CONV_WIDTH - 1, D_FF)),
        'w_ada': n(ks[10], (L, D_MODEL, 6 * D_MODEL)) * (0.5 * D_MODEL ** -0.5),
        'b_ada': 0.01 * n(ks[11], (L, 6 * D_MODEL)),
        'w_in': n(ks[12], (L, D_MODEL, IN_COLS)) * (D_MODEL ** -0.5) * col_scale,
        'b_f': FORGET_BIAS + 0.5 * n(ks[13], (L, FOX_HEADS)),
        'lambda_vecs': 0.1 * n(ks[14], (L, 4, DIFF_HEAD_DIM)),
        'subln_g': 1.0 + 0.02 * n(ks[15], (L, 2 * DIFF_HEAD_DIM)),
        'w_o': n(ks[16], (L, MIX_WIDTH, D_MODEL)) * (MIX_WIDTH ** -0.5) * DEEPNORM_BETA,
        'ln1_g': 1.0 + 0.02 * n(ks[17], (L, D_MODEL)),
        'ln1_b': 0.02 * n(ks[18], (L, D_MODEL)),
        'w_up': n(ks[19], (L, D_MODEL, 2 * D_FF)) * (D_MODEL ** -0.5),
        'conv_w': n(ks[20], (L, CONV_WIDTH, D_FF)) * (CONV_WIDTH ** -0.5),
        'conv_b': 0.02 * n(ks[21], (L, D_FF)),
        'w_down': n(ks[22], (L, D_FF, D_MODEL)) * (D_FF ** -0.5) * DEEPNORM_BETA,
        'ln2_g': 1.0 + 0.02 * n(ks[23], (L, D_MODEL)),
        'ln2_b': 0.02 * n(ks[24], (L, D_MODEL)),
    }


def reference(x_prompt, x_sample, c_prompt, c_sample, cache_fox_k, cache_fox_v, cache_fox_logf,
              cache_diff_k, cache_diff_v, state_ffn_conv, w_ada, b_ada, w_in, b_f, lambda_vecs,
              subln_g, w_o, ln1_g, ln1_b, w_up, conv_w, conv_b, w_down, ln2_g, ln2_b):
    B = x_prompt.shape[0]
    dt = x_prompt.dtype
    e_fk = jnp.zeros((B, 0, FOX_HEADS, FOX_HEAD_DIM), dt)
    e_logf = jnp.zeros((B, 0, FOX_HEADS), jnp.float32)
    e_dk = jnp.zeros((B, 0, 2 * DIFF_HEADS, DIFF_HEAD_DIM), dt)
    e_dv = jnp.zeros((B, 0, DIFF_HEADS, 2 * DIFF_HEAD_DIM), dt)
    e_conv = jnp.zeros((B, CONV_WIDTH - 1, D_FF), dt)
    yp, ys = x_prompt, x_sample
    p_states, s_states = [], []
    for l in range(DEPTH):
        lam_init = _lambda_init(l)
        wts = (w_ada[l], b_ada[l], w_in[l], b_f[l], lambda_vecs[l], subln_g[l], w_o[l],
               ln1_g[l], ln1_b[l], w_up[l], conv_w[l], conv_b[l], w_down[l], ln2_g[l], ln2_b[l])
        yp, st_p = _layer(yp, c_prompt, e_fk, e_fk, e_logf, e_dk, e_dv, e_conv, *wts, lam_init)
        ys, st_s = _layer(ys, c_sample, cache_fox_k[l], cache_fox_v[l], cache_fox_logf[l],
                          cache_diff_k[l], cache_diff_v[l], state_ffn_conv[l], *wts, lam_init)
        p_states.append(st_p)
        s_states.append(st_s)
    p_fk, p_fv, p_logf, p_dk, p_dv, p_conv = [jnp.stack(a, axis=0) for a in zip(*p_states)]
    s_fk, s_fv, s_logf, s_dk, s_dv, s_conv = [jnp.stack(a, axis=0) for a in zip(*s_states)]
    return (yp, ys, p_fk, p_fv, p_logf, p_dk, p_dv, p_conv, s_fk, s_fv, s_logf, s_dk, s_dv, s_conv)
```

```python
import numpy as np
import os
import ml_dtypes
from contextlib import ExitStack
import concourse.bass as bass
import concourse.mybir as mybir
from concourse.bass_utils import run_bass_kernel_spmd

F32 = mybir.dt.float32
BF = mybir.dt.bfloat16
AF = mybir.ActivationFunctionType
ALU = mybir.AluOpType

T = 2048
D = 1024
DFF = 2816
NF = 22
P_LEN = 2048
TS = 32
ALPHA = 2.0 ** 0.25
LAM_INIT = 0.8 - 0.6
WK = 2112
ENG = ('pe', 'act', 'dve', 'pool', 'sp')


def I(name, *a, **k):
    return (name, a, k)


class Prog:
    G = None

    def __init__(s, nc, name):
        s.nc = nc; s.name = name; s.ops = []; s.lastw = {}; s.rd = {}; s.dma_cnt = {}; s.spd = []; s.pld = []

    def op(s, eng, fn, r=(), w=(), dma=None):
        i = len(s.ops); deps = set(); raw = set()
        if eng in ('sp', 'pool') and dma is not None and fn is not None:
            nd = 1
            for ap in (fn[2].get('out'), fn[2].get('in_')):
                try:
                    shp = list(ap.shape)
                    n_ = 1
                    for d_ in shp[:-1]:
                        n_ *= int(d_)
                    nd = max(nd, n_)
                except Exception:
                    nd = max(nd, 1024)
            tot = nd
            lst = s.spd if eng == 'sp' else s.pld
            for (j_, ndj) in reversed(lst):
                tot += ndj
                if tot > (2500 if eng == 'sp' else 6000):
                    deps.add(j_)
                    break
            lst.append((i, nd))
        for x in r:
            if x in s.lastw:
                deps.add(s.lastw[x]); raw.add(s.lastw[x])
        for x in w:
            if x in s.lastw:
                deps.add(s.lastw[x])
            for j in s.rd.get(x, ()):
                deps.add(j)
        for x in r:
            s.rd.setdefault(x, []).append(i)
        for x in w:
            s.lastw[x] = i; s.rd[x] = []
        deps.discard(i)
        o = dict(i=i, eng=eng, fn=fn, deps=deps, raw=raw, dma=dma, sig=False)
        if dma is not None:
            s.dma_cnt[dma] = s.dma_cnt.get(dma, 0) + 1
            o['dval'] = 16 * s.dma_cnt[dma]
        s.ops.append(o)
        return i

    def emit(s):
        nc = s.nc
        last = {}
        for o in s.ops:
            if o['dma'] is not None:
                last[o['dma']] = o['i']
        fin = dict(i=len(s.ops), eng='sp', fn=None, deps=set(last.values()), raw=set(), dma=None, sig=False)
        s.ops.append(fin)
        for o in s.ops:
            o['waits'] = []
            for j in sorted(o['deps']):
                d = s.ops[j]
                if d['dma'] is not None:
                    o['waits'].append(('dma', d['dma'], d['dval']))
                else:
                    if d['eng'] == o['eng'] and (d['eng'] == 'pe' or j not in o['raw']):
                        continue
                    d['sig'] = True
                    o['waits'].append(('eng', j))
        cnt = {e: 0 for e in ENG}
        for o in s.ops:
            if o['dma'] is None and o['sig']:
                cnt[o['eng']] += 1; o['cval'] = cnt[o['eng']]
        G = s.G
        keys = list(s.dma_cnt)
        assert len(keys) <= len(G['dsem']), (s.name, len(keys))
        kidx = {k: n for n, k in enumerate(keys)}
        esem = G['esem']; ebase = dict(G['ebase']); dbase = list(G['dbase'])
        with ExitStack() as es:
            block = es.enter_context(nc.Block())

            def body(engname):
                def f(e):
                    seen = {}
                    for o in s.ops:
                        if o['eng'] != engname:
                            continue
                        for wt in o['waits']:
                            if wt[0] == 'dma':
                                n = kidx[wt[1]]; sem = G['dsem'][n]; val = dbase[n] + wt[2]; key = ('d', n)
                            else:
                                d = s.ops[wt[1]]; sem = esem[d['eng']]; val = ebase[d['eng']] + d['cval']; key = ('e', d['eng'])
                            if seen.get(key, 0) >= val:
                                continue
                            seen[key] = val
                            e.wait_ge(sem, val)
                        if o['fn'] is None:
                            continue
                        nm, a_, k_ = o['fn']
                        inst = getattr(e, nm)(*a_, **k_)
                        if o['dma'] is not None:
                            inst.then_inc(G['dsem'][kidx[o['dma']]], 16)
                        elif o['sig']:
                            inst.then_inc(esem[engname], 1)
                return f
            block.tensor(body('pe')); block.scalar(body('act')); block.vector(body('dve'))
            block.gpsimd(body('pool')); block.sync(body('sp'))
        for e_ in ENG:
            G['ebase'][e_] += cnt[e_]
        for k, n in kidx.items():
            G['dbase'][n] += 16 * s.dma_cnt[k]


_uid = [0]


def _pn():
    _uid[0] += 1
    return f"u{_uid[0]}"


def build():
    nc = bass.Bass("TRN2", target_bir_lowering=False)
    STOP = os.environ.get('KSTOP', 'zz')
    DBG = bool(os.environ.get('KDBG'))
    LVL = int(os.environ.get('KLVL', '99'))
    L3 = int(os.environ.get('KL3', '99'))

    def din(name, shape, dt=F32):
        return nc.dram_tensor(name, list(shape), dt, kind="ExternalInput").ap()

    def dout(name, shape, dt=F32):
        return nc.dram_tensor(name, list(shape), dt, kind="ExternalOutput").ap()

    def dscr(name, shape, dt=BF):
        return nc.dram_tensor(name, list(shape), dt, kind=("ExternalOutput" if DBG else "Internal")).ap()

    xT_p = din("xT_p", [2, 128, 8, T]); x_p = din("x_p", [2, T, D])
    xT_s = din("xT_s", [128, 8, 128]); x_s = din("x_s", [128, D])
    cT = din("cT", [128, 8, 6])
    fkT_s = din("fkT_s", [4, 512, P_LEN]); fv_s = din("fv_s", [4, P_LEN, 512]); lfT_s = din("lfT_s", [4, 8, P_LEN])
    dkT_s = din("dkT_s", [4, 512, P_LEN]); dv_s = din("dv_s", [4, P_LEN, 512]); convT_s = din("convT_s", [128, NF, 4, 2])
    w_ada = din("w_ada", [D, 6 * D]); b_adaT = din("b_adaT", [128, 48]); b_ada_g = din("b_ada_g", [1, 2048])
    w_in = din("w_in", [D, 4104]); nb_f = din("b_f", [8, 1]); lam_v = din("lam_v", [1, 256]); subln = din("subln", [128, 1])
    w_o = din("w_o", [D, D]); lnp = din("lnp", [4, D]); w_up = din("w_up", [D, 2 * DFF]); convw = din("convw", [128, NF, 4])
    w_down = din("w_down", [DFF, D])
    ident_d = din("ident", [128, 128], BF); tri_d = din("tri", [128, 128], BF)
    ropeC_p = din("ropeC_p", [128, T]); ropeS_p = din("ropeS_p", [128, T])
    ropeC_s = din("ropeC_s", [128, 128]); ropeS_s = din("ropeS_s", [128, 128])
    sel_d = din("sel", [6, 3, 128])
    ones3_d = din("ones3", [3, T], BF)
    y_p = dout("y_p", [2, T, D]); y_s = dout("y_s", [128, D])
    fkT_o = [dout("fkT_o0", [512, T]), dout("fkT_o1", [512, T]), dout("sfkT_o", [512, 128])]
    fv_o = [dout("fv_o0", [T, 512]), dout("fv_o1", [T, 512]), dout("sfv_o", [128, 512])]
    lfT_o = [dout("lfT_o0", [8, T]), dout("lfT_o1", [8, T]), dout("slfT_o", [8, 128])]
    dkT_o = [dout("dkT_o0", [512, T]), dout("dkT_o1", [512, T]), dout("sdkT_o", [512, 128])]
    dv_o = [dout("dv_o0", [T, 512]), dout("dv_o1", [T, 512]), dout("sdv_o", [128, 512])]
    convT_o = [dout("convT_o0", [128, NF, 1, 2]), dout("convT_o1", [128, NF, 1, 2]), dout("sconvT_o", [128, NF, 4, 2])]
    wi4 = dscr("wi4", [8, 128, 8, 512]); wiF = dscr("wiF", [128, 8, 8]); wo4 = dscr("wo4", [128, 8, D]); wu4 = dscr("wu4", [NF, 128, 8, 256]); wd4 = dscr("wd4", [128, NF, D])
    TG = [T, T, 128]
    QF = [dscr(f"QF{g}", [512, TG[g]]) for g in range(3)]
    KF = [dscr(f"KF{g}", [512, TG[g]]) for g in range(3)]
    QD = [dscr(f"QD{g}", [512, TG[g]]) for g in range(3)]
    KD = [dscr(f"KD{g}", [512, TG[g]]) for g in range(3)]
    VF = [dscr(f"VF{g}", [TG[g], 512]) for g in range(3)]
    VD = [dscr(f"VD{g}", [TG[g], 512]) for g in range(3)]
    CS = [dscr("CS0", [8, 6, T]), dscr("CS1", [8, 6, T])] + [dscr(f"CSs{s}", [8, 6, WK]) for s in range(4)]
    OTOK = [dscr(f"OTOK{g}", [TG[g], D]) for g in range(3)]
    MG = dscr("MG", [6, 2048], F32)

    with ExitStack() as gs:
        def sb(name, shape, dt=F32):
            return gs.enter_context(nc.sbuf_tensor(name, list(shape), dt))
        Prog.G = dict(esem={e_: gs.enter_context(nc.semaphore(f"ge_{e_}")) for e_ in ENG},
                      dsem=[gs.enter_context(nc.semaphore(f"gd_{n_}")) for n_ in range(32)],
                      ebase={e_: 0 for e_ in ENG}, dbase=[0] * 32)
        ident = sb("ident_sb", [128, 128], BF); tri = sb("tri_sb", [128, 128], BF)
        modF = sb("modF", [128, 48, 6]); sc1p = sb("sc1p", [128, 8, 6]); sc2p = sb("sc2p", [128, 8, 6])
        lnbc = sb("lnbc", [128, 4, D]); neg_lam = sb("neg_lam", [128, 1]); wrs = sb("wrs", [128, 1])
        nbf = sb("nbf", [8, 1]); cw = sb("cw", [128, NF, 4])

        with ExitStack() as st:
            def sb(name, shape, dt=F32):
                _uid[0] += 1
                return st.enter_context(nc.sbuf_tensor(f"{name}_u{_uid[0]}", list(shape), dt))
            P = Prog(nc, "s0")
            cts = sb("cts", [128, 8, 6]); scb = sb("scb", [128, 8, 6], BF)
            modG = sb("modG", [6, 2048])
            wa = [sb(f"wa{i}", [128, 8, 1024], BF) for i in range(2)]
            badT = sb("badT", [128, 48]); bag = sb("bag", [6, 2048])
            lv = sb("lv", [128, 256]); lt = sb("lt", [128, 128]); ls = sb("ls", [128, 2]); le = sb("le", [128, 2])
            subl = sb("subl", [128, 1])
            psA = st.enter_context(nc.psum_tensor("psA_" + _pn(), [128, 512], F32))
            psG = [st.enter_context(nc.psum_tensor(f"psG{i}_" + _pn(), [128, 512], F32)) for i in range(4)]
            for u_ in range(8):
                P.op('pool', I('dma_start', out=wi4[u_], in_=w_in[:, u_ * 512:(u_ + 1) * 512].rearrange("(k p) c -> p k c", p=128)), dma=('wc', u_ % 4))
            P.op('pool', I('dma_start', out=wiF[:, :, :], in_=w_in[:, 4096:4104].rearrange("(k p) c -> p k c", p=128)), dma=('wc', 0))
            P.op('pool', I('dma_start', out=wo4[:, :, :], in_=w_o[:, :].rearrange("(k p) c -> p k c", p=128)), dma=('wc', 1))
            P.op('sp', I('dma_start', out=cts[:], in_=cT[:, :, :]), w=['cts'], dma='l0')
            P.op('sp', I('dma_start', out=badT[:], in_=b_adaT[:, :]), w=['badT'], dma='l1')
            P.op('sp', I('dma_start', out=bag[:], in_=b_ada_g[0, :].partition_broadcast(6)), w=['bag'], dma='l2')
            P.op('sp', I('dma_start', out=ident[:], in_=ident_d[:, :]), w=['ident'], dma='l4')
            P.op('sp', I('dma_start', out=tri[:], in_=tri_d[:, :]), w=['tri'], dma='l5')
            P.op('sp', I('dma_start', out=nbf[:], in_=nb_f[:, :]), w=['nbf'], dma='l6')
            P.op('dve', I('tensor_scalar', out=nbf[:], in0=nbf[:], scalar1=-1.0, scalar2=None, op0=ALU.mult), r=['nbf'], w=['nbf'])
            P.op('sp', I('dma_start', out=cw[:], in_=convw[:, :, :]), w=['cw'], dma='l7')
            P.op('sp', I('dma_start', out=subl[:], in_=subln[:, :]), w=['subl'], dma='l8')
            P.op('sp', I('dma_start', out=lv[:], in_=lam_v[0, :].partition_broadcast(128)), w=['lv'], dma='l9')
            for j in range(4):
                P.op('sp', I('dma_start', out=lnbc[:, j, :], in_=lnp[j, :].partition_broadcast(128)), w=['lnbc'], dma='l10')
            P.op('act', I('activation', out=scb[:], in_=cts[:], func=AF.Silu), r=['cts'], w=['scb'])
            for ch in range(6):
                P.op('pool', I('dma_start', out=wa[ch % 2][:], in_=w_ada[:, ch * 1024:(ch + 1) * 1024].rearrange("(k p) c -> p k c", p=128)),
                     w=[('wa', ch % 2)], dma=('wa', ch % 2))
                for jj in range(8):
                    j = ch * 8 + jj
                    for k in range(8):
                        P.op('pe', I('matmul', psA[:, j * 6:(j + 1) * 6], lhsT=wa[ch % 2][:, k, jj * 128:(jj + 1) * 128], rhs=scb[:, k, :], start=(k == 0), stop=(k == 7)),
                             r=[('wa', ch % 2), 'scb'], w=['psA'])
                if ch in (2, 5):
                    gi = 0 if ch == 2 else 1
                    for half in range(2):
                        for k in range(8):
                            P.op('pe', I('matmul', psG[gi * 2 + half][0:6, :], lhsT=scb[:, k, :], rhs=wa[ch % 2][:, k, half * 512:(half + 1) * 512], start=(k == 0), stop=(k == 7)),
                                 r=[('wa', ch % 2), 'scb'], w=[('psG', gi * 2 + half)])
                        P.op('dve', I('tensor_tensor', out=modG[:, gi * 1024 + half * 512: gi * 1024 + (half + 1) * 512], in0=psG[gi * 2 + half][0:6, :],
                                                                               in1=bag[:, gi * 1024 + half * 512: gi * 1024 + (half + 1) * 512], op=ALU.add),
                             r=[('psG', gi * 2 + half), 'bag'], w=['modG'])
            for j in range(6):
                P.op('dve', I('tensor_tensor', out=modF[:, :, j], in0=psA[:, 0:288].rearrange("p (c j) -> p c j", j=6)[:, :, j], in1=badT[:, :], op=ALU.add),
                     r=['psA', 'badT'], w=['modF'])
            P.op('dve', I('tensor_scalar', out=sc1p[:], in0=modF[:, 8:16, :], scalar1=1.0, scalar2=None, op0=ALU.add), r=['modF'], w=['sc1p'])
            P.op('dve', I('tensor_scalar', out=sc2p[:], in0=modF[:, 32:40, :], scalar1=1.0, scalar2=None, op0=ALU.add), r=['modF'], w=['sc2p'])
            P.op('dve', I('tensor_tensor', out=lt[:, 0:64], in0=lv[:, 0:64], in1=lv[:, 64:128], op=ALU.mult), r=['lv'], w=['lt'])
            P.op('dve', I('tensor_tensor', out=lt[:, 64:128], in0=lv[:, 128:192], in1=lv[:, 192:256], op=ALU.mult), r=['lv'], w=['lt'])
            P.op('dve', I('reduce_sum', out=ls[:, :], in_=lt[:, :].rearrange("p (a b) -> p a b", a=2), axis=mybir.AxisListType.X), r=['lt'], w=['ls'])
            P.op('act', I('activation', out=le[:], in_=ls[:], func=AF.Exp), r=['ls'], w=['le'])
            P.op('dve', I('tensor_tensor', out=neg_lam[:], in0=le[:, 1:2], in1=le[:, 0:1], op=ALU.subtract), r=['le'], w=['nl'])
            P.op('dve', I('tensor_scalar', out=neg_lam[:], in0=neg_lam[:], scalar1=-LAM_INIT, scalar2=None, op0=ALU.add), r=['nl'], w=['nl'])
            P.op('dve', I('tensor_scalar', out=wrs[:], in0=subl[:], scalar1=1.0 - LAM_INIT, scalar2=None, op0=ALU.mult), r=['subl'], w=['wrs'])
            P.op('sp', I('dma_start', out=MG[:, :], in_=modG[:]), r=['modG'], dma='mg')
            if DBG:
                dbgF = dout("dbg_modF", [128, 288]); dbgG = dout("dbg_modG", [6, 2048]); dbgM = dout("dbg_misc", [2, 128, 1])
                P.op('sp', I('dma_start', out=dbgF[:, :], in_=modF[:].rearrange("p a b -> p (a b)")), r=['modF'], dma='dbg')
                P.op('sp', I('dma_start', out=dbgG[:, :], in_=modG[:]), r=['modG'], dma='dbg')
                P.op('sp', I('dma_start', out=dbgM[0], in_=neg_lam[:]), r=['nl'], dma='dbg')
                P.op('sp', I('dma_start', out=dbgM[1], in_=wrs[:]), r=['wrs'], dma='dbg')
            P.emit()
        if STOP == 's0':
            return nc

        for g in [int(c_) for c_ in os.environ.get('KG', '012')]:
            Tg = TG[g]; N = min(512, Tg); nblk = Tg // N; ntile = Tg // 128
            with ExitStack() as st:
                def sb(name, shape, dt=F32):
                    _uid[0] += 1
                    return st.enter_context(nc.sbuf_tensor(f"{name}_u{_uid[0]}", list(shape), dt))
                P = Prog(nc, f"s1g{g}")
                uT = sb("uT", [128, 8, Tg], BF)
                xs = [sb(f"xs{i}", [128, 8, N]) for i in range(1)] * 2
                wr = [sb(f"wr{i}", [128, 8, 512], BF) for i in range(3)]
                wff = sb("wff", [128, 8, 8], BF)
                rC = sb("rC", [128, Tg]); rS = sb("rS", [128, Tg])
                t1 = [sb(f"t1_{i}", [128, N]) for i in range(2)]; t2 = [sb(f"t2_{i}", [128, N]) for i in range(2)]
                r32 = [sb(f"r32_{i}", [128, N]) for i in range(2)]
                stg = [sb(f"stg{i}", [128, N], BF) for i in range(2)]
                v32 = [sb(f"v32_{i}", [128, 512]) for i in range(2)]; vb = [sb(f"vb{i}", [128, 512], BF) for i in range(2)]
                lf = sb("lf", [8, Tg]); lfp = sb("lfp", [8, P_LEN if g == 2 else 8]); cum = sb("cum", [8, WK]); r1 = sb("r1", [8, WK]); ones8 = sb("ones8", [8, WK])
                spl = sb("spl", [8, 6, WK], BF)
                ps = [st.enter_context(nc.psum_tensor(f"ps{i}_" + _pn(), [128, 512], F32)) for i in range(8)]
                pctr = [0]

                def nbank():
                    pctr[0] += 1
                    return pctr[0] % 8
                sctr = [0]
                xTsrc = xT_p[g] if g < 2 else xT_s
                bg = []
                if g == int(os.environ.get('KG', '012')[0]):
                    for f_ in range(NF):
                        bg.append(I('dma_start', out=wu4[f_][:, :, 0:128], in_=w_up[:, f_ * 128:(f_ + 1) * 128].rearrange("(k p) c -> p k c", p=128)))
                        bg.append(I('dma_start', out=wu4[f_][:, :, 128:256], in_=w_up[:, DFF + f_ * 128:DFF + (f_ + 1) * 128].rearrange("(k p) c -> p k c", p=128)))
                    for f0 in (0, 11):
                        bg.append(I('dma_start', out=wd4[:, f0:f0 + 11, :], in_=w_down[f0 * 128:(f0 + 11) * 128, :].rearrange("(k p) c -> p k c", p=128)))
                bgn = [0]

                def bgpop():
                    if bg:
                        P.op('pool', bg.pop(0), dma=('bgc', bgn[0] % 8)); bgn[0] += 1
                P.op('sp', I('dma_start', out=rC[:], in_=(ropeC_p if g < 2 else ropeC_s)[:, :]), w=['rC'], dma='rc')
                P.op('sp', I('dma_start', out=rS[:], in_=(ropeS_p if g < 2 else ropeS_s)[:, :]), w=['rS'], dma='rs')
                P.op('pool', I('memset', ones8[:], 1.0), w=['ones8'])
                for blk in range(nblk):
                    xb = xs[blk % 2]
                    P.op('sp', I('dma_start', out=xb[:], in_=xTsrc[:, :, blk * N:(blk + 1) * N]), w=[('xs', 0)], dma=('xs', 0))
                    for k in range(8):
                        if g < 2:
                            P.op('pool', I('tensor_scalar', out=uT[:, k, blk * N:(blk + 1) * N], in0=xb[:, k, :], scalar1=sc1p[:, k, g:g + 1], scalar2=modF[:, k, g:g + 1], op0=ALU.mult, op1=ALU.add),
                                 r=[('xs', 0)], w=[('uT', blk)])
                            bgpop()
                        else:
                            for s in range(4):
                                P.op('pool', I('tensor_scalar', out=uT[:, k, s * 32:(s + 1) * 32], in0=xb[:, k, s * 32:(s + 1) * 32], scalar1=sc1p[:, k, 2 + s:3 + s], scalar2=modF[:, k, 2 + s:3 + s], op0=ALU.mult, op1=ALU.add),
                                     r=[('xs', 0)], w=[('uT', blk)])
                uTall = [('uT', b) for b in range(nblk)]

                def loadw(c0, ncols=512):
                    i = sctr[0] % 3; sctr[0] += 1
                    P.op('sp', I('dma_start', out=wr[i][:, :, :], in_=wi4[c0 // 512]), w=[('wr', i)], dma=('wr', i))
                    return i

                def fm_group(slot, co, blk):
                    b = nbank()
                    for k in range(8):
                        P.op('pe', I('matmul', ps[b][:, 0:N], lhsT=wr[slot][:, k, co:co + 128], rhs=uT[:, k, blk * N:(blk + 1) * N], start=(k == 0), stop=(k == 7)),
                             r=[('wr', slot), ('uT', blk)], w=[('ps', b)])
                    return b
                octr = [0]
                for which, c0 in (((('q', 0), ('k', 512)) if not os.environ.get('KQ') else (('q', 0),)) if LVL >= 2 else ()):
                    slot = loadw(c0)
                    for c in range(4):
                        for blk in range(nblk):
                            b = fm_group(slot, c * 128, blk)
                            i = octr[0] % 2; octr[0] += 1
                            if which == 'q':
                                P.op('act', I('activation', out=stg[i][:], in_=ps[b][:, 0:N], func=AF.Identity), r=[('ps', b)], w=[('stg', i)])
                            else:
                                P.op('act', I('activation', out=r32[i][:], in_=ps[b][:, 0:N], func=AF.Identity), r=[('ps', b)], w=[('r32', i)])
                                P.op('pool', I('tensor_copy', out=stg[i][:], in_=r32[i][:]), r=[('r32', i)], w=[('stg', i)])
                                P.op('sp', I('dma_start', out=fkT_o[g][c * 128:(c + 1) * 128, blk * N:(blk + 1) * N], in_=r32[i][:]), r=[('r32', i)], dma=('r32o', i))
                            dst = (QF if which == 'q' else KF)[g]
                            P.op('sp', I('dma_start', out=dst[c * 128:(c + 1) * 128, blk * N:(blk + 1) * N], in_=stg[i][:]), r=[('stg', i)], dma=('stgo', i))
                for which, c0 in ((('q', 1024), ('k', 2048)) if LVL >= 3 else ()):
                    sa = loadw(c0); sbw = loadw(c0 + 512)
                    for c in range(4):
                        for blk in range(nblk):
                            ba = fm_group(sa, c * 128, blk); bb = fm_group(sbw, c * 128, blk)
                            i = octr[0] % 2; octr[0] += 1
                            P.op('dve', I('tensor_tensor', out=t1[i][:], in0=ps[ba][:, 0:N], in1=rC[:, blk * N:(blk + 1) * N], op=ALU.mult), r=[('ps', ba), 'rC'], w=[('t1', i)])
                            P.op('dve', I('tensor_tensor', out=t2[i][:], in0=ps[bb][:, 0:N], in1=rS[:, blk * N:(blk + 1) * N], op=ALU.mult), r=[('ps', bb), 'rS'], w=[('t2', i)])
                            P.op('pool', I('tensor_tensor', out=r32[i][:], in0=t1[i][:], in1=t2[i][:], op=ALU.add), r=[('t1', i), ('t2', i)], w=[('r32', i)])
                            bgpop()
                            P.op('act', I('activation', out=stg[i][:], in_=r32[i][:], func=AF.Identity), r=[('r32', i)], w=[('stg', i)])
                            dst = (QD if which == 'q' else KD)[g]
                            P.op('sp', I('dma_start', out=dst[c * 128:(c + 1) * 128, blk * N:(blk + 1) * N], in_=stg[i][:]), r=[('stg', i)], dma=('stgo', i))
                            if which == 'k':
                                P.op('sp', I('dma_start', out=dkT_o[g][c * 128:(c + 1) * 128, blk * N:(blk + 1) * N], in_=r32[i][:]), r=[('r32', i)], dma=('r32o', i))
                for c0, vo, vs in (((3072, fv_o[g], VF[g]), (3584, dv_o[g], VD[g])) if LVL >= 4 else ()):
                    slot = loadw(c0)
                    for tt in range(ntile):
                        b = nbank()
                        for k in range(8):
                            P.op('pe', I('matmul', ps[b][:, :], lhsT=uT[:, k, tt * 128:(tt + 1) * 128], rhs=wr[slot][:, k, :], start=(k == 0), stop=(k == 7)),
                                 r=[('wr', slot)] + uTall, w=[('ps', b)])
                        i = octr[0] % 2; octr[0] += 1
                        P.op('act', I('activation', out=v32[i][:], in_=ps[b][:, :], func=AF.Identity), r=[('ps', b)], w=[('v32', i)])
                        P.op('pool', I('tensor_copy', out=vb[i][:], in_=v32[i][:]), r=[('v32', i)], w=[('vb', i)])
                        P.op('sp', I('dma_start', out=vo[tt * 128:(tt + 1) * 128, :], in_=v32[i][:]), r=[('v32', i)], dma=('v32o', i))
                        P.op('sp', I('dma_start', out=vs[tt * 128:(tt + 1) * 128, :], in_=vb[i][:]), r=[('vb', i)], dma=('vbo', i))
                P.op('sp', I('dma_start', out=wff[:], in_=wiF[:, :, :]), w=['wff'], dma='wff')
                for blk in (range(nblk) if LVL >= 5 else ()):
                    b = nbank()
                    for k in range(8):
                        P.op('pe', I('matmul', ps[b][0:8, 0:N], lhsT=wff[:, k, :], rhs=uT[:, k, blk * N:(blk + 1) * N], start=(k == 0), stop=(k == 7)),
                             r=['wff', ('uT', blk)], w=[('ps', b)])
                    P.op('act', I('activation', out=lf[:, blk * N:(blk + 1) * N], in_=ps[b][0:8, 0:N], func=AF.Exp, bias=nbf[:, 0:1], scale=-1.0), r=[('ps', b)], w=['lf'])
                P.op('act', I('activation', out=lf[:], in_=lf[:], func=AF.Ln, bias=1.0, scale=1.0), r=['lf'], w=['lf'])
                P.op('dve', I('tensor_scalar', out=lf[:], in0=lf[:], scalar1=-1.0, scalar2=None, op0=ALU.mult), r=['lf'], w=['lf'])
                P.op('sp', I('dma_start', out=lfT_o[g][:, :], in_=lf[:]), r=['lf'], dma='lfo')

                def splits(width, csdst):
                    P.op('dve', I('tensor_scalar', out=r1[:, 0:width], in0=cum[:, 0:width], scalar1=8.0, scalar2=None, op0=ALU.mult), r=['cum'], w=['r1'])
                    for j in range(3):
                        P.op('dve', I('tensor_copy', out=spl[:, j, 0:width], in_=r1[:, 0:width]), r=['r1'], w=['spl'])
                        if j < 2:
                            P.op('dve', I('tensor_tensor', out=r1[:, 0:width], in0=r1[:, 0:width], in1=spl[:, j, 0:width], op=ALU.subtract), r=['r1', 'spl'], w=['r1'])
                    P.op('dve', I('tensor_scalar', out=spl[:, 3:6, 0:width], in0=spl[:, 0:3, 0:width], scalar1=-1.0, scalar2=None, op0=ALU.mult), r=['spl'], w=['spl'])
                    P.op('sp', I('dma_start', out=csdst[:, :, 0:width], in_=spl[:, :, 0:width]), r=['spl'], dma='cso')
                if LVL < 6:
                    pass
                elif g < 2:
                    P.op('dve', I('tensor_tensor_scan', out=cum[:, 0:T], data0=ones8[:, 0:T], data1=lf[:, :], initial=0.0, op0=ALU.mult, op1=ALU.add), r=['lf', 'ones8'], w=['cum'])
                    splits(T, CS[g])
                else:
                    for s in range(4):
                        P.op('sp', I('dma_start', out=lfp[:], in_=lfT_s[s]), w=['lfp'], dma='lfp')
                        P.op('dve', I('tensor_tensor_scan', out=cum[:, 0:P_LEN], data0=ones8[:, 0:P_LEN], data1=lfp[:, :], initial=0.0, op0=ALU.mult, op1=ALU.add), r=['lfp', 'ones8', 'spl'], w=['cum'])
                        P.op('dve', I('tensor_tensor_scan', out=cum[:, P_LEN:P_LEN + 32], data0=ones8[:, 0:32], data1=lf[:, s * 32:(s + 1) * 32], initial=cum[:, P_LEN - 1:P_LEN], op0=ALU.mult, op1=ALU.add), r=['lf', 'cum'], w=['cum'])
                        splits(P_LEN + 32, CS[2 + s])
                while bg:
                    bgpop()
                P.emit()
            if STOP == f's1g{g}':
                return nc

            with ExitStack() as st:
                def sb(name, shape, dt=F32):
                    _uid[0] += 1
                    return st.enter_context(nc.sbuf_tensor(f"{name}_u{_uid[0]}", list(shape), dt))
                P = Prog(nc, f"s2g{g}")
                Qa = [sb(f"Qa{i}", [128, Tg], BF) for i in range(2)]; Ka = [sb(f"Ka{i}", [128, WK], BF) for i in range(2)]
                Qd = [[sb(f"Qd{i}_{j}", [128, Tg], BF) for j in range(2)] for i in range(2)]; Kd = [sb(f"Kd{i}", [128, WK], BF) for i in range(2)]
                Vf = [sb(f"Vf{i}", [128, 17, 66], BF) for i in range(2)]; Vd = [sb(f"Vd{i}", [128, 17, 130], BF) for i in range(2)]
                NSB = 4; NPT = 4
                PT = [sb(f"PT{i}", [128, 512], BF) for i in range(NPT)]
                OT = sb("OT", [128, 16, D], BF)
                rec = [sb(f"rec{i}", [128, 4]) for i in range(2)]; nl = [sb(f"nl{i}", [128, 4]) for i in range(2)]
                a32 = [sb(f"a32_{i}", [128, 4, 128]) for i in range(2)]; d32 = [sb(f"d32_{i}", [128, 4, 128]) for i in range(2)]
                mhalf = sb("mhalf", [128, 4])
                P.op('pool', I('memset', mhalf[:], -0.5), w=['mhalf'])
                junk = sb("junk", [128, 128]); ss = [sb(f"ss{i}", [128, 4]) for i in range(2)]; rstd = [sb(f"rstd{i}", [128, 4]) for i in range(2)]
                Sb = [st.enter_context(nc.psum_tensor(f"Sb{i}_" + _pn(), [128, 512], F32)) for i in range(NSB)]
                Ob = [st.enter_context(nc.psum_tensor(f"Ob{i}_" + _pn(), [128, 512], F32)) for i in range(4)]
                for i in range(2):
                    P.op('pool', I('memset', Qa[i][64:128, :], 0.0), w=[('Qa', i)])
                    P.op('sp', I('dma_start', out=Qa[i][67:70, 0:min(Tg, T)], in_=ones3_d[:, 0:min(Tg, T)]), w=[('Qa', i)], dma=('Qa', i))
                    P.op('pool', I('memset', Ka[i][64:128, :], 1.0), w=[('Ka', i)])
                    P.op('pool', I('memset', Qd[i][0][64:128, :], 0.0), w=[('Qd', i)])
                    P.op('pool', I('memset', Qd[i][1][0:64, :], 0.0), w=[('Qd', i)])
                    P.op('pool', I('memset', Vf[i][:, :, 64:66], 1.0), w=[('Vf', i)])
                    P.op('pool', I('memset', Vd[i][:, :, 128:130], 1.0), w=[('Vd', i)])
                sctr = [0]; pctr = [0]; uctr = [0]
                nseq = 1 if g < 2 else 4
                Lq = Tg if g < 2 else 32
                npast = 0 if g < 2 else 16
                for s in range(nseq):
                    cs = CS[g] if g < 2 else CS[2 + s]
                    qc0 = 0 if g < 2 else s * 32
                    qpos0 = 0 if g < 2 else P_LEN
                    nqb = Lq // 512 if g < 2 else 1
                    QB = 512 if g < 2 else 32

                    def attend(kind, h, b):
                        subs = (0,) if kind == 'f' else (0, 1)
                        Kt = Ka[b] if kind == 'f' else Kd[b]
                        Qts = [Qa[b]] if kind == 'f' else Qd[b]
                        Vt = Vf[b] if kind == 'f' else Vd[b]
                        kr = ('Ka', b) if kind == 'f' else ('Kd', b)
                        qr = ('Qa', b) if kind == 'f' else ('Qd', b)
                        vr = ('Vf', b) if kind == 'f' else ('Vd', b)
                        KR = 128
                        VW = 65 if kind == 'f' else 129
                        for qb in range(nqb):
                            u = uctr[0] % 2; uctr[0] += 1
                            for sub in subs:
                                pb0 = 0
                                Qt = Qts[sub]
                                if g < 2:
                                    kts = list(range(0, 4 * qb + 4))
                                else:
                                    kts = list(range(17))
                                if kind == 'f':
                                    obk = [Ob[u * 2]] * 4 if g < 2 else [Ob[u * 2]]
                                    obn = [u * 2] * 4
                                    ocol = [qt * 65 for qt in range(4)]
                                else:
                                    obn = [sub * 2 + qt // 2 for qt in range(4)]
                                    obk = [Ob[n] for n in obn]
                                    ocol = [(qt % 2) * 129 for qt in range(4)]
                                started = set()
                                if g == 2:
                                    sbk = sctr[0] % NSB; sctr[0] += 1
                                    pt = pctr[0] % NPT; pctr[0] += 1
                                    for kt in range(16):
                                        P.op('pe', I('matmul', Sb[sbk][:, kt * 32:(kt + 1) * 32], lhsT=Kt[pb0:pb0 + KR, kt * 128:(kt + 1) * 128], rhs=Qt[pb0:pb0 + KR, 0:32], start=True, stop=True),
                                             r=[kr, qr], w=[('S', sbk)])
                                    P.op('act', I('activation', out=PT[pt][:, :], in_=Sb[sbk][:, :], func=AF.Exp, scale=0.125), r=[('S', sbk)], w=[('PT', pt)])
                                    for kt in range(16):
                                        P.op('pe', I('matmul', obk[0][0:32, ocol[0]:ocol[0] + VW], lhsT=PT[pt][:, kt * 32:(kt + 1) * 32], rhs=Vt[:, kt, 0:VW], start=(kt == 0), stop=False, skip_group_check=True),
                                             r=[('PT', pt), vr], w=[('O', obn[0])])
                                    sbk = sctr[0] % NSB; sctr[0] += 1
                                    pt = pctr[0] % NPT; pctr[0] += 1
                                    P.op('pe', I('matmul', Sb[sbk][0:32, 0:32], lhsT=Kt[pb0:pb0 + KR, P_LEN:P_LEN + 32], rhs=Qt[pb0:pb0 + KR, 0:32], start=True, stop=True),
                                         r=[kr, qr], w=[('S', sbk)])
                                    P.op('act', I('activation', out=PT[pt][0:32, 0:32], in_=Sb[sbk][0:32, 0:32], func=AF.Exp, scale=0.125), r=[('S', sbk)], w=[('PT', pt)])
                                    if kind == 'f':
                                        P.op('pool', I('tensor_tensor', out=PT[pt][0:32, 0:32], in0=PT[pt][0:32, 0:32], in1=tri[0:32, 0:32], op=ALU.mult), r=[('PT', pt)], w=[('PT', pt)])
                                    P.op('pe', I('matmul', obk[0][0:32, ocol[0]:ocol[0] + VW], lhsT=PT[pt][0:32, 0:32], rhs=Vt[0:32, 16, 0:VW], start=False, stop=True, skip_group_check=True),
                                         r=[('PT', pt), vr], w=[('O', obn[0])])
                                else:
                                    recs = []

                                    def front(kt):
                                        j = kt - 4 * qb
                                        qoff = max(j, 0) * 128; nq = 512 - qoff
                                        sbk = sctr[0] % NSB; sctr[0] += 1
                                        pt = pctr[0] % NPT; pctr[0] += 1
                                        P.op('pe', I('matmul', Sb[sbk][:, 0:nq], lhsT=Kt[pb0:pb0 + KR, kt * 128:(kt + 1) * 128], rhs=Qt[pb0:pb0 + KR, qb * 512 + qoff:(qb + 1) * 512], start=True, stop=True),
                                             r=[kr, qr], w=[('S', sbk)])
                                        P.op('act', I('activation', out=PT[pt][:, 0:nq], in_=Sb[sbk][:, 0:nq], func=AF.Exp, scale=0.125), r=[('S', sbk)], w=[('PT', pt)])
                                        if j >= 0:
                                            if kind == 'f':
                                                P.op('pool', I('tensor_tensor', out=PT[pt][:, 0:128], in0=PT[pt][:, 0:128], in1=tri[:, :], op=ALU.mult), r=[('PT', pt)], w=[('PT', pt)])
                                            else:
                                                P.op('pool', I('memset', PT[pt][64:128, 0:64], 0.0), r=[('PT', pt)], w=[('PT', pt)])
                                        return (kt, j, pt)

                                    def back(rc_):
                                        kt, j, pt = rc_
                                        for qt in range(max(j, 0), 4):
                                            cc = (qt - max(j, 0)) * 128
                                            first = obn[qt] not in started
                                            started.add(obn[qt])
                                            P.op('pe', I('matmul', obk[qt][:, ocol[qt]:ocol[qt] + VW], lhsT=PT[pt][:, cc:cc + 128], rhs=Vt[:, kt, 0:VW], start=first, stop=(kt == 4 * qb + qt), skip_group_check=True),
                                                 r=[('PT', pt), vr], w=[('O', obn[qt])])
                                    LA = 3
                                    for idx, kt in enumerate(kts):
                                        recs.append(front(kt))
                                        if idx >= LA:
                                            back(recs[idx - LA])
                                    for rc_ in recs[max(0, len(kts) - LA):]:
                                        back(rc_)
                                nqt = 4 if g < 2 else 1
                                rows = 128 if g < 2 else 32
                                for qt in range(nqt):
                                    P.op('dve', I('reciprocal', out=rec[u][0:rows, qt:qt + 1], in_=obk[qt][0:rows, ocol[qt] + VW - 1:ocol[qt] + VW]), r=[('O', obn[qt])], w=[('rec', u)])
                                tile0 = qb * 4 if g < 2 else s
                                if kind == 'f':
                                    for qt in range(nqt):
                                        P.op('dve', I('tensor_scalar', out=OT[0:rows, tile0 + qt, h * 64:(h + 1) * 64], in0=obk[qt][0:rows, ocol[qt]:ocol[qt] + 64], scalar1=rec[u][0:rows, qt:qt + 1], scalar2=None, op0=ALU.mult),
                                             r=[('O', obn[qt]), ('rec', u)], w=[('OT', tile0 + qt, kind, h)])
                                elif sub == 0:
                                    for qt in range(nqt):
                                        P.op('dve', I('tensor_scalar', out=a32[u][0:rows, qt, :], in0=obk[qt][0:rows, ocol[qt]:ocol[qt] + 128], scalar1=rec[u][0:rows, qt:qt + 1], scalar2=None, op0=ALU.mult),
                                             r=[('O', obn[qt]), ('rec', u)], w=[('a32', u)])
                                else:
                                    P.op('dve', I('tensor_scalar', out=nl[u][0:rows, 0:nqt], in0=rec[u][0:rows, 0:nqt], scalar1=neg_lam[0:rows, 0:1], scalar2=None, op0=ALU.mult), r=[('rec', u)], w=[('nl', u)])
                                    for qt in range(nqt):
                                        P.op('dve', I('scalar_tensor_tensor', out=d32[u][0:rows, qt, :], in0=obk[qt][0:rows, ocol[qt]:ocol[qt] + 128], scalar=nl[u][0:rows, qt:qt + 1], in1=a32[u][0:rows, qt, :], op0=ALU.mult, op1=ALU.add),
                                             r=[('O', obn[qt]), ('nl', u), ('a32', u)], w=[('d32', u)])
                                        P.op('dve', I('scalar_tensor_tensor', out=junk[0:rows, :], in0=d32[u][0:rows, qt, :], scalar=1.0 / 128.0, in1=d32[u][0:rows, qt, :], op0=ALU.mult, op1=ALU.mult, accum_out=ss[u][0:rows, qt:qt + 1]), r=[('d32', u)], w=['junk', ('ss', u)])
                                    P.op('dve', I('tensor_scalar', out=ss[u][0:rows, 0:nqt], in0=ss[u][0:rows, 0:nqt], scalar1=1e-6, scalar2=None, op0=ALU.add), r=[('ss', u)], w=[('ss', u)])
                                    P.op('pool', I('tensor_tensor', out=rstd[u][0:rows, 0:nqt], in0=ss[u][0:rows, 0:nqt], in1=mhalf[0:rows, 0:nqt], op=ALU.pow), r=[('ss', u)], w=[('rstd', u)])
                                    for qt in range(nqt):
                                        P.op('dve', I('tensor_scalar', out=OT[0:rows, tile0 + qt, 512 + h * 128:512 + (h + 1) * 128], in0=d32[u][0:rows, qt, :], scalar1=rstd[u][0:rows, qt:qt + 1], scalar2=None, op0=ALU.mult),
                                             r=[('d32', u), ('rstd', u)], w=[('OT', tile0 + qt, kind, h)])

                    hctr = 0
                    for kind, nh in (('d', 4), ('f', 8)):
                        for h in range(nh):
                            b = hctr % 2; hctr += 1
                            if kind == 'f':
                                P.op('sp', I('dma_start', out=Qa[b][0:64, 0:Lq], in_=QF[g][h * 64:(h + 1) * 64, qc0:qc0 + Lq]), w=[('Qa', b)], dma=('Qa', b))
                                P.op('sp', I('dma_start', out=Qa[b][64:67, 0:Lq], in_=cs[h, 0:3, qpos0:qpos0 + Lq]), w=[('Qa', b)], dma=('Qa', b))
                                if g < 2:
                                    P.op('sp', I('dma_start', out=Ka[b][0:64, 0:T], in_=KF[g][h * 64:(h + 1) * 64, :]), w=[('Ka', b)], dma=('Ka', b))
                                    P.op('sp', I('dma_start', out=Ka[b][67:70, 0:T], in_=cs[h, 3:6, 0:T]), w=[('Ka', b)], dma=('Ka', b))
                                    P.op('sp', I('dma_start', out=Vf[b][:, 0:16, 0:64], in_=VF[g][:, h * 64:(h + 1) * 64].rearrange("(t p) c -> p t c", p=128)), w=[('Vf', b)], dma=('Vf', b))
                                else:
                                    P.op('pool', I('dma_start', out=Ka[b][0:64, 0:P_LEN], in_=fkT_s[s, h * 64:(h + 1) * 64, :]), w=[('Ka', b)], dma=('Ka', b))
                                    P.op('sp', I('dma_start', out=Ka[b][0:64, P_LEN:P_LEN + 32], in_=KF[g][h * 64:(h + 1) * 64, qc0:qc0 + 32]), w=[('Ka', b)], dma=('Ka', b))
                                    P.op('sp', I('dma_start', out=Ka[b][67:70, 0:P_LEN + 32], in_=cs[h, 3:6, 0:P_LEN + 32]), w=[('Ka', b)], dma=('Ka', b))
                                    P.op('pool', I('dma_start', out=Vf[b][:, 0:16, 0:64], in_=fv_s[s, :, h * 64:(h + 1) * 64].rearrange("(t p) c -> p t c", p=128)), w=[('Vf', b)], dma=('Vf', b))
                                    P.op('sp', I('dma_start', out=Vf[b][0:32, 16, 0:64], in_=VF[g][qc0:qc0 + 32, h * 64:(h + 1) * 64]), w=[('Vf', b)], dma=('Vf', b))
                            else:
                                P.op('sp', I('dma_start', out=Qd[b][0][0:64, 0:Lq], in_=QD[g][h * 128:h * 128 + 64, qc0:qc0 + Lq]), w=[('Qd', b)], dma=('Qd', b))
                                P.op('sp', I('dma_start', out=Qd[b][1][64:128, 0:Lq], in_=QD[g][h * 128 + 64:(h + 1) * 128, qc0:qc0 + Lq]), w=[('Qd', b)], dma=('Qd', b))
                                if g < 2:
                                    P.op('sp', I('dma_start', out=Kd[b][:, 0:T], in_=KD[g][h * 128:(h + 1) * 128, :]), w=[('Kd', b)], dma=('Kd', b))
                                    P.op('sp', I('dma_start', out=Vd[b][:, 0:16, 0:128], in_=VD[g][:, h * 128:(h + 1) * 128].rearrange("(t p) c -> p t c", p=128)), w=[('Vd', b)], dma=('Vd', b))
                                else:
                                    P.op('pool', I('dma_start', out=Kd[b][:, 0:P_LEN], in_=dkT_s[s, h * 128:(h + 1) * 128, :]), w=[('Kd', b)], dma=('Kd', b))
                                    P.op('sp', I('dma_start', out=Kd[b][:, P_LEN:P_LEN + 32], in_=KD[g][h * 128:(h + 1) * 128, qc0:qc0 + 32]), w=[('Kd', b)], dma=('Kd', b))
                                    P.op('pool', I('dma_start', out=Vd[b][:, 0:16, 0:128], in_=dv_s[s, :, h * 128:(h + 1) * 128].rearrange("(t p) c -> p t c", p=128)), w=[('Vd', b)], dma=('Vd', b))
                                    P.op('sp', I('dma_start', out=Vd[b][0:32, 16, 0:128], in_=VD[g][qc0:qc0 + 32, h * 128:(h + 1) * 128]), w=[('Vd', b)], dma=('Vd', b))
                            attend(kind, h, b)
                    if g < 2:
                        allot = [('OT', tt, 'f', h) for tt in range(16) for h in range(8)] + [('OT', tt, 'd', h) for tt in range(16) for h in range(4)]
                        P.op('sp', I('dma_start', out=OTOK[g][:, :].rearrange("(t p) c -> p t c", p=128), in_=OT[:, :, :]), r=allot, dma='oto')
                    else:
                        allot = [('OT', s, 'f', h) for h in range(8)] + [('OT', s, 'd', h) for h in range(4)]
                        P.op('sp', I('dma_start', out=OTOK[g][s * 32:(s + 1) * 32, :], in_=OT[0:32, s, :]), r=allot, dma='oto')
                P.emit()
            if STOP == f's2g{g}':
                return nc

            with ExitStack() as st:
                def sb(name, shape, dt=F32):
                    _uid[0] += 1
                    return st.enter_context(nc.sbuf_tensor(f"{name}_u{_uid[0]}", list(shape), dt))
                P = Prog(nc, f"s3g{g}")
                nt = N // 128
                nseg = 1 if g < 2 else 4
                L = N // nseg
                wo = sb("wo", [128, 8, D], BF); wd = sb("wd", [128, NF, D], BF)
                wu = [sb(f"wu{i}", [128, 8, 256], BF) for i in range(3)]
                xtokL = [sb(f"xtok{i}", [128, nt, D]) for i in range(2)]; otkL = [sb(f"otk{i}", [128, nt, D], BF) for i in range(2)]
                oT = sb("oT", [128, 8, N], BF); u2T = sb("u2T", [128, 8, N], BF); hT = sb("hT", [128, NF, N], BF)
                x1b = [sb(f"x1b{i}", [128, D], BF) for i in range(2)]; tmpL = [sb(f"tmp{i}", [128, D]) for i in range(2)]
                apad = [sb(f"apad{i}", [128, nseg, L + 2]) for i in range(2)]
                acc = [sb(f"acc{i}", [128, nseg, L]) for i in range(2)]; sg = [sb(f"sg{i}", [128, nseg, L], BF) for i in range(2)]
                halo = sb("halo", [128, NF, nseg, 2])
                gbc = sb("gbc", [128, 2, D])
                bstL = [sb(f"bst{i}", [128, 2, 6]) for i in range(2)]; mvL = [sb(f"mv{i}", [128, 2]) for i in range(2)]; sdL = [sb(f"sd{i}", [128, 1]) for i in range(2)]; nmrL = [sb(f"nmr{i}", [128, 1]) for i in range(2)]
                ps = [st.enter_context(nc.psum_tensor(f"q{i}_" + _pn(), [128, 512], F32)) for i in range(5)]
                tp = [st.enter_context(nc.psum_tensor(f"tp{i}_" + _pn(), [128, 1024], BF)) for i in range(3)]
                pctr = [0]

                def nbank():
                    pctr[0] += 1
                    return pctr[0] % 5
                xsrc = x_p[g] if g < 2 else x_s
                ydst = y_p[g] if g < 2 else y_s
                wctr = [0]

                lnctr = [0]

                def layernorm(lni, xtok, xk, tt=0):
                    pq = lnctr[0] % 2; lnctr[0] += 1
                    bst = bstL[pq]; mv = mvL[pq]; sd = sdL[pq]; nmr = nmrL[pq]
                    for hh in range(2):
                        P.op('dve', I('bn_stats', out=bst[:, hh, :], in_=xtok[:, tt, hh * 512:(hh + 1) * 512]), r=[('xtok', xk, tt)], w=[('bst', pq)])
                    P.op('dve', I('bn_aggr', out=mv[:], in_=bst[:].rearrange("p a b -> p (a b)")), r=[('bst', pq)], w=[('mv', pq)])
                    P.op('act', I('activation', out=sd[:], in_=mv[:, 1:2], func=AF.Sqrt, bias=1e-5, scale=1.0), r=[('mv', pq)], w=[('sd', pq)])
                    P.op('dve', I('reciprocal', out=sd[:], in_=sd[:]), r=[('sd', pq)], w=[('sd', pq)])
                    P.op('dve', I('tensor_scalar', out=nmr[:], in0=mv[:, 0:1], scalar1=sd[:, 0:1], scalar2=-1.0, op0=ALU.mult, op1=ALU.mult), r=[('mv', pq), ('sd', pq)], w=[('nmr', pq)])
                    P.op('act', I('activation', out=xtok[:, tt, :], in_=xtok[:, tt, :], func=AF.Identity, scale=sd[:, 0:1], bias=nmr[:, 0:1]), r=[('xtok', xk, tt), ('sd', pq), ('nmr', pq)], w=[('xtok', xk, tt)])
                    P.op('pool', I('tensor_tensor', out=xtok[:, tt, :], in0=xtok[:, tt, :], in1=lnbc[:, lni, :], op=ALU.mult), r=[('xtok', xk, tt)], w=[('xtok', xk, tt)])
                    P.op('pool', I('tensor_tensor', out=xtok[:, tt, :], in0=xtok[:, tt, :], in1=lnbc[:, lni + 1, :], op=ALU.add), r=[('xtok', xk, tt)], w=[('xtok', xk, tt)])

                def load_blk(blk):
                    xk = blk % 2
                    t0_ = blk * N
                    P.op('sp', I('dma_start', out=xtokL[xk][:], in_=xsrc[t0_:t0_ + N, :].rearrange("(t p) c -> p t c", p=128)), w=[('xtok', xk, tt) for tt in range(nt)], dma=('xtok', xk))
                    P.op('sp', I('dma_start', out=otkL[xk][:], in_=OTOK[g][t0_:t0_ + N, :].rearrange("(t p) c -> p t c", p=128)), w=[('otk', xk)], dma=('otk', xk))
                load_blk(0)
                P.op('sp', I('dma_start', out=wo[:], in_=wo4[:, :, :]), w=['wo'], dma='wo')
                P.op('act', I('activation', out=wo[:, 4:8, :], in_=wo[:, 4:8, :], func=AF.Identity, scale=wrs[:, 0:1]), r=['wo'], w=['wo'])
                if g < 2:
                    P.op('pool', I('memset', halo[:], 0.0), w=['halo'])
                else:
                    P.op('sp', I('dma_start', out=halo[:], in_=convT_s[:, :, :, :]), w=['halo'], dma='halo')
                for gi in range(2):
                    if g < 2:
                        P.op('sp', I('dma_start', out=gbc[:, gi, :], in_=MG[g, gi * 1024:(gi + 1) * 1024].partition_broadcast(128)), w=['gbc'], dma='gbc')
                    else:
                        for s in range(4):
                            P.op('sp', I('dma_start', out=gbc[s * 32:(s + 1) * 32, gi, :], in_=MG[2 + s, gi * 1024:(gi + 1) * 1024].partition_broadcast(32)), w=['gbc'], dma='gbc')
                P.op('sp', I('dma_start', out=wd[:], in_=wd4[:, :, :]), w=['wd'], dma='wd')
                for blk in range(nblk):
                    t0 = blk * N
                    xk = blk % 2
                    xtok = xtokL[xk]; otk = otkL[xk]
                    if blk + 1 < nblk:
                        load_blk(blk + 1)

                    bsA = {}

                    def phaseA(tt):
                        tb = tt % 2
                        tmp = tmpL[tt % 2]; tk = ('tmp', tt % 2)
                        for k in range(8):
                            P.op('pe', I('transpose', tp[tb][:, k * 128:(k + 1) * 128], otk[:, tt, k * 128:(k + 1) * 128], ident[:]), r=[('otk', xk), 'ident'], w=[('tp', tb)])
                        P.op('act', I('activation', out=oT[:, :, tt * 128:(tt + 1) * 128], in_=tp[tb][:, :].rearrange("p (k t) -> p k t", k=8), func=AF.Identity), r=[('tp', tb)], w=[('oT', tt)])
                        bs = []
                        for half in range(2):
                            b = nbank(); bs.append(b)
                            for k in range(8):
                                P.op('pe', I('matmul', ps[b][:, :], lhsT=oT[:, k, tt * 128:(tt + 1) * 128], rhs=wo[:, k, half * 512:(half + 1) * 512], start=(k == 0), stop=(k == 7)),
                                     r=[('oT', tt), 'wo'], w=[('ps', b)])
                        bsA[tt] = bs

                    def phaseL(tt):
                        tmp = tmpL[tt % 2]; tk = ('tmp', tt % 2)
                        bs = bsA[tt]
                        for half in range(2):
                            P.op('dve', I('tensor_tensor', out=tmp[:, half * 512:(half + 1) * 512], in0=ps[bs[half]][:, :], in1=gbc[:, 0, half * 512:(half + 1) * 512], op=ALU.mult), r=[('ps', bs[half]), 'gbc'], w=[tk])
                        P.op('dve', I('scalar_tensor_tensor', out=xtok[:, tt, :], in0=xtok[:, tt, :], scalar=ALPHA, in1=tmp[:], op0=ALU.mult, op1=ALU.add), r=[('xtok', xk, tt), tk], w=[('xtok', xk, tt)])
                        layernorm(0, xtok, xk, tt=tt)
                        P.op('act', I('activation', out=x1b[tt % 2][:], in_=xtok[:, tt, :], func=AF.Identity), r=[('xtok', xk, tt)], w=[('x1b', tt % 2)])

                    def phaseB(tt):
                        tb2 = 2
                        for k in range(8):
                            P.op('pe', I('transpose', tp[tb2][:, k * 128:(k + 1) * 128], x1b[tt % 2][:, k * 128:(k + 1) * 128], ident[:]), r=[('x1b', tt % 2), 'ident'], w=[('tp', tb2)])
                        for k in range(8):
                            if g < 2:
                                P.op('dve', I('tensor_scalar', out=u2T[:, k, tt * 128:(tt + 1) * 128], in0=tp[tb2][:, k * 128:(k + 1) * 128], scalar1=sc2p[:, k, g:g + 1], scalar2=modF[:, 24 + k, g:g + 1], op0=ALU.mult, op1=ALU.add),
                                     r=[('tp', tb2)], w=['u2T'])
                            else:
                                for s in range(4):
                                    P.op('dve', I('tensor_scalar', out=u2T[:, k, s * 32:(s + 1) * 32], in0=tp[tb2][:, k * 128 + s * 32:k * 128 + (s + 1) * 32], scalar1=sc2p[:, k, 2 + s:3 + s], scalar2=modF[:, 24 + k, 2 + s:3 + s], op0=ALU.mult, op1=ALU.add),
                                         r=[('tp', tb2)], w=['u2T'])
                    seq_ = []
                    for tt in range(nt):
                        seq_.append(('A', tt))
                        if tt >= 1:
                            seq_.append(('L', tt - 1))
                        if tt >= 2:
                            seq_.append(('B', tt - 2))
                    seq_ += [('L', nt - 1)]
                    if nt >= 2:
                        seq_ += [('B', nt - 2)]
                    seq_ += [('B', nt - 1)]
                    for kind_, tt_ in seq_:
                        {'A': phaseA, 'L': phaseL, 'B': phaseB}[kind_](tt_)
                    if L3 < 4:
                        continue
                    for f in range(NF):
                        sl = wctr[0] % 3; wctr[0] += 1
                        P.op('sp', I('dma_start', out=wu[sl][:, :, :], in_=wu4[f]), w=[('wu', sl)], dma=('wu', sl))
                        ba = nbank(); bb = nbank()
                        for k in range(8):
                            P.op('pe', I('matmul', ps[ba][:, 0:N], lhsT=wu[sl][:, k, 0:128], rhs=u2T[:, k, :], start=(k == 0), stop=(k == 7)), r=[('wu', sl), 'u2T'], w=[('ps', ba)])
                        for k in range(8):
                            P.op('pe', I('matmul', ps[bb][:, 0:N], lhsT=wu[sl][:, k, 128:256], rhs=u2T[:, k, :], start=(k == 0), stop=(k == 7)), r=[('wu', sl), 'u2T'], w=[('ps', bb)])
                        ab = f % 2
                        P.op('act', I('activation', out=apad[ab][:, :, 2:L + 2], in_=ps[ba][:, 0:N].rearrange("p (s l) -> p s l", s=nseg), func=AF.Identity), r=[('ps', ba)], w=[('apad', ab)])
                        P.op('pool', I('tensor_copy', out=apad[ab][:, :, 0:2], in_=halo[:, f, :, :]), r=[('halo', f)], w=[('apad', ab)])
                        P.op('dve', I('tensor_scalar', out=acc[ab][:], in0=apad[ab][:, :, 0:L], scalar1=cw[:, f, 0:1], scalar2=cw[:, f, 3:4], op0=ALU.mult, op1=ALU.add), r=[('apad', ab)], w=[('acc', ab)])
                        P.op('dve', I('scalar_tensor_tensor', out=acc[ab][:], in0=apad[ab][:, :, 1:L + 1], scalar=cw[:, f, 1:2], in1=acc[ab][:], op0=ALU.mult, op1=ALU.add), r=[('apad', ab), ('acc', ab)], w=[('acc', ab)])
                        P.op('dve', I('scalar_tensor_tensor', out=acc[ab][:], in0=apad[ab][:, :, 2:L + 2], scalar=cw[:, f, 2:3], in1=acc[ab][:], op0=ALU.mult, op1=ALU.add), r=[('apad', ab), ('acc', ab)], w=[('acc', ab)])
                        P.op('pool', I('tensor_copy', out=halo[:, f, :, :], in_=apad[ab][:, :, L:L + 2]), r=[('apad', ab)], w=[('halo', f)])
                        P.op('act', I('activation', out=sg[ab][:], in_=acc[ab][:], func=AF.Silu), r=[('acc', ab)], w=[('sg', ab)])
                        P.op('dve', I('tensor_tensor', out=hT[:, f, :].rearrange("p (s l) -> p s l", s=nseg), in0=sg[ab][:], in1=ps[bb][:, 0:N].rearrange("p (s l) -> p s l", s=nseg), op=ALU.mult), r=[('sg', ab), ('ps', bb)], w=['hT'])
                    if L3 < 5:
                        continue
                    for tt in range(nt):
                        tmp = tmpL[tt % 2]; tk = ('tmp', tt % 2)
                        bs = []
                        for half in range(2):
                            b = nbank(); bs.append(b)
                            for f in range(NF):
                                P.op('pe', I('matmul', ps[b][:, :], lhsT=hT[:, f, tt * 128:(tt + 1) * 128], rhs=wd[:, f, half * 512:(half + 1) * 512], start=(f == 0), stop=(f == NF - 1)),
                                     r=['hT', 'wd'], w=[('ps', b)])
                        for half in range(2):
                            P.op('dve', I('tensor_tensor', out=tmp[:, half * 512:(half + 1) * 512], in0=ps[bs[half]][:, :], in1=gbc[:, 1, half * 512:(half + 1) * 512], op=ALU.mult), r=[('ps', bs[half]), 'gbc'], w=[tk])
                        P.op('dve', I('scalar_tensor_tensor', out=xtok[:, tt, :], in0=xtok[:, tt, :], scalar=ALPHA, in1=tmp[:], op0=ALU.mult, op1=ALU.add), r=[('xtok', xk, tt), tk], w=[('xtok', xk, tt)])
                        layernorm(2, xtok, xk, tt=tt)
                        P.op('sp', I('dma_start', out=ydst[t0 + tt * 128:t0 + (tt + 1) * 128, :], in_=xtok[:, tt, :]), r=[('xtok', xk, tt)], dma=('yo', tt))
                P.op('sp', I('dma_start', out=convT_o[g][:, :, :, :], in_=halo[:]), r=[('halo', f) for f in range(NF)], dma='cvo')
                P.emit()
            if STOP == f's3g{g}':
                return nc
    return nc


def _rope_tables(pos):
    d = 64
    inv = (10000.0 ** (-np.arange(0, d, 2, dtype=np.float32) / d)).astype(np.float32)
    ang = pos.astype(np.float32)[None, :] * inv[:, None]
    cos = np.cos(ang).astype(np.float32); sin = np.sin(ang).astype(np.float32)
    C = np.concatenate([cos, cos, cos, cos], axis=0)
    S = np.concatenate([-sin, sin, -sin, sin], axis=0)
    return np.ascontiguousarray(C), np.ascontiguousarray(S)


_NC = None
_PREP_ONLY = False


def kernel(x_prompt, x_sample, c_prompt, c_sample, cache_fox_k, cache_fox_v, cache_fox_logf,
           cache_diff_k, cache_diff_v, state_ffn_conv, w_ada, b_ada, w_in, b_f, lambda_vecs,
           subln_g, w_o, ln1_g, ln1_b, w_up, conv_w, conv_b, w_down, ln2_g, ln2_b):
    global _NC
    f32 = np.float32
    A = lambda a: np.ascontiguousarray(np.asarray(a, dtype=f32))
    x_prompt = A(x_prompt); x_sample = A(x_sample)
    w_in0 = A(w_in)[0]
    fq = w_in0[:, 0:512]; fk = w_in0[:, 512:1024]; fv = w_in0[:, 1024:1536]; ff = w_in0[:, 1536:1544]
    dq = w_in0[:, 1544:2056]; dk = w_in0[:, 2056:2568]; dv = w_in0[:, 2568:3080]

    def swp(m):
        return m.reshape(D, 8, 2, 32)[:, :, ::-1, :].reshape(D, 512)
    w_in_ext = np.ascontiguousarray(np.concatenate([fq, fk, dq, swp(dq), dk, swp(dk), fv, dv, ff], axis=1))
    b_ada0 = A(b_ada)[0]
    common = {
        "w_ada": A(w_ada)[0], "b_adaT": np.ascontiguousarray(b_ada0.reshape(48, 128).T),
        "b_ada_g": np.ascontiguousarray(np.concatenate([b_ada0[2048:3072], b_ada0[5120:6144]])[None, :]),
        "w_in": w_in_ext, "b_f": np.ascontiguousarray(A(b_f)[0].reshape(8, 1)),
        "lam_v": A(lambda_vecs)[0].reshape(1, 256), "subln": A(subln_g)[0].reshape(128, 1),
        "w_o": A(w_o)[0], "lnp": np.ascontiguousarray(np.stack([A(ln1_g)[0], A(ln1_b)[0], A(ln2_g)[0], A(ln2_b)[0]])),
        "w_up": A(w_up)[0],
        "convw": np.ascontiguousarray(np.concatenate([A(conv_w)[0], A(conv_b)], axis=0).reshape(4, NF, 128).transpose(2, 1, 0)),
        "w_down": A(w_down)[0],
        "ident": np.eye(128, dtype=f32).astype(ml_dtypes.bfloat16),
        "tri": np.triu(np.ones((128, 128), dtype=f32)).astype(ml_dtypes.bfloat16),
    }
    Cp, Sp = _rope_tables(np.arange(T)); Cs, Ss = _rope_tables(P_LEN + np.arange(TS))
    common["ropeC_p"] = Cp; common["ropeS_p"] = Sp
    common["ropeC_s"] = np.ascontiguousarray(np.tile(Cs, (1, 4))); common["ropeS_s"] = np.ascontiguousarray(np.tile(Ss, (1, 4)))
    sel = np.zeros((6, 3, 128), dtype=f32)
    sel[0, 0, :] = 1; sel[1, 1, :] = 1
    for s in range(4):
        sel[2 + s, 2, s * 32:(s + 1) * 32] = 1
    common["sel"] = sel
    common["ones3"] = np.ones((3, T), dtype=f32).astype(ml_dtypes.bfloat16)
    cfk = A(cache_fox_k)[0]; cfv = A(cache_fox_v)[0]; clf = A(cache_fox_logf)[0]
    cdk = A(cache_diff_k)[0]; cdv = A(cache_diff_v)[0]; cst = A(state_ffn_conv)[0]
    c_prompt = A(c_prompt); c_sample = A(c_sample)
    in_maps = []
    for c in range(8):
        ps_ = slice(2 * c, 2 * c + 2); ss_ = slice(4 * c, 4 * c + 4)
        xp = x_prompt[ps_]
        xs = x_sample[ss_].reshape(128, D)
        call = np.concatenate([c_prompt[ps_], c_sample[ss_]], axis=0)
        m = dict(common)
        m["xT_p"] = np.ascontiguousarray(xp.reshape(2, T, 8, 128).transpose(0, 3, 2, 1))
        m["x_p"] = np.ascontiguousarray(xp)
        m["xT_s"] = np.ascontiguousarray(xs.reshape(128, 8, 128).transpose(2, 1, 0))
        m["x_s"] = np.ascontiguousarray(xs)
        m["cT"] = np.ascontiguousarray(call.reshape(6, 8, 128).transpose(2, 1, 0))
        m["fkT_s"] = np.ascontiguousarray(cfk[ss_].reshape(4, P_LEN, 512).transpose(0, 2, 1))
        m["fv_s"] = np.ascontiguousarray(cfv[ss_].reshape(4, P_LEN, 512))
        m["lfT_s"] = np.ascontiguousarray(clf[ss_].transpose(0, 2, 1))
        m["dkT_s"] = np.ascontiguousarray(cdk[ss_].reshape(4, P_LEN, 512).transpose(0, 2, 1))
        m["dv_s"] = np.ascontiguousarray(cdv[ss_].reshape(4, P_LEN, 512))
        m["convT_s"] = np.ascontiguousarray(cst[ss_].reshape(4, 2, NF, 128).transpose(3, 2, 0, 1))
        in_maps.append(m)
    if _PREP_ONLY:
        return in_maps
    if _NC is None:
        _NC = build()
    res = run_bass_kernel_spmd(_NC, in_maps, core_ids=list(range(8)))
    R = res.results
    yp = np.concatenate([r["y_p"] for r in R], axis=0)
    ys = np.concatenate([r["y_s"].reshape(4, TS, D) for r in R], axis=0)

    def catp(n0, n1):
        return [a for r in R for a in (r[n0], r[n1])]
    p_fk = np.stack([a.T.reshape(T, 8, 64) for a in catp("fkT_o0", "fkT_o1")])[None]
    p_fv = np.stack([a.reshape(T, 8, 64) for a in catp("fv_o0", "fv_o1")])[None]
    p_lf = np.stack([a.T for a in catp("lfT_o0", "lfT_o1")])[None]
    p_dk = np.stack([a.T.reshape(T, 8, 64) for a in catp("dkT_o0", "dkT_o1")])[None]
    p_dv = np.stack([a.reshape(T, 4, 128) for a in catp("dv_o0", "dv_o1")])[None]
    p_cv = np.stack([a.reshape(128, NF, 2).transpose(2, 1, 0).reshape(2, DFF) for a in catp("convT_o0", "convT_o1")])[None]
    s_fk = np.concatenate([r["sfkT_o"].T.reshape(4, TS, 8, 64) for r in R], axis=0)[None]
    s_fv = np.concatenate([r["sfv_o"].reshape(4, TS, 8, 64) for r in R], axis=0)[None]
    s_lf = np.concatenate([r["slfT_o"].T.reshape(4, TS, 8) for r in R], axis=0)[None]
    s_dk = np.concatenate([r["sdkT_o"].T.reshape(4, TS, 8, 64) for r in R], axis=0)[None]
    s_dv = np.concatenate([r["sdv_o"].reshape(4, TS, 4, 128) for r in R], axis=0)[None]
    s_cv = np.concatenate([r["sconvT_o"].transpose(2, 3, 1, 0).reshape(4, 2, DFF) for r in R], axis=0)[None]
    outs = (yp, ys, p_fk, p_fv, p_lf, p_dk, p_dv, p_cv, s_fk, s_fv, s_lf, s_dk, s_dv, s_cv)
    return tuple(np.ascontiguousarray(o, dtype=f32) for o in outs)
```

```python
import numpy as np
import os
import ml_dtypes
from contextlib import ExitStack
import concourse.bass as bass
import concourse.mybir as mybir
from concourse.bass_utils import run_bass_kernel_spmd

F32 = mybir.dt.float32
BF = mybir.dt.bfloat16
AF = mybir.ActivationFunctionType
ALU = mybir.AluOpType

T = 2048
D = 1024
DFF = 2816
NF = 22
P_LEN = 2048
TS = 32
ALPHA = 2.0 ** 0.25
LAM_INIT = 0.8 - 0.6
WK = 2112
ENG = ('pe', 'act', 'dve', 'pool', 'sp')


def I(name, *a, **k):
    return (name, a, k)


class Prog:
    G = None

    def __init__(s, nc, name):
        s.nc = nc; s.name = name; s.ops = []; s.lastw = {}; s.rd = {}; s.dma_cnt = {}; s.spd = []; s.pld = []

    def op(s, eng, fn, r=(), w=(), dma=None):
        i = len(s.ops); deps = set(); raw = set()
        if eng in ('sp', 'pool') and dma is not None and fn is not None:
            nd = 1
            for ap in (fn[2].get('out'), fn[2].get('in_')):
                try:
                    shp = list(ap.shape)
                    n_ = 1
                    for d_ in shp[:-1]:
                        n_ *= int(d_)
                    nd = max(nd, n_)
                except Exception:
                    nd = max(nd, 1024)
            tot = nd
            lst = s.spd if eng == 'sp' else s.pld
            for (j_, ndj) in reversed(lst):
                tot += ndj
                if tot > (2500 if eng == 'sp' else 6000):
                    deps.add(j_)
                    break
            lst.append((i, nd))
        for x in r:
            if x in s.lastw:
                deps.add(s.lastw[x]); raw.add(s.lastw[x])
        for x in w:
            if x in s.lastw:
                deps.add(s.lastw[x])
            for j in s.rd.get(x, ()):
                deps.add(j)
        for x in r:
            s.rd.setdefault(x, []).append(i)
        for x in w:
            s.lastw[x] = i; s.rd[x] = []
        deps.discard(i)
        o = dict(i=i, eng=eng, fn=fn, deps=deps, raw=raw, dma=dma, sig=False)
        if dma is not None:
            s.dma_cnt[dma] = s.dma_cnt.get(dma, 0) + 1
            o['dval'] = 16 * s.dma_cnt[dma]
        s.ops.append(o)
        return i

    def emit(s):
        nc = s.nc
        last = {}
        for o in s.ops:
            if o['dma'] is not None:
                last[o['dma']] = o['i']
        fin = dict(i=len(s.ops), eng='sp', fn=None, deps=set(last.values()), raw=set(), dma=None, sig=False)
        s.ops.append(fin)
        for o in s.ops:
            o['waits'] = []
            for j in sorted(o['deps']):
                d = s.ops[j]
                if d['dma'] is not None:
                    o['waits'].append(('dma', d['dma'], d['dval']))
                else:
                    if d['eng'] == o['eng'] and (d['eng'] == 'pe' or j not in o['raw']):
                        continue
                    d['sig'] = True
                    o['waits'].append(('eng', j))
        cnt = {e: 0 for e in ENG}
        for o in s.ops:
            if o['dma'] is None and o['sig']:
                cnt[o['eng']] += 1; o['cval'] = cnt[o['eng']]
        G = s.G
        keys = list(s.dma_cnt)
        assert len(keys) <= len(G['dsem']), (s.name, len(keys))
        kidx = {k: n for n, k in enumerate(keys)}
        esem = G['esem']; ebase = dict(G['ebase']); dbase = list(G['dbase'])
        with ExitStack() as es:
            block = es.enter_context(nc.Block())

            def body(engname):
                def f(e):
                    seen = {}
                    for o in s.ops:
                        if o['eng'] != engname:
                            continue
                        for wt in o['waits']:
                            if wt[0] == 'dma':
                                n = kidx[wt[1]]; sem = G['dsem'][n]; val = dbase[n] + wt[2]; key = ('d', n)
                            else:
                                d = s.ops[wt[1]]; sem = esem[d['eng']]; val = ebase[d['eng']] + d['cval']; key = ('e', d['eng'])
                            if seen.get(key, 0) >= val:
                                continue
                            seen[key] = val
                            e.wait_ge(sem, val)
                        if o['fn'] is None:
                            continue
                        nm, a_, k_ = o['fn']
                        inst = getattr(e, nm)(*a_, **k_)
                        if o['dma'] is not None:
                            inst.then_inc(G['dsem'][kidx[o['dma']]], 16)
                        elif o['sig']:
                            inst.then_inc(esem[engname], 1)
                return f
            block.tensor(body('pe')); block.scalar(body('act')); block.vector(body('dve'))
            block.gpsimd(body('pool')); block.sync(body('sp'))
        for e_ in ENG:
            G['ebase'][e_] += cnt[e_]
        for k, n in kidx.items():
            G['dbase'][n] += 16 * s.dma_cnt[k]


_uid = [0]


def _pn():
    _uid[0] += 1
    return f"u{_uid[0]}"


def build():
    nc = bass.Bass("TRN2", target_bir_lowering=False)
    STOP = os.environ.get('KSTOP', 'zz')
    DBG = bool(os.environ.get('KDBG'))
    LVL = int(os.environ.get('KLVL', '99'))
    L3 = int(os.environ.get('KL3', '99'))

    def din(name, shape, dt=F32):
        return nc.dram_tensor(name, list(shape), dt, kind="ExternalInput").ap()

    def dout(name, shape, dt=F32):
        return nc.dram_tensor(name, list(shape), dt, kind="ExternalOutput").ap()

    def dscr(name, shape, dt=BF):
        return nc.dram_tensor(name, list(shape), dt, kind=("ExternalOutput" if DBG else "Internal")).ap()

    xT_p = din("xT_p", [2, 128, 8, T]); x_p = din("x_p", [2, T, D])
    xT_s = din("xT_s", [128, 8, 128]); x_s = din("x_s", [128, D])
    cT = din("cT", [128, 8, 6])
    fkT_s = din("fkT_s", [4, 512, P_LEN]); fv_s = din("fv_s", [4, P_LEN, 512]); lfT_s = din("lfT_s", [4, 8, P_LEN])
    dkT_s = din("dkT_s", [4, 512, P_LEN]); dv_s = din("dv_s", [4, P_LEN, 512]); convT_s = din("convT_s", [128, NF, 4, 2])
    w_ada = din("w_ada", [D, 6 * D]); b_adaT = din("b_adaT", [128, 48]); b_ada_g = din("b_ada_g", [1, 2048])
    w_in = din("w_in", [D, 4104]); nb_f = din("b_f", [8, 1]); lam_v = din("lam_v", [1, 256]); subln = din("subln", [128, 1])
    w_o = din("w_o", [D, D]); lnp = din("lnp", [4, D]); w_up = din("w_up", [D, 2 * DFF]); convw = din("convw", [128, NF, 4])
    w_down = din("w_down", [DFF, D])
    ident_d = din("ident", [128, 128], BF); tri_d = din("tri", [128, 128], BF)
    ropeC_p = din("ropeC_p", [128, T]); ropeS_p = din("ropeS_p", [128, T])
    ropeC_s = din("ropeC_s", [128, 128]); ropeS_s = din("ropeS_s", [128, 128])
    sel_d = din("sel", [6, 3, 128])
    ones3_d = din("ones3", [3, T], BF)
    y_p = dout("y_p", [2, T, D]); y_s = dout("y_s", [128, D])
    fkT_o = [dout("fkT_o0", [512, T]), dout("fkT_o1", [512, T]), dout("sfkT_o", [512, 128])]
    fv_o = [dout("fv_o0", [T, 512]), dout("fv_o1", [T, 512]), dout("sfv_o", [128, 512])]
    lfT_o = [dout("lfT_o0", [8, T]), dout("lfT_o1", [8, T]), dout("slfT_o", [8, 128])]
    dkT_o = [dout("dkT_o0", [512, T]), dout("dkT_o1", [512, T]), dout("sdkT_o", [512, 128])]
    dv_o = [dout("dv_o0", [T, 512]), dout("dv_o1", [T, 512]), dout("sdv_o", [128, 512])]
    convT_o = [dout("convT_o0", [128, NF, 1, 2]), dout("convT_o1", [128, NF, 1, 2]), dout("sconvT_o", [128, NF, 4, 2])]
    wi4 = dscr("wi4", [8, 128, 8, 512]); wiF = dscr("wiF", [128, 8, 8]); wo4 = dscr("wo4", [128, 8, D]); wu4 = dscr("wu4", [NF, 128, 8, 256]); wd4 = dscr("wd4", [128, NF, D])
    TG = [T, T, 128]
    QF = [dscr(f"QF{g}", [512, TG[g]]) for g in range(3)]
    KF = [dscr(f"KF{g}", [512, TG[g]]) for g in range(3)]
    QD = [dscr(f"QD{g}", [512, TG[g]]) for g in range(3)]
    KD = [dscr(f"KD{g}", [512, TG[g]]) for g in range(3)]
    VF = [dscr(f"VF{g}", [TG[g], 512]) for g in range(3)]
    VD = [dscr(f"VD{g}", [TG[g], 512]) for g in range(3)]
    CS = [dscr("CS0", [8, 6, T]), dscr("CS1", [8, 6, T])] + [dscr(f"CSs{s}", [8, 6, WK]) for s in range(4)]
    OTOK = [dscr(f"OTOK{g}", [TG[g], D]) for g in range(3)]
    MG = dscr("MG", [6, 2048], F32)
    KaS = [dscr(f"KaS{i}", [512, P_LEN]) for i in range(4)]; KdS = [dscr(f"KdS{i}", [512, P_LEN]) for i in range(4)]
    VfS = [dscr(f"VfS{i}", [P_LEN, 512]) for i in range(4)]; VdS = [dscr(f"VdS{i}", [P_LEN, 512]) for i in range(4)]
    precast = [True]

    with ExitStack() as gs:
        def sb(name, shape, dt=F32):
            return gs.enter_context(nc.sbuf_tensor(name, list(shape), dt))
        Prog.G = dict(esem={e_: gs.enter_context(nc.semaphore(f"ge_{e_}")) for e_ in ENG},
                      dsem=[gs.enter_context(nc.semaphore(f"gd_{n_}")) for n_ in range(32)],
                      ebase={e_: 0 for e_ in ENG}, dbase=[0] * 32)
        ident = sb("ident_sb", [128, 128], BF); tri = sb("tri_sb", [128, 128], BF)
        modF = sb("modF", [128, 48, 6]); sc1p = sb("sc1p", [128, 8, 6]); sc2p = sb("sc2p", [128, 8, 6])
        lnbc = sb("lnbc", [128, 4, D]); neg_lam = sb("neg_lam", [128, 1]); wrs = sb("wrs", [128, 1])
        nbf = sb("nbf", [8, 1]); cw = sb("cw", [128, NF, 4])

        with ExitStack() as st:
            def sb(name, shape, dt=F32):
                _uid[0] += 1
                return st.enter_context(nc.sbuf_tensor(f"{name}_u{_uid[0]}", list(shape), dt))
            P = Prog(nc, "s0")
            cts = sb("cts", [128, 8, 6]); scb = sb("scb", [128, 8, 6], BF)
            modG = sb("modG", [6, 2048])
            wa = [sb(f"wa{i}", [128, 8, 1024], BF) for i in range(2)]
            badT = sb("badT", [128, 48]); bag = sb("bag", [6, 2048])
            lv = sb("lv", [128, 256]); lt = sb("lt", [128, 128]); ls = sb("ls", [128, 2]); le = sb("le", [128, 2])
            subl = sb("subl", [128, 1])
            psA = st.enter_context(nc.psum_tensor("psA_" + _pn(), [128, 512], F32))
            psG = [st.enter_context(nc.psum_tensor(f"psG{i}_" + _pn(), [128, 512], F32)) for i in range(4)]
            for u_ in range(8):
                P.op('pool', I('dma_start', out=wi4[u_], in_=w_in[:, u_ * 512:(u_ + 1) * 512].rearrange("(k p) c -> p k c", p=128)), dma=('wc', u_ % 4))
            P.op('pool', I('dma_start', out=wiF[:, :, :], in_=w_in[:, 4096:4104].rearrange("(k p) c -> p k c", p=128)), dma=('wc', 0))
            P.op('pool', I('dma_start', out=wo4[:, :, :], in_=w_o[:, :].rearrange("(k p) c -> p k c", p=128)), dma=('wc', 1))
            P.op('sp', I('dma_start', out=cts[:], in_=cT[:, :, :]), w=['cts'], dma='l0')
            P.op('sp', I('dma_start', out=badT[:], in_=b_adaT[:, :]), w=['badT'], dma='l1')
            P.op('sp', I('dma_start', out=bag[:], in_=b_ada_g[0, :].partition_broadcast(6)), w=['bag'], dma='l2')
            P.op('sp', I('dma_start', out=ident[:], in_=ident_d[:, :]), w=['ident'], dma='l4')
            P.op('sp', I('dma_start', out=tri[:], in_=tri_d[:, :]), w=['tri'], dma='l5')
            P.op('sp', I('dma_start', out=nbf[:], in_=nb_f[:, :]), w=['nbf'], dma='l6')
            P.op('dve', I('tensor_scalar', out=nbf[:], in0=nbf[:], scalar1=-1.0, scalar2=None, op0=ALU.mult), r=['nbf'], w=['nbf'])
            P.op('sp', I('dma_start', out=cw[:], in_=convw[:, :, :]), w=['cw'], dma='l7')
            P.op('sp', I('dma_start', out=subl[:], in_=subln[:, :]), w=['subl'], dma='l8')
            P.op('sp', I('dma_start', out=lv[:], in_=lam_v[0, :].partition_broadcast(128)), w=['lv'], dma='l9')
            for j in range(4):
                P.op('sp', I('dma_start', out=lnbc[:, j, :], in_=lnp[j, :].partition_broadcast(128)), w=['lnbc'], dma='l10')
            P.op('act', I('activation', out=scb[:], in_=cts[:], func=AF.Silu), r=['cts'], w=['scb'])
            for ch in range(6):
                P.op('pool', I('dma_start', out=wa[ch % 2][:], in_=w_ada[:, ch * 1024:(ch + 1) * 1024].rearrange("(k p) c -> p k c", p=128)),
                     w=[('wa', ch % 2)], dma=('wa', ch % 2))
                for jj in range(8):
                    j = ch * 8 + jj
                    for k in range(8):
                        P.op('pe', I('matmul', psA[:, j * 6:(j + 1) * 6], lhsT=wa[ch % 2][:, k, jj * 128:(jj + 1) * 128], rhs=scb[:, k, :], start=(k == 0), stop=(k == 7)),
                             r=[('wa', ch % 2), 'scb'], w=['psA'])
                if ch in (2, 5):
                    gi = 0 if ch == 2 else 1
                    for half in range(2):
                        for k in range(8):
                            P.op('pe', I('matmul', psG[gi * 2 + half][0:6, :], lhsT=scb[:, k, :], rhs=wa[ch % 2][:, k, half * 512:(half + 1) * 512], start=(k == 0), stop=(k == 7)),
                                 r=[('wa', ch % 2), 'scb'], w=[('psG', gi * 2 + half)])
                        P.op('dve', I('tensor_tensor', out=modG[:, gi * 1024 + half * 512: gi * 1024 + (half + 1) * 512], in0=psG[gi * 2 + half][0:6, :],
                                                                               in1=bag[:, gi * 1024 + half * 512: gi * 1024 + (half + 1) * 512], op=ALU.add),
                             r=[('psG', gi * 2 + half), 'bag'], w=['modG'])
            for j in range(6):
                P.op('dve', I('tensor_tensor', out=modF[:, :, j], in0=psA[:, 0:288].rearrange("p (c j) -> p c j", j=6)[:, :, j], in1=badT[:, :], op=ALU.add),
                     r=['psA', 'badT'], w=['modF'])
            P.op('dve', I('tensor_scalar', out=sc1p[:], in0=modF[:, 8:16, :], scalar1=1.0, scalar2=None, op0=ALU.add), r=['modF'], w=['sc1p'])
            P.op('dve', I('tensor_scalar', out=sc2p[:], in0=modF[:, 32:40, :], scalar1=1.0, scalar2=None, op0=ALU.add), r=['modF'], w=['sc2p'])
            P.op('dve', I('tensor_tensor', out=lt[:, 0:64], in0=lv[:, 0:64], in1=lv[:, 64:128], op=ALU.mult), r=['lv'], w=['lt'])
            P.op('dve', I('tensor_tensor', out=lt[:, 64:128], in0=lv[:, 128:192], in1=lv[:, 192:256], op=ALU.mult), r=['lv'], w=['lt'])
            P.op('dve', I('reduce_sum', out=ls[:, :], in_=lt[:, :].rearrange("p (a b) -> p a b", a=2), axis=mybir.AxisListType.X), r=['lt'], w=['ls'])
            P.op('act', I('activation', out=le[:], in_=ls[:], func=AF.Exp), r=['ls'], w=['le'])
            P.op('dve', I('tensor_tensor', out=neg_lam[:], in0=le[:, 1:2], in1=le[:, 0:1], op=ALU.subtract), r=['le'], w=['nl'])
            P.op('dve', I('tensor_scalar', out=neg_lam[:], in0=neg_lam[:], scalar1=-LAM_INIT, scalar2=None, op0=ALU.add), r=['nl'], w=['nl'])
            P.op('dve', I('tensor_scalar', out=wrs[:], in0=subl[:], scalar1=1.0 - LAM_INIT, scalar2=None, op0=ALU.mult), r=['subl'], w=['wrs'])
            P.op('sp', I('dma_start', out=MG[:, :], in_=modG[:]), r=['modG'], dma='mg')
            if DBG:
                dbgF = dout("dbg_modF", [128, 288]); dbgG = dout("dbg_modG", [6, 2048]); dbgM = dout("dbg_misc", [2, 128, 1])
                P.op('sp', I('dma_start', out=dbgF[:, :], in_=modF[:].rearrange("p a b -> p (a b)")), r=['modF'], dma='dbg')
                P.op('sp', I('dma_start', out=dbgG[:, :], in_=modG[:]), r=['modG'], dma='dbg')
                P.op('sp', I('dma_start', out=dbgM[0], in_=neg_lam[:]), r=['nl'], dma='dbg')
                P.op('sp', I('dma_start', out=dbgM[1], in_=wrs[:]), r=['wrs'], dma='dbg')
            P.emit()
        if STOP == 's0':
            return nc

        for g in [int(c_) for c_ in os.environ.get('KG', '012')]:
            Tg = TG[g]; N = min(512, Tg); nblk = Tg // N; ntile = Tg // 128
            with ExitStack() as st:
                def sb(name, shape, dt=F32):
                    _uid[0] += 1
                    return st.enter_context(nc.sbuf_tensor(f"{name}_u{_uid[0]}", list(shape), dt))
                P = Prog(nc, f"s1g{g}")
                uT = sb("uT", [128, 8, Tg], BF)
                xs = [sb(f"xs{i}", [128, 8, N]) for i in range(1)] * 2
                wr = [sb(f"wr{i}", [128, 8, 512], BF) for i in range(3)]
                wff = sb("wff", [128, 8, 8], BF)
                rC = sb("rC", [128, Tg]); rS = sb("rS", [128, Tg])
                t1 = [sb(f"t1_{i}", [128, N]) for i in range(2)]; t2 = [sb(f"t2_{i}", [128, N]) for i in range(2)]
                r32 = [sb(f"r32_{i}", [128, N]) for i in range(2)]
                stg = [sb(f"stg{i}", [128, N], BF) for i in range(2)]
                v32 = [sb(f"v32_{i}", [128, 512]) for i in range(2)]; vb = [sb(f"vb{i}", [128, 512], BF) for i in range(2)]
                lf = sb("lf", [8, Tg]); lfp = sb("lfp", [8, P_LEN if g == 2 else 8]); cum = sb("cum", [8, WK]); r1 = sb("r1", [8, WK]); ones8 = sb("ones8", [8, WK])
                spl = sb("spl", [8, 6, WK], BF)
                ps = [st.enter_context(nc.psum_tensor(f"ps{i}_" + _pn(), [128, 512], F32)) for i in range(8)]
                pctr = [0]

                def nbank():
                    pctr[0] += 1
                    return pctr[0] % 8
                sctr = [0]
                xTsrc = xT_p[g] if g < 2 else xT_s
                bg = []
                if g == int(os.environ.get('KG', '012')[0]):
                    for f_ in range(NF):
                        bg.append(I('dma_start', out=wu4[f_][:, :, 0:128], in_=w_up[:, f_ * 128:(f_ + 1) * 128].rearrange("(k p) c -> p k c", p=128)))
                        bg.append(I('dma_start', out=wu4[f_][:, :, 128:256], in_=w_up[:, DFF + f_ * 128:DFF + (f_ + 1) * 128].rearrange("(k p) c -> p k c", p=128)))
                    for f0 in (0, 11):
                        bg.append(I('dma_start', out=wd4[:, f0:f0 + 11, :], in_=w_down[f0 * 128:(f0 + 11) * 128, :].rearrange("(k p) c -> p k c", p=128)))
                bgn = [0]

                def bgpop():
                    if bg:
                        P.op('pool', bg.pop(0), dma=('bgc', bgn[0] % 8)); bgn[0] += 1
                P.op('sp', I('dma_start', out=rC[:], in_=(ropeC_p if g < 2 else ropeC_s)[:, :]), w=['rC'], dma='rc')
                P.op('sp', I('dma_start', out=rS[:], in_=(ropeS_p if g < 2 else ropeS_s)[:, :]), w=['rS'], dma='rs')
                P.op('pool', I('memset', ones8[:], 1.0), w=['ones8'])
                for blk in range(nblk):
                    xb = xs[blk % 2]
                    P.op('sp', I('dma_start', out=xb[:], in_=xTsrc[:, :, blk * N:(blk + 1) * N]), w=[('xs', 0)], dma=('xs', 0))
                    for k in range(8):
                        if g < 2:
                            P.op('pool', I('tensor_scalar', out=uT[:, k, blk * N:(blk + 1) * N], in0=xb[:, k, :], scalar1=sc1p[:, k, g:g + 1], scalar2=modF[:, k, g:g + 1], op0=ALU.mult, op1=ALU.add),
                                 r=[('xs', 0)], w=[('uT', blk)])
                            bgpop()
                        else:
                            for s in range(4):
                                P.op('pool', I('tensor_scalar', out=uT[:, k, s * 32:(s + 1) * 32], in0=xb[:, k, s * 32:(s + 1) * 32], scalar1=sc1p[:, k, 2 + s:3 + s], scalar2=modF[:, k, 2 + s:3 + s], op0=ALU.mult, op1=ALU.add),
                                     r=[('xs', 0)], w=[('uT', blk)])
                uTall = [('uT', b) for b in range(nblk)]

                def loadw(c0, ncols=512):
                    i = sctr[0] % 3; sctr[0] += 1
                    P.op('sp', I('dma_start', out=wr[i][:, :, :], in_=wi4[c0 // 512]), w=[('wr', i)], dma=('wr', i))
                    return i

                def fm_group(slot, co, blk):
                    b = nbank()
                    for k in range(8):
                        P.op('pe', I('matmul', ps[b][:, 0:N], lhsT=wr[slot][:, k, co:co + 128], rhs=uT[:, k, blk * N:(blk + 1) * N], start=(k == 0), stop=(k == 7)),
                             r=[('wr', slot), ('uT', blk)], w=[('ps', b)])
                    return b
                octr = [0]
                for which, c0 in (((('q', 0), ('k', 512)) if not os.environ.get('KQ') else (('q', 0),)) if LVL >= 2 else ()):
                    slot = loadw(c0)
                    for c in range(4):
                        for blk in range(nblk):
                            b = fm_group(slot, c * 128, blk)
                            i = octr[0] % 2; octr[0] += 1
                            if which == 'q':
                                P.op('act', I('activation', out=stg[i][:], in_=ps[b][:, 0:N], func=AF.Identity), r=[('ps', b)], w=[('stg', i)])
                            else:
                                P.op('act', I('activation', out=r32[i][:], in_=ps[b][:, 0:N], func=AF.Identity), r=[('ps', b)], w=[('r32', i)])
                                P.op('pool', I('tensor_copy', out=stg[i][:], in_=r32[i][:]), r=[('r32', i)], w=[('stg', i)])
                                P.op('sp', I('dma_start', out=fkT_o[g][c * 128:(c + 1) * 128, blk * N:(blk + 1) * N], in_=r32[i][:]), r=[('r32', i)], dma=('r32o', i))
                            dst = (QF if which == 'q' else KF)[g]
                            P.op('sp', I('dma_start', out=dst[c * 128:(c + 1) * 128, blk * N:(blk + 1) * N], in_=stg[i][:]), r=[('stg', i)], dma=('stgo', i))
                for which, c0 in ((('q', 1024), ('k', 2048)) if LVL >= 3 else ()):
                    sa = loadw(c0); sbw = loadw(c0 + 512)
                    for c in range(4):
                        for blk in range(nblk):
                            ba = fm_group(sa, c * 128, blk); bb = fm_group(sbw, c * 128, blk)
                            i = octr[0] % 2; octr[0] += 1
                            P.op('dve', I('tensor_tensor', out=t1[i][:], in0=ps[ba][:, 0:N], in1=rC[:, blk * N:(blk + 1) * N], op=ALU.mult), r=[('ps', ba), 'rC'], w=[('t1', i)])
                            P.op('dve', I('tensor_tensor', out=t2[i][:], in0=ps[bb][:, 0:N], in1=rS[:, blk * N:(blk + 1) * N], op=ALU.mult), r=[('ps', bb), 'rS'], w=[('t2', i)])
                            P.op('pool', I('tensor_tensor', out=r32[i][:], in0=t1[i][:], in1=t2[i][:], op=ALU.add), r=[('t1', i), ('t2', i)], w=[('r32', i)])
                            bgpop()
                            P.op('act', I('activation', out=stg[i][:], in_=r32[i][:], func=AF.Identity), r=[('r32', i)], w=[('stg', i)])
                            dst = (QD if which == 'q' else KD)[g]
                            P.op('sp', I('dma_start', out=dst[c * 128:(c + 1) * 128, blk * N:(blk + 1) * N], in_=stg[i][:]), r=[('stg', i)], dma=('stgo', i))
                            if which == 'k':
                                P.op('sp', I('dma_start', out=dkT_o[g][c * 128:(c + 1) * 128, blk * N:(blk + 1) * N], in_=r32[i][:]), r=[('r32', i)], dma=('r32o', i))
                for c0, vo, vs in (((3072, fv_o[g], VF[g]), (3584, dv_o[g], VD[g])) if LVL >= 4 else ()):
                    slot = loadw(c0)
                    for tt in range(ntile):
                        b = nbank()
                        for k in range(8):
                            P.op('pe', I('matmul', ps[b][:, :], lhsT=uT[:, k, tt * 128:(tt + 1) * 128], rhs=wr[slot][:, k, :], start=(k == 0), stop=(k == 7)),
                                 r=[('wr', slot)] + uTall, w=[('ps', b)])
                        i = octr[0] % 2; octr[0] += 1
                        P.op('act', I('activation', out=v32[i][:], in_=ps[b][:, :], func=AF.Identity), r=[('ps', b)], w=[('v32', i)])
                        P.op('pool', I('tensor_copy', out=vb[i][:], in_=v32[i][:]), r=[('v32', i)], w=[('vb', i)])
                        P.op('sp', I('dma_start', out=vo[tt * 128:(tt + 1) * 128, :], in_=v32[i][:]), r=[('v32', i)], dma=('v32o', i))
                        P.op('sp', I('dma_start', out=vs[tt * 128:(tt + 1) * 128, :], in_=vb[i][:]), r=[('vb', i)], dma=('vbo', i))
                P.op('sp', I('dma_start', out=wff[:], in_=wiF[:, :, :]), w=['wff'], dma='wff')
                for blk in (range(nblk) if LVL >= 5 else ()):
                    b = nbank()
                    for k in range(8):
                        P.op('pe', I('matmul', ps[b][0:8, 0:N], lhsT=wff[:, k, :], rhs=uT[:, k, blk * N:(blk + 1) * N], start=(k == 0), stop=(k == 7)),
                             r=['wff', ('uT', blk)], w=[('ps', b)])
                    P.op('act', I('activation', out=lf[:, blk * N:(blk + 1) * N], in_=ps[b][0:8, 0:N], func=AF.Exp, bias=nbf[:, 0:1], scale=-1.0), r=[('ps', b)], w=['lf'])
                P.op('act', I('activation', out=lf[:], in_=lf[:], func=AF.Ln, bias=1.0, scale=1.0), r=['lf'], w=['lf'])
                P.op('dve', I('tensor_scalar', out=lf[:], in0=lf[:], scalar1=-1.0, scalar2=None, op0=ALU.mult), r=['lf'], w=['lf'])
                P.op('sp', I('dma_start', out=lfT_o[g][:, :], in_=lf[:]), r=['lf'], dma='lfo')

                def splits(width, csdst):
                    P.op('dve', I('tensor_scalar', out=r1[:, 0:width], in0=cum[:, 0:width], scalar1=8.0, scalar2=None, op0=ALU.mult), r=['cum'], w=['r1'])
                    for j in range(3):
                        P.op('dve', I('tensor_copy', out=spl[:, j, 0:width], in_=r1[:, 0:width]), r=['r1'], w=['spl'])
                        if j < 2:
                            P.op('dve', I('tensor_tensor', out=r1[:, 0:width], in0=r1[:, 0:width], in1=spl[:, j, 0:width], op=ALU.subtract), r=['r1', 'spl'], w=['r1'])
                    P.op('dve', I('tensor_scalar', out=spl[:, 3:6, 0:width], in0=spl[:, 0:3, 0:width], scalar1=-1.0, scalar2=None, op0=ALU.mult), r=['spl'], w=['spl'])
                    P.op('sp', I('dma_start', out=csdst[:, :, 0:width], in_=spl[:, :, 0:width]), r=['spl'], dma='cso')
                if LVL < 6:
                    pass
                elif g < 2:
                    P.op('dve', I('tensor_tensor_scan', out=cum[:, 0:T], data0=ones8[:, 0:T], data1=lf[:, :], initial=0.0, op0=ALU.mult, op1=ALU.add), r=['lf', 'ones8'], w=['cum'])
                    splits(T, CS[g])
                else:
                    for s in range(4):
                        P.op('sp', I('dma_start', out=lfp[:], in_=lfT_s[s]), w=['lfp'], dma='lfp')
                        P.op('dve', I('tensor_tensor_scan', out=cum[:, 0:P_LEN], data0=ones8[:, 0:P_LEN], data1=lfp[:, :], initial=0.0, op0=ALU.mult, op1=ALU.add), r=['lfp', 'ones8', 'spl'], w=['cum'])
                        P.op('dve', I('tensor_tensor_scan', out=cum[:, P_LEN:P_LEN + 32], data0=ones8[:, 0:32], data1=lf[:, s * 32:(s + 1) * 32], initial=cum[:, P_LEN - 1:P_LEN], op0=ALU.mult, op1=ALU.add), r=['lf', 'cum'], w=['cum'])
                        splits(P_LEN + 32, CS[2 + s])
                while bg:
                    bgpop()
                P.emit()
            if STOP == f's1g{g}':
                return nc

            with ExitStack() as st:
                def sb(name, shape, dt=F32):
                    _uid[0] += 1
                    return st.enter_context(nc.sbuf_tensor(f"{name}_u{_uid[0]}", list(shape), dt))
                P = Prog(nc, f"s2g{g}")
                if g < 2:
                    Qa = [sb(f"Qa{i}", [128, Tg], BF) for i in range(2)]; Ka = [sb(f"Ka{i}", [128, WK], BF) for i in range(2)]
                    Qd = [[sb(f"Qd{i}_{j}", [128, Tg], BF) for j in range(2)] for i in range(2)]; Kd = [sb(f"Kd{i}", [128, WK], BF) for i in range(2)]
                    Vf = [sb(f"Vf{i}", [128, 17, 66], BF) for i in range(2)]; Vd = [sb(f"Vd{i}", [128, 17, 130], BF) for i in range(2)]
                else:
                    KaA = [sb(f"KaA{i}", [128, 8, WK], BF) for i in range(2)]; KdA = sb("KdA", [128, 4, WK], BF)
                    QaA = [sb(f"QaA{i}", [128, 8, 32], BF) for i in range(2)]; QzA = [sb(f"QzA{i}", [128, 4, 32], BF) for i in range(2)]
                    Vst = [sb(f"Vst{i}", [128, 16, 512], BF) for i in range(2)]
                    VfA = sb("VfA", [128, 17, 8, 66], BF); VdA = sb("VdA", [128, 17, 4, 130], BF)
                NSB = 4; NPT = 4
                PT = [sb(f"PT{i}", [128, 512], BF) for i in range(NPT)]
                OT = sb("OT", [128, 16 if g < 2 else 4, D], BF)
                rec = [sb(f"rec{i}", [128, 4]) for i in range(2)]; nl = [sb(f"nl{i}", [128, 4]) for i in range(2)]
                a32 = [sb(f"a32_{i}", [128, 4, 128]) for i in range(2)]; d32 = [sb(f"d32_{i}", [128, 4, 128]) for i in range(2)]
                mhalf = sb("mhalf", [128, 4])
                P.op('pool', I('memset', mhalf[:], -0.5), w=['mhalf'])
                junk = sb("junk", [128, 128]); ss = [sb(f"ss{i}", [128, 4]) for i in range(2)]; rstd = [sb(f"rstd{i}", [128, 4]) for i in range(2)]
                Sb = [st.enter_context(nc.psum_tensor(f"Sb{i}_" + _pn(), [128, 512], F32)) for i in range(NSB)]
                Ob = [st.enter_context(nc.psum_tensor(f"Ob{i}_" + _pn(), [128, 512], F32)) for i in range(4)]
                if g < 2:
                    for i in range(2):
                        P.op('pool', I('memset', Qa[i][64:128, :], 0.0), w=[('Qa', i)])
                        P.op('sp', I('dma_start', out=Qa[i][67:70, 0:min(Tg, T)], in_=ones3_d[:, 0:min(Tg, T)]), w=[('Qa', i)], dma=('Qa', i))
                        P.op('pool', I('memset', Ka[i][64:128, :], 1.0), w=[('Ka', i)])
                        P.op('pool', I('memset', Qd[i][0][64:128, :], 0.0), w=[('Qd', i)])
                        P.op('pool', I('memset', Qd[i][1][0:64, :], 0.0), w=[('Qd', i)])
                        P.op('pool', I('memset', Vf[i][:, :, 64:66], 1.0), w=[('Vf', i)])
                        P.op('pool', I('memset', Vd[i][:, :, 128:130], 1.0), w=[('Vd', i)])
                else:
                    for i in range(2):
                        P.op('pool', I('memset', QaA[i][64:128, :, :], 0.0), w=[('Qa', i)])
                        P.op('sp', I('dma_start', out=QaA[i][67:70, :, :], in_=ones3_d[:, 0:256].rearrange("j (h t) -> j h t", h=8)), w=[('Qa', i)], dma=('Qa', i))
                        P.op('pool', I('memset', KaA[i][64:128, :, :], 1.0), w=[('Ka', i)])
                    P.op('pool', I('memset', QzA[0][64:128, :, :], 0.0), w=[('Qd', 0)])
                    P.op('pool', I('memset', QzA[1][0:64, :, :], 0.0), w=[('Qd', 0)])
                    P.op('pool', I('memset', VfA[:, :, :, 64:66], 1.0), w=[('Vf', 0)])
                    P.op('pool', I('memset', VdA[:, :, :, 128:130], 1.0), w=[('Vd', 0)])
                sctr = [0]; pctr = [0]; uctr = [0]
                bg2 = []
                glist = [int(c_) for c_ in os.environ.get('KG', '012')]
                if g < 2 and 2 in glist:
                    mine = [0, 1] if (g == 0 and 1 in glist) else ([2, 3] if g == 1 and 0 in glist else [0, 1, 2, 3])
                    for s_ in mine:
                        bg2.append(I('dma_start', out=KaS[s_][:, :], in_=fkT_s[s_]))
                        bg2.append(I('dma_start', out=VfS[s_][:, :], in_=fv_s[s_]))
                        bg2.append(I('dma_start', out=KdS[s_][:, :], in_=dkT_s[s_]))
                        bg2.append(I('dma_start', out=VdS[s_][:, :], in_=dv_s[s_]))
                bg2n = [0]; bg2c = [0]

                def bg2pop(force=False):
                    bg2c[0] += 1
                    if bg2 and (force or bg2c[0] % 12 == 0):
                        P.op('pool', bg2.pop(0), dma=('bgc', bg2n[0] % 4)); bg2n[0] += 1
                nseq = 1 if g < 2 else 4
                Lq = Tg if g < 2 else 32
                npast = 0 if g < 2 else 16
                for s in range(nseq):
                    cs = CS[g] if g < 2 else CS[2 + s]
                    qc0 = 0 if g < 2 else s * 32
                    qpos0 = 0 if g < 2 else P_LEN
                    nqb = Lq // 512 if g < 2 else 1
                    QB = 512 if g < 2 else 32

                    def attend(kind, h, b, ov=None):
                        subs = (0,) if kind == 'f' else (0, 1)
                        if ov is None:
                            Kt = Ka[b] if kind == 'f' else Kd[b]
                            Qts = [Qa[b]] if kind == 'f' else Qd[b]
                            Vt = Vf[b] if kind == 'f' else Vd[b]
                            kr = ('Ka', b) if kind == 'f' else ('Kd', b)
                            qr = ('Qa', b) if kind == 'f' else ('Qd', b)
                            vr = ('Vf', b) if kind == 'f' else ('Vd', b)
                        else:
                            Kt, Qts, Vt, kr, qr, vr = ov
                        KR = 128
                        VW = 65 if kind == 'f' else 129
                        for qb in range(nqb):
                            u = uctr[0] % 2; uctr[0] += 1
                            for sub in subs:
                                pb0 = 0
                                Qt = Qts[sub]
                                if g < 2:
                                    kts = list(range(0, 4 * qb + 4))
                                else:
                                    kts = list(range(17))
                                if kind == 'f':
                                    obk = [Ob[u * 2]] * 4 if g < 2 else [Ob[u * 2]]
                                    obn = [u * 2] * 4
                                    ocol = [qt * 65 for qt in range(4)]
                                else:
                                    obn = [sub * 2 + qt // 2 for qt in range(4)]
                                    obk = [Ob[n] for n in obn]
                                    ocol = [(qt % 2) * 129 for qt in range(4)]
                                started = set()
                                if g == 2:
                                    sbk = sctr[0] % NSB; sctr[0] += 1
                                    pt = pctr[0] % NPT; pctr[0] += 1
                                    for kt in range(16):
                                        P.op('pe', I('matmul', Sb[sbk][:, kt * 32:(kt + 1) * 32], lhsT=Kt[pb0:pb0 + KR, kt * 128:(kt + 1) * 128], rhs=Qt[pb0:pb0 + KR, 0:32], start=True, stop=True),
                                             r=[kr, qr], w=[('S', sbk)])
                                    P.op('act', I('activation', out=PT[pt][:, :], in_=Sb[sbk][:, :], func=AF.Exp, scale=0.125), r=[('S', sbk)], w=[('PT', pt)])
                                    for kt in range(16):
                                        P.op('pe', I('matmul', obk[0][0:32, ocol[0]:ocol[0] + VW], lhsT=PT[pt][:, kt * 32:(kt + 1) * 32], rhs=Vt[:, kt, 0:VW], start=(kt == 0), stop=False, skip_group_check=True),
                                             r=[('PT', pt), vr], w=[('O', obn[0])])
                                    sbk = sctr[0] % NSB; sctr[0] += 1
                                    pt = pctr[0] % NPT; pctr[0] += 1
                                    P.op('pe', I('matmul', Sb[sbk][0:32, 0:32], lhsT=Kt[pb0:pb0 + KR, P_LEN:P_LEN + 32], rhs=Qt[pb0:pb0 + KR, 0:32], start=True, stop=True),
                                         r=[kr, qr], w=[('S', sbk)])
                                    P.op('act', I('activation', out=PT[pt][0:32, 0:32], in_=Sb[sbk][0:32, 0:32], func=AF.Exp, scale=0.125), r=[('S', sbk)], w=[('PT', pt)])
                                    if kind == 'f':
                                        P.op('pool', I('tensor_tensor', out=PT[pt][0:32, 0:32], in0=PT[pt][0:32, 0:32], in1=tri[0:32, 0:32], op=ALU.mult), r=[('PT', pt)], w=[('PT', pt)])
                                    P.op('pe', I('matmul', obk[0][0:32, ocol[0]:ocol[0] + VW], lhsT=PT[pt][0:32, 0:32], rhs=Vt[0:32, 16, 0:VW], start=False, stop=True, skip_group_check=True),
                                         r=[('PT', pt), vr], w=[('O', obn[0])])
                                else:
                                    recs = []

                                    def front(kt):
                                        j = kt - 4 * qb
                                        qoff = max(j, 0) * 128; nq = 512 - qoff
                                        sbk = sctr[0] % NSB; sctr[0] += 1
                                        pt = pctr[0] % NPT; pctr[0] += 1
                                        P.op('pe', I('matmul', Sb[sbk][:, 0:nq], lhsT=Kt[pb0:pb0 + KR, kt * 128:(kt + 1) * 128], rhs=Qt[pb0:pb0 + KR, qb * 512 + qoff:(qb + 1) * 512], start=True, stop=True),
                                             r=[kr, qr], w=[('S', sbk)])
                                        P.op('act', I('activation', out=PT[pt][:, 0:nq], in_=Sb[sbk][:, 0:nq], func=AF.Exp, scale=0.125), r=[('S', sbk)], w=[('PT', pt)])
                                        if j >= 0:
                                            if kind == 'f':
                                                P.op('pool', I('tensor_tensor', out=PT[pt][:, 0:128], in0=PT[pt][:, 0:128], in1=tri[:, :], op=ALU.mult), r=[('PT', pt)], w=[('PT', pt)])
                                            else:
                                                P.op('pool', I('memset', PT[pt][64:128, 0:64], 0.0), r=[('PT', pt)], w=[('PT', pt)])
                                            bg2pop()
                                        return (kt, j, pt)

                                    def back(rc_):
                                        kt, j, pt = rc_
                                        for qt in range(max(j, 0), 4):
                                            cc = (qt - max(j, 0)) * 128
                                            first = obn[qt] not in started
                                            started.add(obn[qt])
                                            P.op('pe', I('matmul', obk[qt][:, ocol[qt]:ocol[qt] + VW], lhsT=PT[pt][:, cc:cc + 128], rhs=Vt[:, kt, 0:VW], start=first, stop=(kt == 4 * qb + qt), skip_group_check=True),
                                                 r=[('PT', pt), vr], w=[('O', obn[qt])])
                                    LA = 3
                                    for idx, kt in enumerate(kts):
                                        recs.append(front(kt))
                                        if idx >= LA:
                                            back(recs[idx - LA])
                                    for rc_ in recs[max(0, len(kts) - LA):]:
                                        back(rc_)
                                nqt = 4 if g < 2 else 1
                                rows = 128 if g < 2 else 32
                                for qt in range(nqt):
                                    P.op('dve', I('reciprocal', out=rec[u][0:rows, qt:qt + 1], in_=obk[qt][0:rows, ocol[qt] + VW - 1:ocol[qt] + VW]), r=[('O', obn[qt])], w=[('rec', u)])
                                tile0 = qb * 4 if g < 2 else s
                                if kind == 'f':
                                    for qt in range(nqt):
                                        P.op('dve', I('tensor_scalar', out=OT[0:rows, tile0 + qt, h * 64:(h + 1) * 64], in0=obk[qt][0:rows, ocol[qt]:ocol[qt] + 64], scalar1=rec[u][0:rows, qt:qt + 1], scalar2=None, op0=ALU.mult),
                                             r=[('O', obn[qt]), ('rec', u)], w=[('OT', tile0 + qt, kind, h)])
                                elif sub == 0:
                                    for qt in range(nqt):
                                        P.op('dve', I('tensor_scalar', out=a32[u][0:rows, qt, :], in0=obk[qt][0:rows, ocol[qt]:ocol[qt] + 128], scalar1=rec[u][0:rows, qt:qt + 1], scalar2=None, op0=ALU.mult),
                                             r=[('O', obn[qt]), ('rec', u)], w=[('a32', u)])
                                else:
                                    P.op('dve', I('tensor_scalar', out=nl[u][0:rows, 0:nqt], in0=rec[u][0:rows, 0:nqt], scalar1=neg_lam[0:rows, 0:1], scalar2=None, op0=ALU.mult), r=[('rec', u)], w=[('nl', u)])
                                    for qt in range(nqt):
                                        P.op('dve', I('scalar_tensor_tensor', out=d32[u][0:rows, qt, :], in0=obk[qt][0:rows, ocol[qt]:ocol[qt] + 128], scalar=nl[u][0:rows, qt:qt + 1], in1=a32[u][0:rows, qt, :], op0=ALU.mult, op1=ALU.add),
                                             r=[('O', obn[qt]), ('nl', u), ('a32', u)], w=[('d32', u)])
                                        P.op('dve', I('scalar_tensor_tensor', out=junk[0:rows, :], in0=d32[u][0:rows, qt, :], scalar=1.0 / 128.0, in1=d32[u][0:rows, qt, :], op0=ALU.mult, op1=ALU.mult, accum_out=ss[u][0:rows, qt:qt + 1]), r=[('d32', u)], w=['junk', ('ss', u)])
                                    P.op('dve', I('tensor_scalar', out=ss[u][0:rows, 0:nqt], in0=ss[u][0:rows, 0:nqt], scalar1=1e-6, scalar2=None, op0=ALU.add), r=[('ss', u)], w=[('ss', u)])
                                    P.op('pool', I('tensor_tensor', out=rstd[u][0:rows, 0:nqt], in0=ss[u][0:rows, 0:nqt], in1=mhalf[0:rows, 0:nqt], op=ALU.pow), r=[('ss', u)], w=[('rstd', u)])
                                    for qt in range(nqt):
                                        P.op('dve', I('tensor_scalar', out=OT[0:rows, tile0 + qt, 512 + h * 128:512 + (h + 1) * 128], in0=d32[u][0:rows, qt, :], scalar1=rstd[u][0:rows, qt:qt + 1], scalar2=None, op0=ALU.mult),
                                             r=[('d32', u), ('rstd', u)], w=[('OT', tile0 + qt, kind, h)])

                    if g == 2:
                        b = s % 2
                        pre_ = (0 in glist or 1 in glist)

                        def ld_ka(s_):
                            b_ = s_ % 2; cs_ = CS[2 + s_]; c0_ = s_ * 32
                            if pre_:
                                P.op('sp', I('dma_start', out=KaA[b_][0:64, :, 0:P_LEN], in_=KaS[s_][:, :].rearrange("(h d) t -> d h t", d=64)), w=[('Ka', b_)], dma=('Ka', b_))
                            else:
                                P.op('pool', I('dma_start', out=KaA[b_][0:64, :, 0:P_LEN], in_=fkT_s[s_].rearrange("(h d) t -> d h t", d=64)), w=[('Ka', b_)], dma=('Ka', b_))
                            P.op('sp', I('dma_start', out=KaA[b_][0:64, :, P_LEN:P_LEN + 32], in_=KF[g][:, c0_:c0_ + 32].rearrange("(h d) t -> d h t", d=64)), w=[('Ka', b_)], dma=('Ka', b_))
                            P.op('sp', I('dma_start', out=KaA[b_][67:70, :, 0:P_LEN + 32], in_=cs_[:, 3:6, 0:P_LEN + 32].rearrange("h j t -> j h t")), w=[('Ka', b_)], dma=('Ka', b_))
                            P.op('sp', I('dma_start', out=QaA[b_][0:64, :, :], in_=QF[g][:, c0_:c0_ + 32].rearrange("(h d) t -> d h t", d=64)), w=[('Qa', b_)], dma=('Qa', b_))
                            P.op('sp', I('dma_start', out=QaA[b_][64:67, :, :], in_=cs_[:, 0:3, P_LEN:P_LEN + 32].rearrange("h j t -> j h t")), w=[('Qa', b_)], dma=('Qa', b_))

                        def ld_vst(s_, which):
                            if pre_:
                                src_ = VfS if which == 0 else VdS
                                P.op('sp', I('dma_start', out=Vst[which][:, :, :], in_=src_[s_][:, :].rearrange("(t p) c -> p t c", p=128)), w=[('Vst', which)], dma=('Vst', which))
                            else:
                                src_ = fv_s if which == 0 else dv_s
                                P.op('pool', I('dma_start', out=Vst[which][:, :, :], in_=src_[s_].rearrange("(t p) c -> p t c", p=128)), w=[('Vst', which)], dma=('Vst', which))
                        if s == 0:
                            ld_ka(0); ld_vst(0, 0); ld_vst(0, 1)
                        c0 = s * 32
                        P.op('pool', I('tensor_copy', out=VfA[:, 0:16, :, 0:64], in_=Vst[0][:, :, :].rearrange("p t (h c) -> p t h c", h=8)), r=[('Vst', 0)], w=[('Vf', 0)])
                        P.op('sp', I('dma_start', out=VfA[0:32, 16, :, 0:64], in_=VF[g][c0:c0 + 32, :].rearrange("t (h c) -> t h c", h=8)), w=[('Vf', 0)], dma=('Vf', 0))
                        if s + 1 < 4:
                            ld_ka(s + 1); ld_vst(s + 1, 0)
                        if pre_:
                            P.op('sp', I('dma_start', out=KdA[:, :, 0:P_LEN], in_=KdS[s][:, :].rearrange("(h d) t -> d h t", d=128)), w=[('Kd', 0)], dma=('Kd', 0))
                        else:
                            P.op('pool', I('dma_start', out=KdA[:, :, 0:P_LEN], in_=dkT_s[s].rearrange("(h d) t -> d h t", d=128)), w=[('Kd', 0)], dma=('Kd', 0))
                        P.op('sp', I('dma_start', out=KdA[:, :, P_LEN:P_LEN + 32], in_=KD[g][:, c0:c0 + 32].rearrange("(h d) t -> d h t", d=128)), w=[('Kd', 0)], dma=('Kd', 0))
                        qv_ = QD[g][:, c0:c0 + 32].rearrange("(h m d) t -> m d h t", m=2, d=64)
                        P.op('sp', I('dma_start', out=QzA[0][0:64, :, :], in_=qv_[0]), w=[('Qd', 0)], dma=('Qd', 0))
                        P.op('sp', I('dma_start', out=QzA[1][64:128, :, :], in_=qv_[1]), w=[('Qd', 0)], dma=('Qd', 0))
                        P.op('pool', I('tensor_copy', out=VdA[:, 0:16, :, 0:128], in_=Vst[1][:, :, :].rearrange("p t (h c) -> p t h c", h=4)), r=[('Vst', 1)], w=[('Vd', 0)])
                        P.op('sp', I('dma_start', out=VdA[0:32, 16, :, 0:128], in_=VD[g][c0:c0 + 32, :].rearrange("t (h c) -> t h c", h=4)), w=[('Vd', 0)], dma=('Vd', 0))
                        if s + 1 < 4:
                            ld_vst(s + 1, 1)
                        for h in range(8):
                            attend('f', h, b, ov=(KaA[b][:, h, :], [QaA[b][:, h, :]], VfA[:, :, h, :], ('Ka', b), ('Qa', b), ('Vf', 0)))
                        for h in range(4):
                            attend('d', h, 0, ov=(KdA[:, h, :], [QzA[0][:, h, :], QzA[1][:, h, :]], VdA[:, :, h, :], ('Kd', 0), ('Qd', 0), ('Vd', 0)))
                    hctr = 0
                    for kind, nh in ((('d', 4), ('f', 8)) if g < 2 else ()):
                        for h in range(nh):
                            b = hctr % 2; hctr += 1
                            if kind == 'f':
                                P.op('sp', I('dma_start', out=Qa[b][0:64, 0:Lq], in_=QF[g][h * 64:(h + 1) * 64, qc0:qc0 + Lq]), w=[('Qa', b)], dma=('Qa', b))
                                P.op('sp', I('dma_start', out=Qa[b][64:67, 0:Lq], in_=cs[h, 0:3, qpos0:qpos0 + Lq]), w=[('Qa', b)], dma=('Qa', b))
                                if g < 2:
                                    P.op('sp', I('dma_start', out=Ka[b][0:64, 0:T], in_=KF[g][h * 64:(h + 1) * 64, :]), w=[('Ka', b)], dma=('Ka', b))
                                    P.op('sp', I('dma_start', out=Ka[b][67:70, 0:T], in_=cs[h, 3:6, 0:T]), w=[('Ka', b)], dma=('Ka', b))
                                    P.op('sp', I('dma_start', out=Vf[b][:, 0:16, 0:64], in_=VF[g][:, h * 64:(h + 1) * 64].rearrange("(t p) c -> p t c", p=128)), w=[('Vf', b)], dma=('Vf', b))
                                else:
                                    P.op('pool', I('dma_start', out=Ka[b][0:64, 0:P_LEN], in_=fkT_s[s, h * 64:(h + 1) * 64, :]), w=[('Ka', b)], dma=('Ka', b))
                                    P.op('sp', I('dma_start', out=Ka[b][0:64, P_LEN:P_LEN + 32], in_=KF[g][h * 64:(h + 1) * 64, qc0:qc0 + 32]), w=[('Ka', b)], dma=('Ka', b))
                                    P.op('sp', I('dma_start', out=Ka[b][67:70, 0:P_LEN + 32], in_=cs[h, 3:6, 0:P_LEN + 32]), w=[('Ka', b)], dma=('Ka', b))
                                    P.op('pool', I('dma_start', out=Vf[b][:, 0:16, 0:64], in_=fv_s[s, :, h * 64:(h + 1) * 64].rearrange("(t p) c -> p t c", p=128)), w=[('Vf', b)], dma=('Vf', b))
                                    P.op('sp', I('dma_start', out=Vf[b][0:32, 16, 0:64], in_=VF[g][qc0:qc0 + 32, h * 64:(h + 1) * 64]), w=[('Vf', b)], dma=('Vf', b))
                            else:
                                P.op('sp', I('dma_start', out=Qd[b][0][0:64, 0:Lq], in_=QD[g][h * 128:h * 128 + 64, qc0:qc0 + Lq]), w=[('Qd', b)], dma=('Qd', b))
                                P.op('sp', I('dma_start', out=Qd[b][1][64:128, 0:Lq], in_=QD[g][h * 128 + 64:(h + 1) * 128, qc0:qc0 + Lq]), w=[('Qd', b)], dma=('Qd', b))
                                if g < 2:
                                    P.op('sp', I('dma_start', out=Kd[b][:, 0:T], in_=KD[g][h * 128:(h + 1) * 128, :]), w=[('Kd', b)], dma=('Kd', b))
                                    P.op('sp', I('dma_start', out=Vd[b][:, 0:16, 0:128], in_=VD[g][:, h * 128:(h + 1) * 128].rearrange("(t p) c -> p t c", p=128)), w=[('Vd', b)], dma=('Vd', b))
                                else:
                                    P.op('pool', I('dma_start', out=Kd[b][:, 0:P_LEN], in_=dkT_s[s, h * 128:(h + 1) * 128, :]), w=[('Kd', b)], dma=('Kd', b))
                                    P.op('sp', I('dma_start', out=Kd[b][:, P_LEN:P_LEN + 32], in_=KD[g][h * 128:(h + 1) * 128, qc0:qc0 + 32]), w=[('Kd', b)], dma=('Kd', b))
                                    P.op('pool', I('dma_start', out=Vd[b][:, 0:16, 0:128], in_=dv_s[s, :, h * 128:(h + 1) * 128].rearrange("(t p) c -> p t c", p=128)), w=[('Vd', b)], dma=('Vd', b))
                                    P.op('sp', I('dma_start', out=Vd[b][0:32, 16, 0:128], in_=VD[g][qc0:qc0 + 32, h * 128:(h + 1) * 128]), w=[('Vd', b)], dma=('Vd', b))
                            attend(kind, h, b)
                    if g < 2:
                        allot = [('OT', tt, 'f', h) for tt in range(16) for h in range(8)] + [('OT', tt, 'd', h) for tt in range(16) for h in range(4)]
                        P.op('sp', I('dma_start', out=OTOK[g][:, :].rearrange("(t p) c -> p t c", p=128), in_=OT[:, :, :]), r=allot, dma='oto')
                    else:
                        allot = [('OT', s, 'f', h) for h in range(8)] + [('OT', s, 'd', h) for h in range(4)]
                        P.op('sp', I('dma_start', out=OTOK[g][s * 32:(s + 1) * 32, :], in_=OT[0:32, s, :]), r=allot, dma='oto')
                while bg2:
                    bg2pop(force=True)
                P.emit()
            if STOP == f's2g{g}':
                return nc

            with ExitStack() as st:
                def sb(name, shape, dt=F32):
                    _uid[0] += 1
                    return st.enter_context(nc.sbuf_tensor(f"{name}_u{_uid[0]}", list(shape), dt))
                P = Prog(nc, f"s3g{g}")
                nt = N // 128
                nseg = 1 if g < 2 else 4
                L = N // nseg
                wo = sb("wo", [128, 8, D], BF); wd = sb("wd", [128, NF, D], BF)
                wu = [sb(f"wu{i}", [128, 8, 256], BF) for i in range(3)]
                xtokL = [sb(f"xtok{i}", [128, nt, D]) for i in range(2)]; otkL = [sb(f"otk{i}", [128, nt, D], BF) for i in range(2)]
                oT = sb("oT", [128, 8, N], BF); u2T = sb("u2T", [128, 8, N], BF); hT = sb("hT", [128, NF, N], BF)
                x1b = [sb(f"x1b{i}", [128, D], BF) for i in range(2)]; tmpL = [sb(f"tmp{i}", [128, D]) for i in range(2)]
                apad = [sb(f"apad{i}", [128, nseg, L + 2]) for i in range(2)]
                acc = [sb(f"acc{i}", [128, nseg, L]) for i in range(2)]; sg = [sb(f"sg{i}", [128, nseg, L], BF) for i in range(2)]
                halo = sb("halo", [128, NF, nseg, 2])
                gbc = sb("gbc", [128, 2, D])
                bstL = [sb(f"bst{i}", [128, 2, 6]) for i in range(2)]; mvL = [sb(f"mv{i}", [128, 2]) for i in range(2)]; sdL = [sb(f"sd{i}", [128, 1]) for i in range(2)]; nmrL = [sb(f"nmr{i}", [128, 1]) for i in range(2)]
                ps = [st.enter_context(nc.psum_tensor(f"q{i}_" + _pn(), [128, 512], F32)) for i in range(5)]
                tp = [st.enter_context(nc.psum_tensor(f"tp{i}_" + _pn(), [128, 1024], BF)) for i in range(3)]
                pctr = [0]

                def nbank():
                    pctr[0] += 1
                    return pctr[0] % 5
                xsrc = x_p[g] if g < 2 else x_s
                ydst = y_p[g] if g < 2 else y_s
                wctr = [0]

                lnctr = [0]

                def layernorm(lni, xtok, xk, tt=0):
                    pq = lnctr[0] % 2; lnctr[0] += 1
                    bst = bstL[pq]; mv = mvL[pq]; sd = sdL[pq]; nmr = nmrL[pq]
                    for hh in range(2):
                        P.op('dve', I('bn_stats', out=bst[:, hh, :], in_=xtok[:, tt, hh * 512:(hh + 1) * 512]), r=[('xtok', xk, tt)], w=[('bst', pq)])
                    P.op('dve', I('bn_aggr', out=mv[:], in_=bst[:].rearrange("p a b -> p (a b)")), r=[('bst', pq)], w=[('mv', pq)])
                    P.op('act', I('activation', out=sd[:], in_=mv[:, 1:2], func=AF.Sqrt, bias=1e-5, scale=1.0), r=[('mv', pq)], w=[('sd', pq)])
                    P.op('dve', I('reciprocal', out=sd[:], in_=sd[:]), r=[('sd', pq)], w=[('sd', pq)])
                    P.op('dve', I('tensor_scalar', out=nmr[:], in0=mv[:, 0:1], scalar1=sd[:, 0:1], scalar2=-1.0, op0=ALU.mult, op1=ALU.mult), r=[('mv', pq), ('sd', pq)], w=[('nmr', pq)])
                    P.op('act', I('activation', out=xtok[:, tt, :], in_=xtok[:, tt, :], func=AF.Identity, scale=sd[:, 0:1], bias=nmr[:, 0:1]), r=[('xtok', xk, tt), ('sd', pq), ('nmr', pq)], w=[('xtok', xk, tt)])
                    P.op('pool', I('tensor_tensor', out=xtok[:, tt, :], in0=xtok[:, tt, :], in1=lnbc[:, lni, :], op=ALU.mult), r=[('xtok', xk, tt)], w=[('xtok', xk, tt)])
                    P.op('pool', I('tensor_tensor', out=xtok[:, tt, :], in0=xtok[:, tt, :], in1=lnbc[:, lni + 1, :], op=ALU.add), r=[('xtok', xk, tt)], w=[('xtok', xk, tt)])

                def load_blk(blk):
                    xk = blk % 2
                    t0_ = blk * N
                    P.op('sp', I('dma_start', out=xtokL[xk][:], in_=xsrc[t0_:t0_ + N, :].rearrange("(t p) c -> p t c", p=128)), w=[('xtok', xk, tt) for tt in range(nt)], dma=('xtok', xk))
                    P.op('sp', I('dma_start', out=otkL[xk][:], in_=OTOK[g][t0_:t0_ + N, :].rearrange("(t p) c -> p t c", p=128)), w=[('otk', xk)], dma=('otk', xk))
                load_blk(0)
                P.op('sp', I('dma_start', out=wo[:], in_=wo4[:, :, :]), w=['wo'], dma='wo')
                P.op('act', I('activation', out=wo[:, 4:8, :], in_=wo[:, 4:8, :], func=AF.Identity, scale=wrs[:, 0:1]), r=['wo'], w=['wo'])
                if g < 2:
                    P.op('pool', I('memset', halo[:], 0.0), w=['halo'])
                else:
                    P.op('sp', I('dma_start', out=halo[:], in_=convT_s[:, :, :, :]), w=['halo'], dma='halo')
                for gi in range(2):
                    if g < 2:
                        P.op('sp', I('dma_start', out=gbc[:, gi, :], in_=MG[g, gi * 1024:(gi + 1) * 1024].partition_broadcast(128)), w=['gbc'], dma='gbc')
                    else:
                        for s in range(4):
                            P.op('sp', I('dma_start', out=gbc[s * 32:(s + 1) * 32, gi, :], in_=MG[2 + s, gi * 1024:(gi + 1) * 1024].partition_broadcast(32)), w=['gbc'], dma='gbc')
                P.op('sp', I('dma_start', out=wd[:], in_=wd4[:, :, :]), w=['wd'], dma='wd')
                for blk in range(nblk):
                    t0 = blk * N
                    xk = blk % 2
                    xtok = xtokL[xk]; otk = otkL[xk]
                    if blk + 1 < nblk:
                        load_blk(blk + 1)

                    bsA = {}

                    def phaseA(tt):
                        tb = tt % 2
                        tmp = tmpL[tt % 2]; tk = ('tmp', tt % 2)
                        for k in range(8):
                            P.op('pe', I('transpose', tp[tb][:, k * 128:(k + 1) * 128], otk[:, tt, k * 128:(k + 1) * 128], ident[:]), r=[('otk', xk), 'ident'], w=[('tp', tb)])
                        P.op('act', I('activation', out=oT[:, :, tt * 128:(tt + 1) * 128], in_=tp[tb][:, :].rearrange("p (k t) -> p k t", k=8), func=AF.Identity), r=[('tp', tb)], w=[('oT', tt)])
                        bs = []
                        for half in range(2):
                            b = nbank(); bs.append(b)
                            for k in range(8):
                                P.op('pe', I('matmul', ps[b][:, :], lhsT=oT[:, k, tt * 128:(tt + 1) * 128], rhs=wo[:, k, half * 512:(half + 1) * 512], start=(k == 0), stop=(k == 7)),
                                     r=[('oT', tt), 'wo'], w=[('ps', b)])
                        bsA[tt] = bs

                    def phaseL(tt):
                        tmp = tmpL[tt % 2]; tk = ('tmp', tt % 2)
                        bs = bsA[tt]
                        for half in range(2):
                            P.op('dve', I('tensor_tensor', out=tmp[:, half * 512:(half + 1) * 512], in0=ps[bs[half]][:, :], in1=gbc[:, 0, half * 512:(half + 1) * 512], op=ALU.mult), r=[('ps', bs[half]), 'gbc'], w=[tk])
                        P.op('dve', I('scalar_tensor_tensor', out=xtok[:, tt, :], in0=xtok[:, tt, :], scalar=ALPHA, in1=tmp[:], op0=ALU.mult, op1=ALU.add), r=[('xtok', xk, tt), tk], w=[('xtok', xk, tt)])
                        layernorm(0, xtok, xk, tt=tt)
                        P.op('act', I('activation', out=x1b[tt % 2][:], in_=xtok[:, tt, :], func=AF.Identity), r=[('xtok', xk, tt)], w=[('x1b', tt % 2)])

                    def phaseB(tt):
                        tb2 = 2
                        for k in range(8):
                            P.op('pe', I('transpose', tp[tb2][:, k * 128:(k + 1) * 128], x1b[tt % 2][:, k * 128:(k + 1) * 128], ident[:]), r=[('x1b', tt % 2), 'ident'], w=[('tp', tb2)])
                        for k in range(8):
                            if g < 2:
                                P.op('dve', I('tensor_scalar', out=u2T[:, k, tt * 128:(tt + 1) * 128], in0=tp[tb2][:, k * 128:(k + 1) * 128], scalar1=sc2p[:, k, g:g + 1], scalar2=modF[:, 24 + k, g:g + 1], op0=ALU.mult, op1=ALU.add),
                                     r=[('tp', tb2)], w=['u2T'])
                            else:
                                for s in range(4):
                                    P.op('dve', I('tensor_scalar', out=u2T[:, k, s * 32:(s + 1) * 32], in0=tp[tb2][:, k * 128 + s * 32:k * 128 + (s + 1) * 32], scalar1=sc2p[:, k, 2 + s:3 + s], scalar2=modF[:, 24 + k, 2 + s:3 + s], op0=ALU.mult, op1=ALU.add),
                                         r=[('tp', tb2)], w=['u2T'])
                    seq_ = []
                    for tt in range(nt):
                        seq_.append(('A', tt))
                        if tt >= 1:
                            seq_.append(('L', tt - 1))
                        if tt >= 2:
                            seq_.append(('B', tt - 2))
                    seq_ += [('L', nt - 1)]
                    if nt >= 2:
                        seq_ += [('B', nt - 2)]
                    seq_ += [('B', nt - 1)]
                    for kind_, tt_ in seq_:
                        {'A': phaseA, 'L': phaseL, 'B': phaseB}[kind_](tt_)
                    if L3 < 4:
                        continue
                    for f in range(NF):
                        sl = wctr[0] % 3; wctr[0] += 1
                        P.op('sp', I('dma_start', out=wu[sl][:, :, :], in_=wu4[f]), w=[('wu', sl)], dma=('wu', sl))
                        ba = nbank(); bb = nbank()
                        for k in range(8):
                            P.op('pe', I('matmul', ps[ba][:, 0:N], lhsT=wu[sl][:, k, 0:128], rhs=u2T[:, k, :], start=(k == 0), stop=(k == 7)), r=[('wu', sl), 'u2T'], w=[('ps', ba)])
                        for k in range(8):
                            P.op('pe', I('matmul', ps[bb][:, 0:N], lhsT=wu[sl][:, k, 128:256], rhs=u2T[:, k, :], start=(k == 0), stop=(k == 7)), r=[('wu', sl), 'u2T'], w=[('ps', bb)])
                        ab = f % 2
                        P.op('act', I('activation', out=apad[ab][:, :, 2:L + 2], in_=ps[ba][:, 0:N].rearrange("p (s l) -> p s l", s=nseg), func=AF.Identity), r=[('ps', ba)], w=[('apad', ab)])
                        P.op('pool', I('tensor_copy', out=apad[ab][:, :, 0:2], in_=halo[:, f, :, :]), r=[('halo', f)], w=[('apad', ab)])
                        P.op('dve', I('tensor_scalar', out=acc[ab][:], in0=apad[ab][:, :, 0:L], scalar1=cw[:, f, 0:1], scalar2=cw[:, f, 3:4], op0=ALU.mult, op1=ALU.add), r=[('apad', ab)], w=[('acc', ab)])
                        P.op('dve', I('scalar_tensor_tensor', out=acc[ab][:], in0=apad[ab][:, :, 1:L + 1], scalar=cw[:, f, 1:2], in1=acc[ab][:], op0=ALU.mult, op1=ALU.add), r=[('apad', ab), ('acc', ab)], w=[('acc', ab)])
                        P.op('dve', I('scalar_tensor_tensor', out=acc[ab][:], in0=apad[ab][:, :, 2:L + 2], scalar=cw[:, f, 2:3], in1=acc[ab][:], op0=ALU.mult, op1=ALU.add), r=[('apad', ab), ('acc', ab)], w=[('acc', ab)])
                        P.op('pool', I('tensor_copy', out=halo[:, f, :, :], in_=apad[ab][:, :, L:L + 2]), r=[('apad', ab)], w=[('halo', f)])
                        P.op('act', I('activation', out=sg[ab][:], in_=acc[ab][:], func=AF.Silu), r=[('acc', ab)], w=[('sg', ab)])
                        P.op('dve', I('tensor_tensor', out=hT[:, f, :].rearrange("p (s l) -> p s l", s=nseg), in0=sg[ab][:], in1=ps[bb][:, 0:N].rearrange("p (s l) -> p s l", s=nseg), op=ALU.mult), r=[('sg', ab), ('ps', bb)], w=['hT'])
                    if L3 < 5:
                        continue
                    for tt in range(nt):
                        tmp = tmpL[tt % 2]; tk = ('tmp', tt % 2)
                        bs = []
                        for half in range(2):
                            b = nbank(); bs.append(b)
                            for f in range(NF):
                                P.op('pe', I('matmul', ps[b][:, :], lhsT=hT[:, f, tt * 128:(tt + 1) * 128], rhs=wd[:, f, half * 512:(half + 1) * 512], start=(f == 0), stop=(f == NF - 1)),
                                     r=['hT', 'wd'], w=[('ps', b)])
                        for half in range(2):
                            P.op('dve', I('tensor_tensor', out=tmp[:, half * 512:(half + 1) * 512], in0=ps[bs[half]][:, :], in1=gbc[:, 1, half * 512:(half + 1) * 512], op=ALU.mult), r=[('ps', bs[half]), 'gbc'], w=[tk])
                        P.op('dve', I('scalar_tensor_tensor', out=xtok[:, tt, :], in0=xtok[:, tt, :], scalar=ALPHA, in1=tmp[:], op0=ALU.mult, op1=ALU.add), r=[('xtok', xk, tt), tk], w=[('xtok', xk, tt)])
                        layernorm(2, xtok, xk, tt=tt)
                        P.op('sp', I('dma_start', out=ydst[t0 + tt * 128:t0 + (tt + 1) * 128, :], in_=xtok[:, tt, :]), r=[('xtok', xk, tt)], dma=('yo', tt))
                P.op('sp', I('dma_start', out=convT_o[g][:, :, :, :], in_=halo[:]), r=[('halo', f) for f in range(NF)], dma='cvo')
                P.emit()
            if STOP == f's3g{g}':
                return nc
    return nc


def _rope_tables(pos):
    d = 64
    inv = (10000.0 ** (-np.arange(0, d, 2, dtype=np.float32) / d)).astype(np.float32)
    ang = pos.astype(np.float32)[None, :] * inv[:, None]
    cos = np.cos(ang).astype(np.float32); sin = np.sin(ang).astype(np.float32)
    C = np.concatenate([cos, cos, cos, cos], axis=0)
    S = np.concatenate([-sin, sin, -sin, sin], axis=0)
    return np.ascontiguousarray(C), np.ascontiguousarray(S)


_NC = None
_PREP_ONLY = False


def kernel(x_prompt, x_sample, c_prompt, c_sample, cache_fox_k, cache_fox_v, cache_fox_logf,
           cache_diff_k, cache_diff_v, state_ffn_conv, w_ada, b_ada, w_in, b_f, lambda_vecs,
           subln_g, w_o, ln1_g, ln1_b, w_up, conv_w, conv_b, w_down, ln2_g, ln2_b):
    global _NC
    f32 = np.float32
    A = lambda a: np.ascontiguousarray(np.asarray(a, dtype=f32))
    x_prompt = A(x_prompt); x_sample = A(x_sample)
    w_in0 = A(w_in)[0]
    fq = w_in0[:, 0:512]; fk = w_in0[:, 512:1024]; fv = w_in0[:, 1024:1536]; ff = w_in0[:, 1536:1544]
    dq = w_in0[:, 1544:2056]; dk = w_in0[:, 2056:2568]; dv = w_in0[:, 2568:3080]

    def swp(m):
        return m.reshape(D, 8, 2, 32)[:, :, ::-1, :].reshape(D, 512)
    w_in_ext = np.ascontiguousarray(np.concatenate([fq, fk, dq, swp(dq), dk, swp(dk), fv, dv, ff], axis=1))
    b_ada0 = A(b_ada)[0]
    common = {
        "w_ada": A(w_ada)[0], "b_adaT": np.ascontiguousarray(b_ada0.reshape(48, 128).T),
        "b_ada_g": np.ascontiguousarray(np.concatenate([b_ada0[2048:3072], b_ada0[5120:6144]])[None, :]),
        "w_in": w_in_ext, "b_f": np.ascontiguousarray(A(b_f)[0].reshape(8, 1)),
        "lam_v": A(lambda_vecs)[0].reshape(1, 256), "subln": A(subln_g)[0].reshape(128, 1),
        "w_o": A(w_o)[0], "lnp": np.ascontiguousarray(np.stack([A(ln1_g)[0], A(ln1_b)[0], A(ln2_g)[0], A(ln2_b)[0]])),
        "w_up": A(w_up)[0],
        "convw": np.ascontiguousarray(np.concatenate([A(conv_w)[0], A(conv_b)], axis=0).reshape(4, NF, 128).transpose(2, 1, 0)),
        "w_down": A(w_down)[0],
        "ident": np.eye(128, dtype=f32).astype(ml_dtypes.bfloat16),
        "tri": np.triu(np.ones((128, 128), dtype=f32)).astype(ml_dtypes.bfloat16),
    }
    Cp, Sp = _rope_tables(np.arange(T)); Cs, Ss = _rope_tables(P_LEN + np.arange(TS))
    common["ropeC_p"] = Cp; common["ropeS_p"] = Sp
    common["ropeC_s"] = np.ascontiguousarray(np.tile(Cs, (1, 4))); common["ropeS_s"] = np.ascontiguousarray(np.tile(Ss, (1, 4)))
    sel = np.zeros((6, 3, 128), dtype=f32)
    sel[0, 0, :] = 1; sel[1, 1, :] = 1
    for s in range(4):
        sel[2 + s, 2, s * 32:(s + 1) * 32] = 1
    common["sel"] = sel
    common["ones3"] = np.ones((3, T), dtype=f32).astype(ml_dtypes.bfloat16)
    cfk = A(cache_fox_k)[0]; cfv = A(cache_fox_v)[0]; clf = A(cache_fox_logf)[0]
    cdk = A(cache_diff_k)[0]; cdv = A(cache_diff_v)[0]; cst = A(state_ffn_conv)[0]
    c_prompt = A(c_prompt); c_sample = A(c_sample)
    in_maps = []
    for c in range(8):
        ps_ = slice(2 * c, 2 * c + 2); ss_ = slice(4 * c, 4 * c + 4)
        xp = x_prompt[ps_]
        xs = x_sample[ss_].reshape(128, D)
        call = np.concatenate([c_prompt[ps_], c_sample[ss_]], axis=0)
        m = dict(common)
        m["xT_p"] = np.ascontiguousarray(xp.reshape(2, T, 8, 128).transpose(0, 3, 2, 1))
        m["x_p"] = np.ascontiguousarray(xp)
        m["xT_s"] = np.ascontiguousarray(xs.reshape(128, 8, 128).transpose(2, 1, 0))
        m["x_s"] = np.ascontiguousarray(xs)
        m["cT"] = np.ascontiguousarray(call.reshape(6, 8, 128).transpose(2, 1, 0))
        m["fkT_s"] = np.ascontiguousarray(cfk[ss_].reshape(4, P_LEN, 512).transpose(0, 2, 1))
        m["fv_s"] = np.ascontiguousarray(cfv[ss_].reshape(4, P_LEN, 512))
        m["lfT_s"] = np.ascontiguousarray(clf[ss_].transpose(0, 2, 1))
        m["dkT_s"] = np.ascontiguousarray(cdk[ss_].reshape(4, P_LEN, 512).transpose(0, 2, 1))
        m["dv_s"] = np.ascontiguousarray(cdv[ss_].reshape(4, P_LEN, 512))
        m["convT_s"] = np.ascontiguousarray(cst[ss_].reshape(4, 2, NF, 128).transpose(3, 2, 0, 1))
        in_maps.append(m)
    if _PREP_ONLY:
        return in_maps
    if _NC is None:
        _NC = build()
    res = run_bass_kernel_spmd(_NC, in_maps, core_ids=list(range(8)))
    R = res.results
    yp = np.concatenate([r["y_p"] for r in R], axis=0)
    ys = np.concatenate([r["y_s"].reshape(4, TS, D) for r in R], axis=0)

    def catp(n0, n1):
        return [a for r in R for a in (r[n0], r[n1])]
    p_fk = np.stack([a.T.reshape(T, 8, 64) for a in catp("fkT_o0", "fkT_o1")])[None]
    p_fv = np.stack([a.reshape(T, 8, 64) for a in catp("fv_o0", "fv_o1")])[None]
    p_lf = np.stack([a.T for a in catp("lfT_o0", "lfT_o1")])[None]
    p_dk = np.stack([a.T.reshape(T, 8, 64) for a in catp("dkT_o0", "dkT_o1")])[None]
    p_dv = np.stack([a.reshape(T, 4, 128) for a in catp("dv_o0", "dv_o1")])[None]
    p_cv = np.stack([a.reshape(128, NF, 2).transpose(2, 1, 0).reshape(2, DFF) for a in catp("convT_o0", "convT_o1")])[None]
    s_fk = np.concatenate([r["sfkT_o"].T.reshape(4, TS, 8, 64) for r in R], axis=0)[None]
    s_fv = np.concatenate([r["sfv_o"].reshape(4, TS, 8, 64) for r in R], axis=0)[None]
    s_lf = np.concatenate([r["slfT_o"].T.reshape(4, TS, 8) for r in R], axis=0)[None]
    s_dk = np.concatenate([r["sdkT_o"].T.reshape(4, TS, 8, 64) for r in R], axis=0)[None]
    s_dv = np.concatenate([r["sdv_o"].reshape(4, TS, 4, 128) for r in R], axis=0)[None]
    s_cv = np.concatenate([r["sconvT_o"].transpose(2, 3, 1, 0).reshape(4, 2, DFF) for r in R], axis=0)[None]
    outs = (yp, ys, p_fk, p_fv, p_lf, p_dk, p_dv, p_cv, s_fk, s_fv, s_lf, s_dk, s_dv, s_cv)
    return tuple(np.ascontiguousarray(o, dtype=f32) for o in outs)
```

```python
import numpy as np
import os
import ml_dtypes
from contextlib import ExitStack
import concourse.bass as bass
import concourse.mybir as mybir
from concourse.bass_utils import run_bass_kernel_spmd

F32 = mybir.dt.float32
BF = mybir.dt.bfloat16
AF = mybir.ActivationFunctionType
ALU = mybir.AluOpType

T = 2048
D = 1024
DFF = 2816
NF = 22
P_LEN = 2048
TS = 32
ALPHA = 2.0 ** 0.25
LAM_INIT = 0.8 - 0.6
WK = 2112
ENG = ('pe', 'act', 'dve', 'pool', 'sp')


def I(name, *a, **k):
    return (name, a, k)


class Prog:
    G = None

    def __init__(s, nc, name):
        s.nc = nc; s.name = name; s.ops = []; s.lastw = {}; s.rd = {}; s.dma_cnt = {}; s.spd = []; s.pld = []

    def op(s, eng, fn, r=(), w=(), dma=None):
        i = len(s.ops); deps = set(); raw = set()
        if eng in ('sp', 'pool') and dma is not None and fn is not None:
            nd = 1
            for ap in (fn[2].get('out'), fn[2].get('in_')):
                try:
                    shp = list(ap.shape)
                    n_ = 1
                    for d_ in shp[:-1]:
                        n_ *= int(d_)
                    nd = max(nd, n_)
                except Exception:
                    nd = max(nd, 1024)
            tot = nd
            lst = s.spd if eng == 'sp' else s.pld
            for (j_, ndj) in reversed(lst):
                tot += ndj
                if tot > (2500 if eng == 'sp' else 6000):
                    deps.add(j_)
                    break
            lst.append((i, nd))
        for x in r:
            if x in s.lastw:
                deps.add(s.lastw[x]); raw.add(s.lastw[x])
        for x in w:
            if x in s.lastw:
                deps.add(s.lastw[x])
            for j in s.rd.get(x, ()):
                deps.add(j)
        for x in r:
            s.rd.setdefault(x, []).append(i)
        for x in w:
            s.lastw[x] = i; s.rd[x] = []
        deps.discard(i)
        o = dict(i=i, eng=eng, fn=fn, deps=deps, raw=raw, dma=dma, sig=False)
        if dma is not None:
            s.dma_cnt[dma] = s.dma_cnt.get(dma, 0) + 1
            o['dval'] = 16 * s.dma_cnt[dma]
        s.ops.append(o)
        return i

    def emit(s):
        nc = s.nc
        last = {}
        for o in s.ops:
            if o['dma'] is not None:
                last[o['dma']] = o['i']
        fin = dict(i=len(s.ops), eng='sp', fn=None, deps=set(last.values()), raw=set(), dma=None, sig=False)
        s.ops.append(fin)
        for o in s.ops:
            o['waits'] = []
            for j in sorted(o['deps']):
                d = s.ops[j]
                if d['dma'] is not None:
                    o['waits'].append(('dma', d['dma'], d['dval']))
                else:
                    if d['eng'] == o['eng'] and (d['eng'] == 'pe' or j not in o['raw']):
                        continue
                    d['sig'] = True
                    o['waits'].append(('eng', j))
        cnt = {e: 0 for e in ENG}
        for o in s.ops:
            if o['dma'] is None and o['sig']:
                cnt[o['eng']] += 1; o['cval'] = cnt[o['eng']]
        G = s.G
        keys = list(s.dma_cnt)
        assert len(keys) <= len(G['dsem']), (s.name, len(keys))
        kidx = {k: n for n, k in enumerate(keys)}
        esem = G['esem']; ebase = dict(G['ebase']); dbase = list(G['dbase'])
        with ExitStack() as es:
            block = es.enter_context(nc.Block())

            def body(engname):
                def f(e):
                    seen = {}
                    for o in s.ops:
                        if o['eng'] != engname:
                            continue
                        for wt in o['waits']:
                            if wt[0] == 'dma':
                                n = kidx[wt[1]]; sem = G['dsem'][n]; val = dbase[n] + wt[2]; key = ('d', n)
                            else:
                                d = s.ops[wt[1]]; sem = esem[d['eng']]; val = ebase[d['eng']] + d['cval']; key = ('e', d['eng'])
                            if seen.get(key, 0) >= val:
                                continue
                            seen[key] = val
                            e.wait_ge(sem, val)
                        if o['fn'] is None:
                            continue
                        nm, a_, k_ = o['fn']
                        inst = getattr(e, nm)(*a_, **k_)
                        if o['dma'] is not None:
                            inst.then_inc(G['dsem'][kidx[o['dma']]], 16)
                        elif o['sig']:
                            inst.then_inc(esem[engname], 1)
                return f
            block.tensor(body('pe')); block.scalar(body('act')); block.vector(body('dve'))
            block.gpsimd(body('pool')); block.sync(body('sp'))
        for e_ in ENG:
            G['ebase'][e_] += cnt[e_]
        for k, n in kidx.items():
            G['dbase'][n] += 16 * s.dma_cnt[k]


_uid = [0]


def _pn():
    _uid[0] += 1
    return f"u{_uid[0]}"


def build():
    nc = bass.Bass("TRN2", target_bir_lowering=False)
    STOP = os.environ.get('KSTOP', 'zz')
    DBG = bool(os.environ.get('KDBG'))
    LVL = int(os.environ.get('KLVL', '99'))
    L3 = int(os.environ.get('KL3', '99'))

    def din(name, shape, dt=F32):
        return nc.dram_tensor(name, list(shape), dt, kind="ExternalInput").ap()

    def dout(name, shape, dt=F32):
        return nc.dram_tensor(name, list(shape), dt, kind="ExternalOutput").ap()

    def dscr(name, shape, dt=BF):
        return nc.dram_tensor(name, list(shape), dt, kind=("ExternalOutput" if DBG else "Internal")).ap()

    xT_p = din("xT_p", [2, 128, 8, T]); x_p = din("x_p", [2, T, D])
    xT_s = din("xT_s", [128, 8, 128]); x_s = din("x_s", [128, D])
    cT = din("cT", [128, 8, 6])
    fkT_s = din("fkT_s", [4, 512, P_LEN]); fv_s = din("fv_s", [4, P_LEN, 512]); lfT_s = din("lfT_s", [4, 8, P_LEN])
    dkT_s = din("dkT_s", [4, 512, P_LEN]); dv_s = din("dv_s", [4, P_LEN, 512]); convT_s = din("convT_s", [128, NF, 4, 2])
    w_ada = din("w_ada", [D, 6 * D]); b_adaT = din("b_adaT", [128, 48]); b_ada_g = din("b_ada_g", [1, 2048])
    w_in = din("w_in", [D, 4104]); nb_f = din("b_f", [8, 1]); lam_v = din("lam_v", [1, 256]); subln = din("subln", [128, 1])
    w_o = din("w_o", [D, D]); lnp = din("lnp", [4, D]); w_up = din("w_up", [D, 2 * DFF]); convw = din("convw", [128, NF, 4])
    w_down = din("w_down", [DFF, D])
    ident_d = din("ident", [128, 128], BF); tri_d = din("tri", [128, 128], BF)
    ropeC_p = din("ropeC_p", [128, T]); ropeS_p = din("ropeS_p", [128, T])
    ropeC_s = din("ropeC_s", [128, 128]); ropeS_s = din("ropeS_s", [128, 128])
    sel_d = din("sel", [6, 3, 128])
    ones3_d = din("ones3", [3, T], BF)
    y_p = dout("y_p", [2, T, D]); y_s = dout("y_s", [128, D])
    fkT_o = [dout("fkT_o0", [512, T]), dout("fkT_o1", [512, T]), dout("sfkT_o", [512, 128])]
    fv_o = [dout("fv_o0", [T, 512]), dout("fv_o1", [T, 512]), dout("sfv_o", [128, 512])]
    lfT_o = [dout("lfT_o0", [8, T]), dout("lfT_o1", [8, T]), dout("slfT_o", [8, 128])]
    dkT_o = [dout("dkT_o0", [512, T]), dout("dkT_o1", [512, T]), dout("sdkT_o", [512, 128])]
    dv_o = [dout("dv_o0", [T, 512]), dout("dv_o1", [T, 512]), dout("sdv_o", [128, 512])]
    convT_o = [dout("convT_o0", [128, NF, 1, 2]), dout("convT_o1", [128, NF, 1, 2]), dout("sconvT_o", [128, NF, 4, 2])]
    wi4 = dscr("wi4", [8, 128, 8, 512]); wiF = dscr("wiF", [128, 8, 8]); wo4 = dscr("wo4", [128, 8, D]); wu4 = dscr("wu4", [NF, 128, 8, 256]); wd4 = dscr("wd4", [128, NF, D])
    TG = [T, T, 128]
    QF = [dscr(f"QF{g}", [512, TG[g]]) for g in range(3)]
    KF = [dscr(f"KF{g}", [512, TG[g]]) for g in range(3)]
    QD = [dscr(f"QD{g}", [512, TG[g]]) for g in range(3)]
    KD = [dscr(f"KD{g}", [512, TG[g]]) for g in range(3)]
    VF = [dscr(f"VF{g}", [TG[g], 512]) for g in range(3)]
    VD = [dscr(f"VD{g}", [TG[g], 512]) for g in range(3)]
    CS = [dscr("CS0", [8, 6, T]), dscr("CS1", [8, 6, T])] + [dscr(f"CSs{s}", [8, 6, WK]) for s in range(4)]
    OTOK = [dscr(f"OTOK{g}", [TG[g], D]) for g in range(3)]
    MG = dscr("MG", [6, 2048], F32)
    KaS = [dscr(f"KaS{i}", [512, P_LEN]) for i in range(4)]; KdS = [dscr(f"KdS{i}", [512, P_LEN]) for i in range(4)]
    VfS = [dscr(f"VfS{i}", [P_LEN, 512]) for i in range(4)]; VdS = [dscr(f"VdS{i}", [P_LEN, 512]) for i in range(4)]
    precast = [True]

    with ExitStack() as gs:
        def sb(name, shape, dt=F32):
            return gs.enter_context(nc.sbuf_tensor(name, list(shape), dt))
        Prog.G = dict(esem={e_: gs.enter_context(nc.semaphore(f"ge_{e_}")) for e_ in ENG},
                      dsem=[gs.enter_context(nc.semaphore(f"gd_{n_}")) for n_ in range(32)],
                      ebase={e_: 0 for e_ in ENG}, dbase=[0] * 32)
        ident = sb("ident_sb", [128, 128], BF); tri = sb("tri_sb", [128, 128], BF)
        modF = sb("modF", [128, 48, 6]); sc1p = sb("sc1p", [128, 8, 6]); sc2p = sb("sc2p", [128, 8, 6])
        lnbc = sb("lnbc", [128, 4, D]); neg_lam = sb("neg_lam", [128, 1]); wrs = sb("wrs", [128, 1])
        nbf = sb("nbf", [8, 1]); cw = sb("cw", [128, NF, 4])

        with ExitStack() as st:
            def sb(name, shape, dt=F32):
                _uid[0] += 1
                return st.enter_context(nc.sbuf_tensor(f"{name}_u{_uid[0]}", list(shape), dt))
            P = Prog(nc, "s0")
            cts = sb("cts", [128, 8, 6]); scb = sb("scb", [128, 8, 6], BF)
            modG = sb("modG", [6, 2048])
            wa = [sb(f"wa{i}", [128, 8, 1024], BF) for i in range(2)]
            badT = sb("badT", [128, 48]); bag = sb("bag", [6, 2048])
            lv = sb("lv", [128, 256]); lt = sb("lt", [128, 128]); ls = sb("ls", [128, 2]); le = sb("le", [128, 2])
            subl = sb("subl", [128, 1])
            psA = st.enter_context(nc.psum_tensor("psA_" + _pn(), [128, 512], F32))
            psG = [st.enter_context(nc.psum_tensor(f"psG{i}_" + _pn(), [128, 512], F32)) for i in range(4)]
            for u_ in range(8):
                P.op('pool', I('dma_start', out=wi4[u_], in_=w_in[:, u_ * 512:(u_ + 1) * 512].rearrange("(k p) c -> p k c", p=128)), dma=('wc', u_))
            P.op('pool', I('dma_start', out=wiF[:, :, :], in_=w_in[:, 4096:4104].rearrange("(k p) c -> p k c", p=128)), dma=('wc', 8))
            P.op('pool', I('dma_start', out=wo4[:, :, :], in_=w_o[:, :].rearrange("(k p) c -> p k c", p=128)), dma=('wc', 9))
            P.op('sp', I('dma_start', out=cts[:], in_=cT[:, :, :]), w=['cts'], dma='l0')
            P.op('sp', I('dma_start', out=badT[:], in_=b_adaT[:, :]), w=['badT'], dma='l1')
            P.op('sp', I('dma_start', out=bag[:], in_=b_ada_g[0, :].partition_broadcast(6)), w=['bag'], dma='l2')
            P.op('sp', I('dma_start', out=ident[:], in_=ident_d[:, :]), w=['ident'], dma='l4')
            P.op('sp', I('dma_start', out=tri[:], in_=tri_d[:, :]), w=['tri'], dma='l5')
            P.op('sp', I('dma_start', out=nbf[:], in_=nb_f[:, :]), w=['nbf'], dma='l6')
            P.op('dve', I('tensor_scalar', out=nbf[:], in0=nbf[:], scalar1=-1.0, scalar2=None, op0=ALU.mult), r=['nbf'], w=['nbf'])
            P.op('sp', I('dma_start', out=cw[:], in_=convw[:, :, :]), w=['cw'], dma='l7')
            P.op('sp', I('dma_start', out=subl[:], in_=subln[:, :]), w=['subl'], dma='l8')
            P.op('sp', I('dma_start', out=lv[:], in_=lam_v[0, :].partition_broadcast(128)), w=['lv'], dma='l9')
            for j in range(4):
                P.op('sp', I('dma_start', out=lnbc[:, j, :], in_=lnp[j, :].partition_broadcast(128)), w=['lnbc'], dma='l10')
            P.op('act', I('activation', out=scb[:], in_=cts[:], func=AF.Silu), r=['cts'], w=['scb'])
            for ch in range(6):
                P.op('pool', I('dma_start', out=wa[ch % 2][:], in_=w_ada[:, ch * 1024:(ch + 1) * 1024].rearrange("(k p) c -> p k c", p=128)),
                     w=[('wa', ch % 2)], dma=('wa', ch % 2))
                for jj in range(8):
                    j = ch * 8 + jj
                    for k in range(8):
                        P.op('pe', I('matmul', psA[:, j * 6:(j + 1) * 6], lhsT=wa[ch % 2][:, k, jj * 128:(jj + 1) * 128], rhs=scb[:, k, :], start=(k == 0), stop=(k == 7)),
                             r=[('wa', ch % 2), 'scb'], w=['psA'])
                if ch in (2, 5):
                    gi = 0 if ch == 2 else 1
                    for half in range(2):
                        for k in range(8):
                            P.op('pe', I('matmul', psG[gi * 2 + half][0:6, :], lhsT=scb[:, k, :], rhs=wa[ch % 2][:, k, half * 512:(half + 1) * 512], start=(k == 0), stop=(k == 7)),
                                 r=[('wa', ch % 2), 'scb'], w=[('psG', gi * 2 + half)])
                        P.op('dve', I('tensor_tensor', out=modG[:, gi * 1024 + half * 512: gi * 1024 + (half + 1) * 512], in0=psG[gi * 2 + half][0:6, :],
                                                                               in1=bag[:, gi * 1024 + half * 512: gi * 1024 + (half + 1) * 512], op=ALU.add),
                             r=[('psG', gi * 2 + half), 'bag'], w=['modG'])
            for j in range(6):
                P.op('dve', I('tensor_tensor', out=modF[:, :, j], in0=psA[:, 0:288].rearrange("p (c j) -> p c j", j=6)[:, :, j], in1=badT[:, :], op=ALU.add),
                     r=['psA', 'badT'], w=['modF'])
            P.op('dve', I('tensor_scalar', out=sc1p[:], in0=modF[:, 8:16, :], scalar1=1.0, scalar2=None, op0=ALU.add), r=['modF'], w=['sc1p'])
            P.op('dve', I('tensor_scalar', out=sc2p[:], in0=modF[:, 32:40, :], scalar1=1.0, scalar2=None, op0=ALU.add), r=['modF'], w=['sc2p'])
            P.op('dve', I('tensor_tensor', out=lt[:, 0:64], in0=lv[:, 0:64], in1=lv[:, 64:128], op=ALU.mult), r=['lv'], w=['lt'])
            P.op('dve', I('tensor_tensor', out=lt[:, 64:128], in0=lv[:, 128:192], in1=lv[:, 192:256], op=ALU.mult), r=['lv'], w=['lt'])
            P.op('dve', I('reduce_sum', out=ls[:, :], in_=lt[:, :].rearrange("p (a b) -> p a b", a=2), axis=mybir.AxisListType.X), r=['lt'], w=['ls'])
            P.op('act', I('activation', out=le[:], in_=ls[:], func=AF.Exp), r=['ls'], w=['le'])
            P.op('dve', I('tensor_tensor', out=neg_lam[:], in0=le[:, 1:2], in1=le[:, 0:1], op=ALU.subtract), r=['le'], w=['nl'])
            P.op('dve', I('tensor_scalar', out=neg_lam[:], in0=neg_lam[:], scalar1=-LAM_INIT, scalar2=None, op0=ALU.add), r=['nl'], w=['nl'])
            P.op('dve', I('tensor_scalar', out=wrs[:], in0=subl[:], scalar1=1.0 - LAM_INIT, scalar2=None, op0=ALU.mult), r=['subl'], w=['wrs'])
            P.op('sp', I('dma_start', out=MG[:, :], in_=modG[:]), r=['modG'], dma='mg')
            if DBG:
                dbgF = dout("dbg_modF", [128, 288]); dbgG = dout("dbg_modG", [6, 2048]); dbgM = dout("dbg_misc", [2, 128, 1])
                P.op('sp', I('dma_start', out=dbgF[:, :], in_=modF[:].rearrange("p a b -> p (a b)")), r=['modF'], dma='dbg')
                P.op('sp', I('dma_start', out=dbgG[:, :], in_=modG[:]), r=['modG'], dma='dbg')
                P.op('sp', I('dma_start', out=dbgM[0], in_=neg_lam[:]), r=['nl'], dma='dbg')
                P.op('sp', I('dma_start', out=dbgM[1], in_=wrs[:]), r=['wrs'], dma='dbg')
            P.emit()
        if STOP == 's0':
            return nc

        for g in [int(c_) for c_ in os.environ.get('KG', '012')]:
            Tg = TG[g]; N = min(512, Tg); nblk = Tg // N; ntile = Tg // 128
            with ExitStack() as st:
                def sb(name, shape, dt=F32):
                    _uid[0] += 1
                    return st.enter_context(nc.sbuf_tensor(f"{name}_u{_uid[0]}", list(shape), dt))
                P = Prog(nc, f"s1g{g}")
                uT = sb("uT", [128, 8, Tg], BF)
                xs = [sb(f"xs{i}", [128, 4, N]) for i in range(2)]
                wr = [sb(f"wr{i}", [128, 8, 512], BF) for i in range(4)]
                wff = sb("wff", [128, 8, 8], BF)
                rC = sb("rC", [128, Tg]); rS = sb("rS", [128, Tg])
                t1 = [sb(f"t1_{i}", [128, N]) for i in range(2)]; t2 = [sb(f"t2_{i}", [128, N]) for i in range(2)]
                r32 = [sb(f"r32_{i}", [128, N]) for i in range(2)]
                stg = [sb(f"stg{i}", [128, N], BF) for i in range(2)]
                v32 = [sb(f"v32_{i}", [128, 512]) for i in range(2)]; vb = [sb(f"vb{i}", [128, 512], BF) for i in range(2)]
                lf = sb("lf", [8, Tg]); lfp = sb("lfp", [8, P_LEN if g == 2 else 8]); cum = sb("cum", [8, WK]); r1 = sb("r1", [8, WK]); ones8 = sb("ones8", [8, WK])
                spl = sb("spl", [8, 6, WK], BF)
                ps = [st.enter_context(nc.psum_tensor(f"ps{i}_" + _pn(), [128, 512], F32)) for i in range(8)]
                pctr = [0]

                def nbank():
                    pctr[0] += 1
                    return pctr[0] % 8
                sctr = [0]
                xTsrc = xT_p[g] if g < 2 else xT_s
                bg = []
                if g == int(os.environ.get('KG', '012')[0]):
                    for f_ in range(NF):
                        bg.append(I('dma_start', out=wu4[f_][:, :, 0:128], in_=w_up[:, f_ * 128:(f_ + 1) * 128].rearrange("(k p) c -> p k c", p=128)))
                        bg.append(I('dma_start', out=wu4[f_][:, :, 128:256], in_=w_up[:, DFF + f_ * 128:DFF + (f_ + 1) * 128].rearrange("(k p) c -> p k c", p=128)))
                    for f0 in (0, 11):
                        bg.append(I('dma_start', out=wd4[:, f0:f0 + 11, :], in_=w_down[f0 * 128:(f0 + 11) * 128, :].rearrange("(k p) c -> p k c", p=128)))
                bgn = [0]

                def bgpop():
                    if bg:
                        P.op('pool', bg.pop(0), dma=('bgc', bgn[0] % 8)); bgn[0] += 1
                P.op('pool', I('memset', ones8[:], 1.0), w=['ones8'])
                NWS = 4
                wslot = {}
                wissued = [0]

                def issue_upto(u_):
                    while wissued[0] <= min(u_, 7):
                        uu = wissued[0]; i = uu % NWS
                        P.op('sp', I('dma_start', out=wr[i][:, :, :], in_=wi4[uu]), w=[('wr', i)], dma=('wr', i))
                        wslot[uu] = i; wissued[0] += 1
                if LVL >= 2:
                    issue_upto(1)

                for blk in range(nblk):
                    for hf in range(2):
                        xb = xs[hf]
                        P.op('sp', I('dma_start', out=xb[:], in_=xTsrc[:, hf * 4:(hf + 1) * 4, blk * N:(blk + 1) * N]), w=[('xs', hf)], dma=('xs', hf))
                        for k4 in range(4):
                            k = hf * 4 + k4
                            if g < 2:
                                if k4 % 2 == 0:
                                    P.op('pool', I('tensor_scalar', out=uT[:, k, blk * N:(blk + 1) * N], in0=xb[:, k4, :], scalar1=sc1p[:, k, g:g + 1], scalar2=modF[:, k, g:g + 1], op0=ALU.mult, op1=ALU.add),
                                         r=[('xs', hf)], w=[('uT', blk)])
                                else:
                                    P.op('act', I('activation', out=uT[:, k, blk * N:(blk + 1) * N], in_=xb[:, k4, :], func=AF.Identity, scale=sc1p[:, k, g:g + 1], bias=modF[:, k, g:g + 1]),
                                         r=[('xs', hf)], w=[('uT', blk)])
                            else:
                                for s in range(4):
                                    P.op('pool', I('tensor_scalar', out=uT[:, k, s * 32:(s + 1) * 32], in0=xb[:, k4, s * 32:(s + 1) * 32], scalar1=sc1p[:, k, 2 + s:3 + s], scalar2=modF[:, k, 2 + s:3 + s], op0=ALU.mult, op1=ALU.add),
                                         r=[('xs', hf)], w=[('uT', blk)])
                P.op('sp', I('dma_start', out=rC[:], in_=(ropeC_p if g < 2 else ropeC_s)[:, :]), w=['rC'], dma='rc')
                P.op('sp', I('dma_start', out=rS[:], in_=(ropeS_p if g < 2 else ropeS_s)[:, :]), w=['rS'], dma='rs')
                uTall = [('uT', b) for b in range(nblk)]

                def loadw(c0, ncols=512):
                    uu = c0 // 512
                    issue_upto(uu)
                    return wslot[uu]

                def fm_group(slot, co, blk):
                    b = nbank()
                    for k in range(8):
                        P.op('pe', I('matmul', ps[b][:, 0:N], lhsT=wr[slot][:, k, co:co + 128], rhs=uT[:, k, blk * N:(blk + 1) * N], start=(k == 0), stop=(k == 7)),
                             r=[('wr', slot), ('uT', blk)], w=[('ps', b)])
                    return b
                octr = [0]
                for which, c0 in (((('q', 0), ('k', 512)) if not os.environ.get('KQ') else (('q', 0),)) if LVL >= 2 else ()):
                    slot = loadw(c0)
                    issue_upto(c0 // 512 + 2)
                    for blk in range(nblk):
                        for c in range(4):
                            b = fm_group(slot, c * 128, blk)
                            i = octr[0] % 2; octr[0] += 1
                            if which == 'q':
                                P.op('act', I('activation', out=stg[i][:], in_=ps[b][:, 0:N], func=AF.Identity), r=[('ps', b)], w=[('stg', i)])
                            else:
                                P.op('act', I('activation', out=r32[i][:], in_=ps[b][:, 0:N], func=AF.Identity), r=[('ps', b)], w=[('r32', i)])
                                P.op('pool', I('tensor_copy', out=stg[i][:], in_=r32[i][:]), r=[('r32', i)], w=[('stg', i)])
                                bgpop()
                                P.op('sp', I('dma_start', out=fkT_o[g][c * 128:(c + 1) * 128, blk * N:(blk + 1) * N], in_=r32[i][:]), r=[('r32', i)], dma=('r32o', i))
                            dst = (QF if which == 'q' else KF)[g]
                            P.op('sp', I('dma_start', out=dst[c * 128:(c + 1) * 128, blk * N:(blk + 1) * N], in_=stg[i][:]), r=[('stg', i)], dma=('stgo', i))
                for which, c0 in ((('q', 1024), ('k', 2048)) if LVL >= 3 else ()):
                    sa = loadw(c0); sbw = loadw(c0 + 512)
                    issue_upto(c0 // 512 + 3)
                    for c in range(4):
                        for blk in range(nblk):
                            ba = fm_group(sa, c * 128, blk); bb = fm_group(sbw, c * 128, blk)
                            i = octr[0] % 2; octr[0] += 1
                            P.op('dve', I('tensor_tensor', out=t1[i][:], in0=ps[ba][:, 0:N], in1=rC[:, blk * N:(blk + 1) * N], op=ALU.mult), r=[('ps', ba), 'rC'], w=[('t1', i)])
                            P.op('dve', I('tensor_tensor', out=t2[i][:], in0=ps[bb][:, 0:N], in1=rS[:, blk * N:(blk + 1) * N], op=ALU.mult), r=[('ps', bb), 'rS'], w=[('t2', i)])
                            P.op('pool', I('tensor_tensor', out=r32[i][:], in0=t1[i][:], in1=t2[i][:], op=ALU.add), r=[('t1', i), ('t2', i)], w=[('r32', i)])
                            bgpop()
                            P.op('act', I('activation', out=stg[i][:], in_=r32[i][:], func=AF.Identity), r=[('r32', i)], w=[('stg', i)])
                            dst = (QD if which == 'q' else KD)[g]
                            P.op('sp', I('dma_start', out=dst[c * 128:(c + 1) * 128, blk * N:(blk + 1) * N], in_=stg[i][:]), r=[('stg', i)], dma=('stgo', i))
                            if which == 'k':
                                P.op('sp', I('dma_start', out=dkT_o[g][c * 128:(c + 1) * 128, blk * N:(blk + 1) * N], in_=r32[i][:]), r=[('r32', i)], dma=('r32o', i))
                for c0, vo, vs in (((3072, fv_o[g], VF[g]), (3584, dv_o[g], VD[g])) if LVL >= 4 else ()):
                    slot = loadw(c0)
                    issue_upto(c0 // 512 + 1)
                    for tt in range(ntile):
                        b = nbank()
                        for k in range(8):
                            P.op('pe', I('matmul', ps[b][:, :], lhsT=uT[:, k, tt * 128:(tt + 1) * 128], rhs=wr[slot][:, k, :], start=(k == 0), stop=(k == 7)),
                                 r=[('wr', slot)] + uTall, w=[('ps', b)])
                        i = octr[0] % 2; octr[0] += 1
                        P.op('act', I('activation', out=v32[i][:], in_=ps[b][:, :], func=AF.Identity), r=[('ps', b)], w=[('v32', i)])
                        P.op('pool', I('tensor_copy', out=vb[i][:], in_=v32[i][:]), r=[('v32', i)], w=[('vb', i)])
                        bgpop()
                        P.op('sp', I('dma_start', out=vo[tt * 128:(tt + 1) * 128, :], in_=v32[i][:]), r=[('v32', i)], dma=('v32o', i))
                        P.op('sp', I('dma_start', out=vs[tt * 128:(tt + 1) * 128, :], in_=vb[i][:]), r=[('vb', i)], dma=('vbo', i))
                P.op('sp', I('dma_start', out=wff[:], in_=wiF[:, :, :]), w=['wff'], dma='wff')
                for blk in (range(nblk) if LVL >= 5 else ()):
                    b = nbank()
                    for k in range(8):
                        P.op('pe', I('matmul', ps[b][0:8, 0:N], lhsT=wff[:, k, :], rhs=uT[:, k, blk * N:(blk + 1) * N], start=(k == 0), stop=(k == 7)),
                             r=['wff', ('uT', blk)], w=[('ps', b)])
                    P.op('act', I('activation', out=lf[:, blk * N:(blk + 1) * N], in_=ps[b][0:8, 0:N], func=AF.Exp, bias=nbf[:, 0:1], scale=-1.0), r=[('ps', b)], w=['lf'])
                P.op('act', I('activation', out=lf[:], in_=lf[:], func=AF.Ln, bias=1.0, scale=1.0), r=['lf'], w=['lf'])
                P.op('dve', I('tensor_scalar', out=lf[:], in0=lf[:], scalar1=-1.0, scalar2=None, op0=ALU.mult), r=['lf'], w=['lf'])
                P.op('sp', I('dma_start', out=lfT_o[g][:, :], in_=lf[:]), r=['lf'], dma='lfo')

                def splits(width, csdst):
                    P.op('dve', I('tensor_scalar', out=r1[:, 0:width], in0=cum[:, 0:width], scalar1=8.0, scalar2=None, op0=ALU.mult), r=['cum'], w=['r1'])
                    for j in range(3):
                        P.op('dve', I('tensor_copy', out=spl[:, j, 0:width], in_=r1[:, 0:width]), r=['r1'], w=['spl'])
                        if j < 2:
                            P.op('dve', I('tensor_tensor', out=r1[:, 0:width], in0=r1[:, 0:width], in1=spl[:, j, 0:width], op=ALU.subtract), r=['r1', 'spl'], w=['r1'])
                    P.op('dve', I('tensor_scalar', out=spl[:, 3:6, 0:width], in0=spl[:, 0:3, 0:width], scalar1=-1.0, scalar2=None, op0=ALU.mult), r=['spl'], w=['spl'])
                    P.op('sp', I('dma_start', out=csdst[:, :, 0:width], in_=spl[:, :, 0:width]), r=['spl'], dma='cso')
                if LVL < 6:
                    pass
                elif g < 2:
                    P.op('dve', I('tensor_tensor_scan', out=cum[:, 0:T], data0=ones8[:, 0:T], data1=lf[:, :], initial=0.0, op0=ALU.mult, op1=ALU.add), r=['lf', 'ones8'], w=['cum'])
                    splits(T, CS[g])
                else:
                    for s in range(4):
                        P.op('sp', I('dma_start', out=lfp[:], in_=lfT_s[s]), w=['lfp'], dma='lfp')
                        P.op('dve', I('tensor_tensor_scan', out=cum[:, 0:P_LEN], data0=ones8[:, 0:P_LEN], data1=lfp[:, :], initial=0.0, op0=ALU.mult, op1=ALU.add), r=['lfp', 'ones8', 'spl'], w=['cum'])
                        P.op('dve', I('tensor_tensor_scan', out=cum[:, P_LEN:P_LEN + 32], data0=ones8[:, 0:32], data1=lf[:, s * 32:(s + 1) * 32], initial=cum[:, P_LEN - 1:P_LEN], op0=ALU.mult, op1=ALU.add), r=['lf', 'cum'], w=['cum'])
                        splits(P_LEN + 32, CS[2 + s])
                while bg:
                    bgpop()
                P.emit()
            if STOP == f's1g{g}':
                return nc

            with ExitStack() as st:
                def sb(name, shape, dt=F32):
                    _uid[0] += 1
                    return st.enter_context(nc.sbuf_tensor(f"{name}_u{_uid[0]}", list(shape), dt))
                P = Prog(nc, f"s2g{g}")
                if g < 2:
                    Qa = [sb(f"Qa{i}", [128, Tg], BF) for i in range(2)]; Ka = [sb(f"Ka{i}", [128, WK], BF) for i in range(2)]
                    Qd = [[sb(f"Qd{i}_{j}", [128, Tg], BF) for j in range(2)] for i in range(2)]; Kd = [sb(f"Kd{i}", [128, WK], BF) for i in range(2)]
                    Vf = [sb(f"Vf{i}", [128, 17, 66], BF) for i in range(2)]; Vd = [sb(f"Vd{i}", [128, 17, 130], BF) for i in range(2)]
                else:
                    KaA = [sb(f"KaA{i}", [128, 8, WK], BF) for i in range(2)]; KdA = sb("KdA", [128, 4, WK], BF)
                    QaA = [sb(f"QaA{i}", [128, 8, 32], BF) for i in range(2)]; QzA = [sb(f"QzA{i}", [128, 4, 32], BF) for i in range(2)]
                    Vst = [sb(f"Vst{i}", [128, 16, 512], BF) for i in range(2)]
                    VfA = sb("VfA", [128, 17, 8, 66], BF); VdA = sb("VdA", [128, 17, 4, 130], BF)
                NSB = 4; NPT = 4
                PT = [sb(f"PT{i}", [128, 512], BF) for i in range(NPT)]
                OT = sb("OT", [128, 16 if g < 2 else 4, D], BF)
                rec = [sb(f"rec{i}", [128, 4]) for i in range(2)]; nl = [sb(f"nl{i}", [128, 4]) for i in range(2)]
                a32 = [sb(f"a32_{i}", [128, 4, 128]) for i in range(2)]; d32 = [sb(f"d32_{i}", [128, 4, 128]) for i in range(2)]
                mhalf = sb("mhalf", [128, 4])
                P.op('pool', I('memset', mhalf[:], -0.5), w=['mhalf'])
                junk = sb("junk", [128, 128]); ss = [sb(f"ss{i}", [128, 4]) for i in range(2)]; rstd = [sb(f"rstd{i}", [128, 4]) for i in range(2)]
                Sb = [st.enter_context(nc.psum_tensor(f"Sb{i}_" + _pn(), [128, 512], F32)) for i in range(NSB)]
                Ob = [st.enter_context(nc.psum_tensor(f"Ob{i}_" + _pn(), [128, 512], F32)) for i in range(4)]
                if g < 2:
                    for i in range(2):
                        P.op('pool', I('memset', Qa[i][64:128, :], 0.0), w=[('Qa', i)])
                        P.op('sp', I('dma_start', out=Qa[i][67:70, 0:min(Tg, T)], in_=ones3_d[:, 0:min(Tg, T)]), w=[('Qa', i)], dma=('Qa', i))
                        P.op('pool', I('memset', Ka[i][64:128, :], 1.0), w=[('Ka', i)])
                        P.op('pool', I('memset', Qd[i][0][64:128, :], 0.0), w=[('Qd', i)])
                        P.op('pool', I('memset', Qd[i][1][0:64, :], 0.0), w=[('Qd', i)])
                        P.op('pool', I('memset', Vf[i][:, :, 64:66], 1.0), w=[('Vf', i)])
                        P.op('pool', I('memset', Vd[i][:, :, 128:130], 1.0), w=[('Vd', i)])
                else:
                    for i in range(2):
                        P.op('pool', I('memset', QaA[i][64:128, :, :], 0.0), w=[('Qa', i)])
                        P.op('sp', I('dma_start', out=QaA[i][67:70, :, :], in_=ones3_d[:, 0:256].rearrange("j (h t) -> j h t", h=8)), w=[('Qa', i)], dma=('Qa', i))
                        P.op('pool', I('memset', KaA[i][64:128, :, :], 1.0), w=[('Ka', i)])
                    P.op('pool', I('memset', QzA[0][64:128, :, :], 0.0), w=[('Qd', 0)])
                    P.op('pool', I('memset', QzA[1][0:64, :, :], 0.0), w=[('Qd', 0)])
                    P.op('pool', I('memset', VfA[:, :, :, 64:66], 1.0), w=[('Vf', 0)])
                    P.op('pool', I('memset', VdA[:, :, :, 128:130], 1.0), w=[('Vd', 0)])
                sctr = [0]; pctr = [0]; uctr = [0]
                bg2 = []
                glist = [int(c_) for c_ in os.environ.get('KG', '012')]
                if g < 2 and 2 in glist:
                    mine = [0, 1] if (g == 0 and 1 in glist) else ([2, 3] if g == 1 and 0 in glist else [0, 1, 2, 3])
                    for s_ in mine:
                        bg2.append(I('dma_start', out=KaS[s_][:, :], in_=fkT_s[s_]))
                        bg2.append(I('dma_start', out=VfS[s_][:, :], in_=fv_s[s_]))
                        bg2.append(I('dma_start', out=KdS[s_][:, :], in_=dkT_s[s_]))
                        bg2.append(I('dma_start', out=VdS[s_][:, :], in_=dv_s[s_]))
                bg2n = [0]; bg2c = [0]

                def bg2pop(force=False):
                    bg2c[0] += 1
                    if bg2 and (force or bg2c[0] % 12 == 0):
                        P.op('pool', bg2.pop(0), dma=('bgc', bg2n[0] % 8)); bg2n[0] += 1
                nseq = 1 if g < 2 else 4
                Lq = Tg if g < 2 else 32
                npast = 0 if g < 2 else 16
                for s in range(nseq):
                    cs = CS[g] if g < 2 else CS[2 + s]
                    qc0 = 0 if g < 2 else s * 32
                    qpos0 = 0 if g < 2 else P_LEN
                    nqb = Lq // 512 if g < 2 else 1
                    QB = 512 if g < 2 else 32

                    def attend(kind, h, b, ov=None):
                        subs = (0,) if kind == 'f' else (0, 1)
                        if ov is None:
                            Kt = Ka[b] if kind == 'f' else Kd[b]
                            Qts = [Qa[b]] if kind == 'f' else Qd[b]
                            Vt = Vf[b] if kind == 'f' else Vd[b]
                            kr = ('Ka', b) if kind == 'f' else ('Kd', b)
                            qr = ('Qa', b) if kind == 'f' else ('Qd', b)
                            vr = ('Vf', b) if kind == 'f' else ('Vd', b)
                        else:
                            Kt, Qts, Vt, kr, qr, vr = ov
                        KR = 128
                        VW = 65 if kind == 'f' else 129
                        for qb in range(nqb):
                            u = uctr[0] % 2; uctr[0] += 1
                            for sub in subs:
                                pb0 = 0
                                Qt = Qts[sub]
                                if g < 2:
                                    kts = list(range(0, 4 * qb + 4))
                                else:
                                    kts = list(range(17))
                                if kind == 'f':
                                    obk = [Ob[u * 2]] * 4 if g < 2 else [Ob[u * 2]]
                                    obn = [u * 2] * 4
                                    ocol = [qt * 65 for qt in range(4)]
                                else:
                                    obn = [sub * 2 + qt // 2 for qt in range(4)]
                                    obk = [Ob[n] for n in obn]
                                    ocol = [(qt % 2) * 129 for qt in range(4)]
                                started = set()
                                if g == 2:
                                    sbk = sctr[0] % NSB; sctr[0] += 1
                                    pt = pctr[0] % NPT; pctr[0] += 1
                                    for kt in range(16):
                                        P.op('pe', I('matmul', Sb[sbk][:, kt * 32:(kt + 1) * 32], lhsT=Kt[pb0:pb0 + KR, kt * 128:(kt + 1) * 128], rhs=Qt[pb0:pb0 + KR, 0:32], start=True, stop=True),
                                             r=[kr, qr], w=[('S', sbk)])
                                    P.op('act', I('activation', out=PT[pt][:, :], in_=Sb[sbk][:, :], func=AF.Exp, scale=0.125), r=[('S', sbk)], w=[('PT', pt)])
                                    for kt in range(16):
                                        P.op('pe', I('matmul', obk[0][0:32, ocol[0]:ocol[0] + VW], lhsT=PT[pt][:, kt * 32:(kt + 1) * 32], rhs=Vt[:, kt, 0:VW], start=(kt == 0), stop=False, skip_group_check=True),
                                             r=[('PT', pt), vr], w=[('O', obn[0])])
                                    sbk = sctr[0] % NSB; sctr[0] += 1
                                    pt = pctr[0] % NPT; pctr[0] += 1
                                    P.op('pe', I('matmul', Sb[sbk][0:32, 0:32], lhsT=Kt[pb0:pb0 + KR, P_LEN:P_LEN + 32], rhs=Qt[pb0:pb0 + KR, 0:32], start=True, stop=True),
                                         r=[kr, qr], w=[('S', sbk)])
                                    P.op('act', I('activation', out=PT[pt][0:32, 0:32], in_=Sb[sbk][0:32, 0:32], func=AF.Exp, scale=0.125), r=[('S', sbk)], w=[('PT', pt)])
                                    if kind == 'f':
                                        P.op('pool', I('tensor_tensor', out=PT[pt][0:32, 0:32], in0=PT[pt][0:32, 0:32], in1=tri[0:32, 0:32], op=ALU.mult), r=[('PT', pt)], w=[('PT', pt)])
                                    P.op('pe', I('matmul', obk[0][0:32, ocol[0]:ocol[0] + VW], lhsT=PT[pt][0:32, 0:32], rhs=Vt[0:32, 16, 0:VW], start=False, stop=True, skip_group_check=True),
                                         r=[('PT', pt), vr], w=[('O', obn[0])])
                                else:
                                    recs = []

                                    def front(kt):
                                        j = kt - 4 * qb
                                        qoff = max(j, 0) * 128; nq = 512 - qoff
                                        sbk = sctr[0] % NSB; sctr[0] += 1
                                        pt = pctr[0] % NPT; pctr[0] += 1
                                        P.op('pe', I('matmul', Sb[sbk][:, 0:nq], lhsT=Kt[pb0:pb0 + KR, kt * 128:(kt + 1) * 128], rhs=Qt[pb0:pb0 + KR, qb * 512 + qoff:(qb + 1) * 512], start=True, stop=True),
                                             r=[kr, qr], w=[('S', sbk)])
                                        P.op('act', I('activation', out=PT[pt][:, 0:nq], in_=Sb[sbk][:, 0:nq], func=AF.Exp, scale=0.125), r=[('S', sbk)], w=[('PT', pt)])
                                        if j >= 0:
                                            if kind == 'f':
                                                P.op('pool', I('tensor_tensor', out=PT[pt][:, 0:128], in0=PT[pt][:, 0:128], in1=tri[:, :], op=ALU.mult), r=[('PT', pt)], w=[('PT', pt)])
                                            else:
                                                P.op('pool', I('memset', PT[pt][64:128, 0:64], 0.0), r=[('PT', pt)], w=[('PT', pt)])
                                            bg2pop()
                                        return (kt, j, pt)

                                    def back(rc_):
                                        kt, j, pt = rc_
                                        for qt in range(max(j, 0), 4):
                                            cc = (qt - max(j, 0)) * 128
                                            first = obn[qt] not in started
                                            started.add(obn[qt])
                                            P.op('pe', I('matmul', obk[qt][:, ocol[qt]:ocol[qt] + VW], lhsT=PT[pt][:, cc:cc + 128], rhs=Vt[:, kt, 0:VW], start=first, stop=(kt == 4 * qb + qt), skip_group_check=True),
                                                 r=[('PT', pt), vr], w=[('O', obn[qt])])
                                    LA = 3
                                    for idx, kt in enumerate(kts):
                                        recs.append(front(kt))
                                        if idx >= LA:
                                            back(recs[idx - LA])
                                    for rc_ in recs[max(0, len(kts) - LA):]:
                                        back(rc_)
                                nqt = 4 if g < 2 else 1
                                rows = 128 if g < 2 else 32
                                for qt in range(nqt):
                                    P.op('dve', I('reciprocal', out=rec[u][0:rows, qt:qt + 1], in_=obk[qt][0:rows, ocol[qt] + VW - 1:ocol[qt] + VW]), r=[('O', obn[qt])], w=[('rec', u)])
                                tile0 = qb * 4 if g < 2 else s
                                if kind == 'f':
                                    for qt in range(nqt):
                                        P.op('dve', I('tensor_scalar', out=OT[0:rows, tile0 + qt, h * 64:(h + 1) * 64], in0=obk[qt][0:rows, ocol[qt]:ocol[qt] + 64], scalar1=rec[u][0:rows, qt:qt + 1], scalar2=None, op0=ALU.mult),
                                             r=[('O', obn[qt]), ('rec', u)], w=[('OT', tile0 + qt, kind, h)])
                                elif sub == 0:
                                    for qt in range(nqt):
                                        P.op('dve', I('tensor_scalar', out=a32[u][0:rows, qt, :], in0=obk[qt][0:rows, ocol[qt]:ocol[qt] + 128], scalar1=rec[u][0:rows, qt:qt + 1], scalar2=None, op0=ALU.mult),
                                             r=[('O', obn[qt]), ('rec', u)], w=[('a32', u)])
                                else:
                                    P.op('dve', I('tensor_scalar', out=nl[u][0:rows, 0:nqt], in0=rec[u][0:rows, 0:nqt], scalar1=neg_lam[0:rows, 0:1], scalar2=None, op0=ALU.mult), r=[('rec', u)], w=[('nl', u)])
                                    for qt in range(nqt):
                                        P.op('dve', I('scalar_tensor_tensor', out=d32[u][0:rows, qt, :], in0=obk[qt][0:rows, ocol[qt]:ocol[qt] + 128], scalar=nl[u][0:rows, qt:qt + 1], in1=a32[u][0:rows, qt, :], op0=ALU.mult, op1=ALU.add),
                                             r=[('O', obn[qt]), ('nl', u), ('a32', u)], w=[('d32', u)])
                                        P.op('dve', I('scalar_tensor_tensor', out=junk[0:rows, :], in0=d32[u][0:rows, qt, :], scalar=1.0 / 128.0, in1=d32[u][0:rows, qt, :], op0=ALU.mult, op1=ALU.mult, accum_out=ss[u][0:rows, qt:qt + 1]), r=[('d32', u)], w=['junk', ('ss', u)])
                                    P.op('dve', I('tensor_scalar', out=ss[u][0:rows, 0:nqt], in0=ss[u][0:rows, 0:nqt], scalar1=1e-6, scalar2=None, op0=ALU.add), r=[('ss', u)], w=[('ss', u)])
                                    P.op('pool', I('tensor_tensor', out=rstd[u][0:rows, 0:nqt], in0=ss[u][0:rows, 0:nqt], in1=mhalf[0:rows, 0:nqt], op=ALU.pow), r=[('ss', u)], w=[('rstd', u)])
                                    for qt in range(nqt):
                                        P.op('dve', I('tensor_scalar', out=OT[0:rows, tile0 + qt, 512 + h * 128:512 + (h + 1) * 128], in0=d32[u][0:rows, qt, :], scalar1=rstd[u][0:rows, qt:qt + 1], scalar2=None, op0=ALU.mult),
                                             r=[('d32', u), ('rstd', u)], w=[('OT', tile0 + qt, kind, h)])

                    if g == 2:
                        b = s % 2
                        pre_ = (0 in glist or 1 in glist)

                        def ld_ka(s_):
                            b_ = s_ % 2; cs_ = CS[2 + s_]; c0_ = s_ * 32
                            if pre_:
                                P.op('sp', I('dma_start', out=KaA[b_][0:64, :, 0:P_LEN], in_=KaS[s_][:, :].rearrange("(h d) t -> d h t", d=64)), w=[('Ka', b_)], dma=('Ka', b_))
                            else:
                                P.op('pool', I('dma_start', out=KaA[b_][0:64, :, 0:P_LEN], in_=fkT_s[s_].rearrange("(h d) t -> d h t", d=64)), w=[('Ka', b_)], dma=('Ka', b_))
                            P.op('sp', I('dma_start', out=KaA[b_][0:64, :, P_LEN:P_LEN + 32], in_=KF[g][:, c0_:c0_ + 32].rearrange("(h d) t -> d h t", d=64)), w=[('Ka', b_)], dma=('Ka', b_))
                            P.op('sp', I('dma_start', out=KaA[b_][67:70, :, 0:P_LEN + 32], in_=cs_[:, 3:6, 0:P_LEN + 32].rearrange("h j t -> j h t")), w=[('Ka', b_)], dma=('Ka', b_))
                            P.op('sp', I('dma_start', out=QaA[b_][0:64, :, :], in_=QF[g][:, c0_:c0_ + 32].rearrange("(h d) t -> d h t", d=64)), w=[('Qa', b_)], dma=('Qa', b_))
                            P.op('sp', I('dma_start', out=QaA[b_][64:67, :, :], in_=cs_[:, 0:3, P_LEN:P_LEN + 32].rearrange("h j t -> j h t")), w=[('Qa', b_)], dma=('Qa', b_))

                        def ld_vst(s_, which):
                            if pre_:
                                src_ = VfS if which == 0 else VdS
                                P.op('sp', I('dma_start', out=Vst[which][:, :, :], in_=src_[s_][:, :].rearrange("(t p) c -> p t c", p=128)), w=[('Vst', which)], dma=('Vst', which))
                            else:
                                src_ = fv_s if which == 0 else dv_s
                                P.op('pool', I('dma_start', out=Vst[which][:, :, :], in_=src_[s_].rearrange("(t p) c -> p t c", p=128)), w=[('Vst', which)], dma=('Vst', which))
                        if s == 0:
                            ld_ka(0); ld_vst(0, 0); ld_vst(0, 1)
                        c0 = s * 32
                        P.op('pool', I('tensor_copy', out=VfA[:, 0:16, :, 0:64], in_=Vst[0][:, :, :].rearrange("p t (h c) -> p t h c", h=8)), r=[('Vst', 0)], w=[('Vf', 0)])
                        P.op('sp', I('dma_start', out=VfA[0:32, 16, :, 0:64], in_=VF[g][c0:c0 + 32, :].rearrange("t (h c) -> t h c", h=8)), w=[('Vf', 0)], dma=('Vf', 0))
                        if s + 1 < 4:
                            ld_ka(s + 1); ld_vst(s + 1, 0)
                        if pre_:
                            P.op('sp', I('dma_start', out=KdA[:, :, 0:P_LEN], in_=KdS[s][:, :].rearrange("(h d) t -> d h t", d=128)), w=[('Kd', 0)], dma=('Kd', 0))
                        else:
                            P.op('pool', I('dma_start', out=KdA[:, :, 0:P_LEN], in_=dkT_s[s].rearrange("(h d) t -> d h t", d=128)), w=[('Kd', 0)], dma=('Kd', 0))
                        P.op('sp', I('dma_start', out=KdA[:, :, P_LEN:P_LEN + 32], in_=KD[g][:, c0:c0 + 32].rearrange("(h d) t -> d h t", d=128)), w=[('Kd', 0)], dma=('Kd', 0))
                        qv_ = QD[g][:, c0:c0 + 32].rearrange("(h m d) t -> m d h t", m=2, d=64)
                        P.op('sp', I('dma_start', out=QzA[0][0:64, :, :], in_=qv_[0]), w=[('Qd', 0)], dma=('Qd', 0))
                        P.op('sp', I('dma_start', out=QzA[1][64:128, :, :], in_=qv_[1]), w=[('Qd', 0)], dma=('Qd', 0))
                        P.op('pool', I('tensor_copy', out=VdA[:, 0:16, :, 0:128], in_=Vst[1][:, :, :].rearrange("p t (h c) -> p t h c", h=4)), r=[('Vst', 1)], w=[('Vd', 0)])
                        P.op('sp', I('dma_start', out=VdA[0:32, 16, :, 0:128], in_=VD[g][c0:c0 + 32, :].rearrange("t (h c) -> t h c", h=4)), w=[('Vd', 0)], dma=('Vd', 0))
                        if s + 1 < 4:
                            ld_vst(s + 1, 1)
                        for h in range(8):
                            attend('f', h, b, ov=(KaA[b][:, h, :], [QaA[b][:, h, :]], VfA[:, :, h, :], ('Ka', b), ('Qa', b), ('Vf', 0)))
                        for h in range(4):
                            attend('d', h, 0, ov=(KdA[:, h, :], [QzA[0][:, h, :], QzA[1][:, h, :]], VdA[:, :, h, :], ('Kd', 0), ('Qd', 0), ('Vd', 0)))
                    hctr = 0
                    for kind, nh in ((('d', 4), ('f', 8)) if g < 2 else ()):
                        for h in range(nh):
                            b = hctr % 2; hctr += 1
                            if kind == 'f':
                                P.op('sp', I('dma_start', out=Qa[b][0:64, 0:Lq], in_=QF[g][h * 64:(h + 1) * 64, qc0:qc0 + Lq]), w=[('Qa', b)], dma=('Qa', b))
                                P.op('sp', I('dma_start', out=Qa[b][64:67, 0:Lq], in_=cs[h, 0:3, qpos0:qpos0 + Lq]), w=[('Qa', b)], dma=('Qa', b))
                                if g < 2:
                                    P.op('sp', I('dma_start', out=Ka[b][0:64, 0:T], in_=KF[g][h * 64:(h + 1) * 64, :]), w=[('Ka', b)], dma=('Ka', b))
                                    P.op('sp', I('dma_start', out=Ka[b][67:70, 0:T], in_=cs[h, 3:6, 0:T]), w=[('Ka', b)], dma=('Ka', b))
                                    P.op('sp', I('dma_start', out=Vf[b][:, 0:16, 0:64], in_=VF[g][:, h * 64:(h + 1) * 64].rearrange("(t p) c -> p t c", p=128)), w=[('Vf', b)], dma=('Vf', b))
                                else:
                                    P.op('pool', I('dma_start', out=Ka[b][0:64, 0:P_LEN], in_=fkT_s[s, h * 64:(h + 1) * 64, :]), w=[('Ka', b)], dma=('Ka', b))
                                    P.op('sp', I('dma_start', out=Ka[b][0:64, P_LEN:P_LEN + 32], in_=KF[g][h * 64:(h + 1) * 64, qc0:qc0 + 32]), w=[('Ka', b)], dma=('Ka', b))
                                    P.op('sp', I('dma_start', out=Ka[b][67:70, 0:P_LEN + 32], in_=cs[h, 3:6, 0:P_LEN + 32]), w=[('Ka', b)], dma=('Ka', b))
                                    P.op('pool', I('dma_start', out=Vf[b][:, 0:16, 0:64], in_=fv_s[s, :, h * 64:(h + 1) * 64].rearrange("(t p) c -> p t c", p=128)), w=[('Vf', b)], dma=('Vf', b))
                                    P.op('sp', I('dma_start', out=Vf[b][0:32, 16, 0:64], in_=VF[g][qc0:qc0 + 32, h * 64:(h + 1) * 64]), w=[('Vf', b)], dma=('Vf', b))
                            else:
                                P.op('sp', I('dma_start', out=Qd[b][0][0:64, 0:Lq], in_=QD[g][h * 128:h * 128 + 64, qc0:qc0 + Lq]), w=[('Qd', b)], dma=('Qd', b))
                                P.op('sp', I('dma_start', out=Qd[b][1][64:128, 0:Lq], in_=QD[g][h * 128 + 64:(h + 1) * 128, qc0:qc0 + Lq]), w=[('Qd', b)], dma=('Qd', b))
                                if g < 2:
                                    P.op('sp', I('dma_start', out=Kd[b][:, 0:T], in_=KD[g][h * 128:(h + 1) * 128, :]), w=[('Kd', b)], dma=('Kd', b))
                                    P.op('sp', I('dma_start', out=Vd[b][:, 0:16, 0:128], in_=VD[g][:, h * 128:(h + 1) * 128].rearrange("(t p) c -> p t c", p=128)), w=[('Vd', b)], dma=('Vd', b))
                                else:
                                    P.op('pool', I('dma_start', out=Kd[b][:, 0:P_LEN], in_=dkT_s[s, h * 128:(h + 1) * 128, :]), w=[('Kd', b)], dma=('Kd', b))
                                    P.op('sp', I('dma_start', out=Kd[b][:, P_LEN:P_LEN + 32], in_=KD[g][h * 128:(h + 1) * 128, qc0:qc0 + 32]), w=[('Kd', b)], dma=('Kd', b))
                                    P.op('pool', I('dma_start', out=Vd[b][:, 0:16, 0:128], in_=dv_s[s, :, h * 128:(h + 1) * 128].rearrange("(t p) c -> p t c", p=128)), w=[('Vd', b)], dma=('Vd', b))
                                    P.op('sp', I('dma_start', out=Vd[b][0:32, 16, 0:128], in_=VD[g][qc0:qc0 + 32, h * 128:(h + 1) * 128]), w=[('Vd', b)], dma=('Vd', b))
                            attend(kind, h, b)
                    if g < 2:
                        allot = [('OT', tt, 'f', h) for tt in range(16) for h in range(8)] + [('OT', tt, 'd', h) for tt in range(16) for h in range(4)]
                        P.op('sp', I('dma_start', out=OTOK[g][:, :].rearrange("(t p) c -> p t c", p=128), in_=OT[:, :, :]), r=allot, dma='oto')
                    else:
                        allot = [('OT', s, 'f', h) for h in range(8)] + [('OT', s, 'd', h) for h in range(4)]
                        P.op('sp', I('dma_start', out=OTOK[g][s * 32:(s + 1) * 32, :], in_=OT[0:32, s, :]), r=allot, dma=('oto', s))
                while bg2:
                    bg2pop(force=True)
                P.emit()
            if STOP == f's2g{g}':
                return nc

            with ExitStack() as st:
                def sb(name, shape, dt=F32):
                    _uid[0] += 1
                    return st.enter_context(nc.sbuf_tensor(f"{name}_u{_uid[0]}", list(shape), dt))
                P = Prog(nc, f"s3g{g}")
                nt = N // 128
                nseg = 1 if g < 2 else 4
                L = N // nseg
                wo = sb("wo", [128, 8, D], BF); wd = sb("wd", [128, NF, D], BF)
                wu = [sb(f"wu{i}", [128, 8, 256], BF) for i in range(3)]
                xtokL = [sb(f"xtok{i}", [128, nt, D]) for i in range(2)]; otkL = [sb(f"otk{i}", [128, nt, D], BF) for i in range(2)]
                oT = sb("oT", [128, 8, N], BF); u2T = sb("u2T", [128, 8, N], BF); hT = sb("hT", [128, NF, N], BF)
                x1b = [sb(f"x1b{i}", [128, D], BF) for i in range(2)]; tmpL = [sb(f"tmp{i}", [128, D]) for i in range(2)]
                apad = [sb(f"apad{i}", [128, nseg, L + 2]) for i in range(2)]
                acc = [sb(f"acc{i}", [128, nseg, L]) for i in range(2)]; sg = [sb(f"sg{i}", [128, nseg, L], BF) for i in range(2)]
                halo = sb("halo", [128, NF, nseg, 2])
                gbc = sb("gbc", [128, 2, D])
                bstL = [sb(f"bst{i}", [128, 2, 6]) for i in range(2)]; mvL = [sb(f"mv{i}", [128, 2]) for i in range(2)]; sdL = [sb(f"sd{i}", [128, 1]) for i in range(2)]; nmrL = [sb(f"nmr{i}", [128, 1]) for i in range(2)]
                ps = [st.enter_context(nc.psum_tensor(f"q{i}_" + _pn(), [128, 512], F32)) for i in range(5)]
                tp = [st.enter_context(nc.psum_tensor(f"tp{i}_" + _pn(), [128, 1024], BF)) for i in range(3)]
                pctr = [0]

                def nbank():
                    pctr[0] += 1
                    return pctr[0] % 5
                xsrc = x_p[g] if g < 2 else x_s
                ydst = y_p[g] if g < 2 else y_s
                wctr = [0]

                lnctr = [0]

                def layernorm(lni, xtok, xk, tt=0):
                    pq = lnctr[0] % 2; lnctr[0] += 1
                    bst = bstL[pq]; mv = mvL[pq]; sd = sdL[pq]; nmr = nmrL[pq]
                    for hh in range(2):
                        P.op('dve', I('bn_stats', out=bst[:, hh, :], in_=xtok[:, tt, hh * 512:(hh + 1) * 512]), r=[('xtok', xk, tt)], w=[('bst', pq)])
                    P.op('dve', I('bn_aggr', out=mv[:], in_=bst[:].rearrange("p a b -> p (a b)")), r=[('bst', pq)], w=[('mv', pq)])
                    P.op('act', I('activation', out=sd[:], in_=mv[:, 1:2], func=AF.Sqrt, bias=1e-5, scale=1.0), r=[('mv', pq)], w=[('sd', pq)])
                    P.op('dve', I('reciprocal', out=sd[:], in_=sd[:]), r=[('sd', pq)], w=[('sd', pq)])
                    P.op('dve', I('tensor_scalar', out=nmr[:], in0=mv[:, 0:1], scalar1=sd[:, 0:1], scalar2=-1.0, op0=ALU.mult, op1=ALU.mult), r=[('mv', pq), ('sd', pq)], w=[('nmr', pq)])
                    P.op('act', I('activation', out=xtok[:, tt, :], in_=xtok[:, tt, :], func=AF.Identity, scale=sd[:, 0:1], bias=nmr[:, 0:1]), r=[('xtok', xk, tt), ('sd', pq), ('nmr', pq)], w=[('xtok', xk, tt)])
                    P.op('pool', I('tensor_tensor', out=xtok[:, tt, :], in0=xtok[:, tt, :], in1=lnbc[:, lni, :], op=ALU.mult), r=[('xtok', xk, tt)], w=[('xtok', xk, tt)])
                    P.op('pool', I('tensor_tensor', out=xtok[:, tt, :], in0=xtok[:, tt, :], in1=lnbc[:, lni + 1, :], op=ALU.add), r=[('xtok', xk, tt)], w=[('xtok', xk, tt)])

                def load_blk(blk):
                    xk = blk % 2
                    t0_ = blk * N
                    P.op('sp', I('dma_start', out=xtokL[xk][:], in_=xsrc[t0_:t0_ + N, :].rearrange("(t p) c -> p t c", p=128)), w=[('xtok', xk, tt) for tt in range(nt)], dma=('xtok', xk))
                    P.op('sp', I('dma_start', out=otkL[xk][:], in_=OTOK[g][t0_:t0_ + N, :].rearrange("(t p) c -> p t c", p=128)), w=[('otk', xk)], dma=('otk', xk))
                load_blk(0)
                P.op('sp', I('dma_start', out=wo[:], in_=wo4[:, :, :]), w=['wo'], dma='wo')
                P.op('act', I('activation', out=wo[:, 4:8, :], in_=wo[:, 4:8, :], func=AF.Identity, scale=wrs[:, 0:1]), r=['wo'], w=['wo'])
                if g < 2:
                    P.op('pool', I('memset', halo[:], 0.0), w=['halo'])
                else:
                    P.op('sp', I('dma_start', out=halo[:], in_=convT_s[:, :, :, :]), w=['halo'], dma='halo')
                for gi in range(2):
                    if g < 2:
                        P.op('sp', I('dma_start', out=gbc[:, gi, :], in_=MG[g, gi * 1024:(gi + 1) * 1024].partition_broadcast(128)), w=['gbc'], dma='gbc')
                    else:
                        for s in range(4):
                            P.op('sp', I('dma_start', out=gbc[s * 32:(s + 1) * 32, gi, :], in_=MG[2 + s, gi * 1024:(gi + 1) * 1024].partition_broadcast(32)), w=['gbc'], dma='gbc')
                P.op('sp', I('dma_start', out=wd[:], in_=wd4[:, :, :]), w=['wd'], dma='wd')
                for blk in range(nblk):
                    t0 = blk * N
                    xk = blk % 2
                    xtok = xtokL[xk]; otk = otkL[xk]
                    if blk + 1 < nblk:
                        load_blk(blk + 1)

                    bsA = {}

                    def phaseA(tt):
                        tb = tt % 2
                        tmp = tmpL[tt % 2]; tk = ('tmp', tt % 2)
                        for k in range(8):
                            P.op('pe', I('transpose', tp[tb][:, k * 128:(k + 1) * 128], otk[:, tt, k * 128:(k + 1) * 128], ident[:]), r=[('otk', xk), 'ident'], w=[('tp', tb)])
                        P.op('act', I('activation', out=oT[:, :, tt * 128:(tt + 1) * 128], in_=tp[tb][:, :].rearrange("p (k t) -> p k t", k=8), func=AF.Identity), r=[('tp', tb)], w=[('oT', tt)])
                        bs = []
                        for half in range(2):
                            b = nbank(); bs.append(b)
                            for k in range(8):
                                P.op('pe', I('matmul', ps[b][:, :], lhsT=oT[:, k, tt * 128:(tt + 1) * 128], rhs=wo[:, k, half * 512:(half + 1) * 512], start=(k == 0), stop=(k == 7)),
                                     r=[('oT', tt), 'wo'], w=[('ps', b)])
                        bsA[tt] = bs

                    def phaseL(tt):
                        tmp = tmpL[tt % 2]; tk = ('tmp', tt % 2)
                        bs = bsA[tt]
                        for half in range(2):
                            P.op('dve', I('tensor_tensor', out=tmp[:, half * 512:(half + 1) * 512], in0=ps[bs[half]][:, :], in1=gbc[:, 0, half * 512:(half + 1) * 512], op=ALU.mult), r=[('ps', bs[half]), 'gbc'], w=[tk])
                        P.op('dve', I('scalar_tensor_tensor', out=xtok[:, tt, :], in0=xtok[:, tt, :], scalar=ALPHA, in1=tmp[:], op0=ALU.mult, op1=ALU.add), r=[('xtok', xk, tt), tk], w=[('xtok', xk, tt)])
                        layernorm(0, xtok, xk, tt=tt)
                        P.op('act', I('activation', out=x1b[tt % 2][:], in_=xtok[:, tt, :], func=AF.Identity), r=[('xtok', xk, tt)], w=[('x1b', tt % 2)])

                    def phaseB(tt):
                        tb2 = 2
                        for k in range(8):
                            P.op('pe', I('transpose', tp[tb2][:, k * 128:(k + 1) * 128], x1b[tt % 2][:, k * 128:(k + 1) * 128], ident[:]), r=[('x1b', tt % 2), 'ident'], w=[('tp', tb2)])
                        for k in range(8):
                            if g < 2:
                                P.op('dve', I('tensor_scalar', out=u2T[:, k, tt * 128:(tt + 1) * 128], in0=tp[tb2][:, k * 128:(k + 1) * 128], scalar1=sc2p[:, k, g:g + 1], scalar2=modF[:, 24 + k, g:g + 1], op0=ALU.mult, op1=ALU.add),
                                     r=[('tp', tb2)], w=['u2T'])
                            else:
                                for s in range(4):
                                    P.op('dve', I('tensor_scalar', out=u2T[:, k, s * 32:(s + 1) * 32], in0=tp[tb2][:, k * 128 + s * 32:k * 128 + (s + 1) * 32], scalar1=sc2p[:, k, 2 + s:3 + s], scalar2=modF[:, 24 + k, 2 + s:3 + s], op0=ALU.mult, op1=ALU.add),
                                         r=[('tp', tb2)], w=['u2T'])
                    seq_ = []
                    for tt in range(nt):
                        seq_.append(('A', tt))
                        if tt >= 1:
                            seq_.append(('L', tt - 1))
                        if tt >= 2:
                            seq_.append(('B', tt - 2))
                    seq_ += [('L', nt - 1)]
                    if nt >= 2:
                        seq_ += [('B', nt - 2)]
                    seq_ += [('B', nt - 1)]
                    for kind_, tt_ in seq_:
                        {'A': phaseA, 'L': phaseL, 'B': phaseB}[kind_](tt_)
                    if L3 < 4:
                        continue
                    for f in range(NF):
                        sl = wctr[0] % 3; wctr[0] += 1
                        P.op('sp', I('dma_start', out=wu[sl][:, :, :], in_=wu4[f]), w=[('wu', sl)], dma=('wu', sl))
                        ba = nbank(); bb = nbank()
                        for k in range(8):
                            P.op('pe', I('matmul', ps[ba][:, 0:N], lhsT=wu[sl][:, k, 0:128], rhs=u2T[:, k, :], start=(k == 0), stop=(k == 7)), r=[('wu', sl), 'u2T'], w=[('ps', ba)])
                        for k in range(8):
                            P.op('pe', I('matmul', ps[bb][:, 0:N], lhsT=wu[sl][:, k, 128:256], rhs=u2T[:, k, :], start=(k == 0), stop=(k == 7)), r=[('wu', sl), 'u2T'], w=[('ps', bb)])
                        ab = f % 2
                        P.op('act', I('activation', out=apad[ab][:, :, 2:L + 2], in_=ps[ba][:, 0:N].rearrange("p (s l) -> p s l", s=nseg), func=AF.Identity), r=[('ps', ba)], w=[('apad', ab)])
                        P.op('pool', I('tensor_copy', out=apad[ab][:, :, 0:2], in_=halo[:, f, :, :]), r=[('halo', f)], w=[('apad', ab)])
                        P.op('dve', I('tensor_scalar', out=acc[ab][:], in0=apad[ab][:, :, 0:L], scalar1=cw[:, f, 0:1], scalar2=cw[:, f, 3:4], op0=ALU.mult, op1=ALU.add), r=[('apad', ab)], w=[('acc', ab)])
                        P.op('dve', I('scalar_tensor_tensor', out=acc[ab][:], in0=apad[ab][:, :, 1:L + 1], scalar=cw[:, f, 1:2], in1=acc[ab][:], op0=ALU.mult, op1=ALU.add), r=[('apad', ab), ('acc', ab)], w=[('acc', ab)])
                        P.op('dve', I('scalar_tensor_tensor', out=acc[ab][:], in0=apad[ab][:, :, 2:L + 2], scalar=cw[:, f, 2:3], in1=acc[ab][:], op0=ALU.mult, op1=ALU.add), r=[('apad', ab), ('acc', ab)], w=[('acc', ab)])
                        P.op('pool', I('tensor_copy', out=halo[:, f, :, :], in_=apad[ab][:, :, L:L + 2]), r=[('apad', ab)], w=[('halo', f)])
                        P.op('act', I('activation', out=sg[ab][:], in_=acc[ab][:], func=AF.Silu), r=[('acc', ab)], w=[('sg', ab)])
                        P.op('dve', I('tensor_tensor', out=hT[:, f, :].rearrange("p (s l) -> p s l", s=nseg), in0=sg[ab][:], in1=ps[bb][:, 0:N].rearrange("p (s l) -> p s l", s=nseg), op=ALU.mult), r=[('sg', ab), ('ps', bb)], w=['hT'])
                    if L3 < 5:
                        continue
                    for tt in range(nt):
                        tmp = tmpL[tt % 2]; tk = ('tmp', tt % 2)
                        bs = []
                        for half in range(2):
                            b = nbank(); bs.append(b)
                            for f in range(NF):
                                P.op('pe', I('matmul', ps[b][:, :], lhsT=hT[:, f, tt * 128:(tt + 1) * 128], rhs=wd[:, f, half * 512:(half + 1) * 512], start=(f == 0), stop=(f == NF - 1)),
                                     r=['hT', 'wd'], w=[('ps', b)])
                        for half in range(2):
                            P.op('dve', I('tensor_tensor', out=tmp[:, half * 512:(half + 1) * 512], in0=ps[bs[half]][:, :], in1=gbc[:, 1, half * 512:(half + 1) * 512], op=ALU.mult), r=[('ps', bs[half]), 'gbc'], w=[tk])
                        P.op('dve', I('scalar_tensor_tensor', out=xtok[:, tt, :], in0=xtok[:, tt, :], scalar=ALPHA, in1=tmp[:], op0=ALU.mult, op1=ALU.add), r=[('xtok', xk, tt), tk], w=[('xtok', xk, tt)])
                        layernorm(2, xtok, xk, tt=tt)
                        P.op('sp', I('dma_start', out=ydst[t0 + tt * 128:t0 + (tt + 1) * 128, :], in_=xtok[:, tt, :]), r=[('xtok', xk, tt)], dma=('yo', tt))
                P.op('sp', I('dma_start', out=convT_o[g][:, :, :, :], in_=halo[:]), r=[('halo', f) for f in range(NF)], dma='cvo')
                P.emit()
            if STOP == f's3g{g}':
                return nc
    return nc


def _rope_tables(pos):
    d = 64
    inv = (10000.0 ** (-np.arange(0, d, 2, dtype=np.float32) / d)).astype(np.float32)
    ang = pos.astype(np.float32)[None, :] * inv[:, None]
    cos = np.cos(ang).astype(np.float32); sin = np.sin(ang).astype(np.float32)
    C = np.concatenate([cos, cos, cos, cos], axis=0)
    S = np.concatenate([-sin, sin, -sin, sin], axis=0)
    return np.ascontiguousarray(C), np.ascontiguousarray(S)


_NC = None
_PREP_ONLY = False


def kernel(x_prompt, x_sample, c_prompt, c_sample, cache_fox_k, cache_fox_v, cache_fox_logf,
           cache_diff_k, cache_diff_v, state_ffn_conv, w_ada, b_ada, w_in, b_f, lambda_vecs,
           subln_g, w_o, ln1_g, ln1_b, w_up, conv_w, conv_b, w_down, ln2_g, ln2_b):
    global _NC
    f32 = np.float32
    A = lambda a: np.ascontiguousarray(np.asarray(a, dtype=f32))
    x_prompt = A(x_prompt); x_sample = A(x_sample)
    w_in0 = A(w_in)[0]
    fq = w_in0[:, 0:512]; fk = w_in0[:, 512:1024]; fv = w_in0[:, 1024:1536]; ff = w_in0[:, 1536:1544]
    dq = w_in0[:, 1544:2056]; dk = w_in0[:, 2056:2568]; dv = w_in0[:, 2568:3080]

    def swp(m):
        return m.reshape(D, 8, 2, 32)[:, :, ::-1, :].reshape(D, 512)
    w_in_ext = np.ascontiguousarray(np.concatenate([fq, fk, dq, swp(dq), dk, swp(dk), fv, dv, ff], axis=1))
    b_ada0 = A(b_ada)[0]
    common = {
        "w_ada": A(w_ada)[0], "b_adaT": np.ascontiguousarray(b_ada0.reshape(48, 128).T),
        "b_ada_g": np.ascontiguousarray(np.concatenate([b_ada0[2048:3072], b_ada0[5120:6144]])[None, :]),
        "w_in": w_in_ext, "b_f": np.ascontiguousarray(A(b_f)[0].reshape(8, 1)),
        "lam_v": A(lambda_vecs)[0].reshape(1, 256), "subln": A(subln_g)[0].reshape(128, 1),
        "w_o": A(w_o)[0], "lnp": np.ascontiguousarray(np.stack([A(ln1_g)[0], A(ln1_b)[0], A(ln2_g)[0], A(ln2_b)[0]])),
        "w_up": A(w_up)[0],
        "convw": np.ascontiguousarray(np.concatenate([A(conv_w)[0], A(conv_b)], axis=0).reshape(4, NF, 128).transpose(2, 1, 0)),
        "w_down": A(w_down)[0],
        "ident": np.eye(128, dtype=f32).astype(ml_dtypes.bfloat16),
        "tri": np.triu(np.ones((128, 128), dtype=f32)).astype(ml_dtypes.bfloat16),
    }
    Cp, Sp = _rope_tables(np.arange(T)); Cs, Ss = _rope_tables(P_LEN + np.arange(TS))
    common["ropeC_p"] = Cp; common["ropeS_p"] = Sp
    common["ropeC_s"] = np.ascontiguousarray(np.tile(Cs, (1, 4))); common["ropeS_s"] = np.ascontiguousarray(np.tile(Ss, (1, 4)))
    sel = np.zeros((6, 3, 128), dtype=f32)
    sel[0, 0, :] = 1; sel[1, 1, :] = 1
    for s in range(4):
        sel[2 + s, 2, s * 32:(s + 1) * 32] = 1
    common["sel"] = sel
    common["ones3"] = np.ones((3, T), dtype=f32).astype(ml_dtypes.bfloat16)
    cfk = A(cache_fox_k)[0]; cfv = A(cache_fox_v)[0]; clf = A(cache_fox_logf)[0]
    cdk = A(cache_diff_k)[0]; cdv = A(cache_diff_v)[0]; cst = A(state_ffn_conv)[0]
    c_prompt = A(c_prompt); c_sample = A(c_sample)
    in_maps = []
    for c in range(8):
        ps_ = slice(2 * c, 2 * c + 2); ss_ = slice(4 * c, 4 * c + 4)
        xp = x_prompt[ps_]
        xs = x_sample[ss_].reshape(128, D)
        call = np.concatenate([c_prompt[ps_], c_sample[ss_]], axis=0)
        m = dict(common)
        m["xT_p"] = np.ascontiguousarray(xp.reshape(2, T, 8, 128).transpose(0, 3, 2, 1))
        m["x_p"] = np.ascontiguousarray(xp)
        m["xT_s"] = np.ascontiguousarray(xs.reshape(128, 8, 128).transpose(2, 1, 0))
        m["x_s"] = np.ascontiguousarray(xs)
        m["cT"] = np.ascontiguousarray(call.reshape(6, 8, 128).transpose(2, 1, 0))
        m["fkT_s"] = np.ascontiguousarray(cfk[ss_].reshape(4, P_LEN, 512).transpose(0, 2, 1))
        m["fv_s"] = np.ascontiguousarray(cfv[ss_].reshape(4, P_LEN, 512))
        m["lfT_s"] = np.ascontiguousarray(clf[ss_].transpose(0, 2, 1))
        m["dkT_s"] = np.ascontiguousarray(cdk[ss_].reshape(4, P_LEN, 512).transpose(0, 2, 1))
        m["dv_s"] = np.ascontiguousarray(cdv[ss_].reshape(4, P_LEN, 512))
        m["convT_s"] = np.ascontiguousarray(cst[ss_].reshape(4, 2, NF, 128).transpose(3, 2, 0, 1))
        in_maps.append(m)
    if _PREP_ONLY:
        return in_maps
    if _NC is None:
        _NC = build()
    res = run_bass_kernel_spmd(_NC, in_maps, core_ids=list(range(8)))
    R = res.results
    yp = np.concatenate([r["y_p"] for r in R], axis=0)
    ys = np.concatenate([r["y_s"].reshape(4, TS, D) for r in R], axis=0)

    def catp(n0, n1):
        return [a for r in R for a in (r[n0], r[n1])]
    p_fk = np.stack([a.T.reshape(T, 8, 64) for a in catp("fkT_o0", "fkT_o1")])[None]
    p_fv = np.stack([a.reshape(T, 8, 64) for a in catp("fv_o0", "fv_o1")])[None]
    p_lf = np.stack([a.T for a in catp("lfT_o0", "lfT_o1")])[None]
    p_dk = np.stack([a.T.reshape(T, 8, 64) for a in catp("dkT_o0", "dkT_o1")])[None]
    p_dv = np.stack([a.reshape(T, 4, 128) for a in catp("dv_o0", "dv_o1")])[None]
    p_cv = np.stack([a.reshape(128, NF, 2).transpose(2, 1, 0).reshape(2, DFF) for a in catp("convT_o0", "convT_o1")])[None]
    s_fk = np.concatenate([r["sfkT_o"].T.reshape(4, TS, 8, 64) for r in R], axis=0)[None]
    s_fv = np.concatenate([r["sfv_o"].reshape(4, TS, 8, 64) for r in R], axis=0)[None]
    s_lf = np.concatenate([r["slfT_o"].T.reshape(4, TS, 8) for r in R], axis=0)[None]
    s_dk = np.concatenate([r["sdkT_o"].T.reshape(4, TS, 8, 64) for r in R], axis=0)[None]
    s_dv = np.concatenate([r["sdv_o"].reshape(4, TS, 4, 128) for r in R], axis=0)[None]
    s_cv = np.concatenate([r["sconvT_o"].transpose(2, 3, 1, 0).reshape(4, 2, DFF) for r in R], axis=0)[None]
    outs = (yp, ys, p_fk, p_fv, p_lf, p_dk, p_dv, p_cv, s_fk, s_fv, s_lf, s_dk, s_dv, s_cv)
    return tuple(np.ascontiguousarray(o, dtype=f32) for o in outs)
```

```python
import numpy as np
import os
import ml_dtypes
from contextlib import ExitStack
import concourse.bass as bass
import concourse.mybir as mybir
from concourse.bass_utils import run_bass_kernel_spmd

F32 = mybir.dt.float32
BF = mybir.dt.bfloat16
AF = mybir.ActivationFunctionType
ALU = mybir.AluOpType

T = 2048
D = 1024
DFF = 2816
NF = 22
P_LEN = 2048
TS = 32
ALPHA = 2.0 ** 0.25
LAM_INIT = 0.8 - 0.6
WK = 2112
ENG = ('pe', 'act', 'dve', 'pool', 'sp')


def I(name, *a, **k):
    return (name, a, k)


class Prog:
    G = None

    def __init__(s, nc, name):
        s.nc = nc; s.name = name; s.ops = []; s.lastw = {}; s.rd = {}; s.dma_cnt = {}; s.spd = []; s.pld = []

    def op(s, eng, fn, r=(), w=(), dma=None):
        i = len(s.ops); deps = set(); raw = set()
        if eng in ('sp', 'pool') and dma is not None and fn is not None:
            nd = 1
            for ap in (fn[2].get('out'), fn[2].get('in_')):
                try:
                    shp = list(ap.shape)
                    n_ = 1
                    for d_ in shp[:-1]:
                        n_ *= int(d_)
                    nd = max(nd, n_)
                except Exception:
                    nd = max(nd, 1024)
            tot = nd
            lst = s.spd if eng == 'sp' else s.pld
            for (j_, ndj) in reversed(lst):
                tot += ndj
                if tot > (2500 if eng == 'sp' else 6000):
                    deps.add(j_)
                    break
            lst.append((i, nd))
        for x in r:
            if x in s.lastw:
                deps.add(s.lastw[x]); raw.add(s.lastw[x])
        for x in w:
            if x in s.lastw:
                deps.add(s.lastw[x])
            for j in s.rd.get(x, ()):
                deps.add(j)
        for x in r:
            s.rd.setdefault(x, []).append(i)
        for x in w:
            s.lastw[x] = i; s.rd[x] = []
        deps.discard(i)
        if dma is not None:
            dma = (dma, eng)
        o = dict(i=i, eng=eng, fn=fn, deps=deps, raw=raw, dma=dma, sig=False)
        if dma is not None:
            s.dma_cnt[dma] = s.dma_cnt.get(dma, 0) + 1
            o['dval'] = 16 * s.dma_cnt[dma]
        s.ops.append(o)
        return i

    def emit(s):
        nc = s.nc
        last = {}
        for o in s.ops:
            if o['dma'] is not None:
                last[o['dma']] = o['i']
        fin = dict(i=len(s.ops), eng='sp', fn=None, deps=set(last.values()), raw=set(), dma=None, sig=False)
        s.ops.append(fin)
        for o in s.ops:
            o['waits'] = []
            for j in sorted(o['deps']):
                d = s.ops[j]
                if d['dma'] is not None:
                    o['waits'].append(('dma', d['dma'], d['dval']))
                else:
                    if d['eng'] == o['eng'] and (d['eng'] == 'pe' or (j not in o['raw'] and d['eng'] != 'pool')):
                        continue
                    d['sig'] = True
                    o['waits'].append(('eng', j))
        cnt = {e: 0 for e in ENG}
        for o in s.ops:
            if o['dma'] is None and o['sig']:
                cnt[o['eng']] += 1; o['cval'] = cnt[o['eng']]
        G = s.G
        keys = list(s.dma_cnt)
        swk = [k for k in keys if k[1] == 'pool']; hwk = [k for k in keys if k[1] != 'pool']
        assert len(swk) <= 12 and len(hwk) <= len(G['dsem']) - 12, (s.name, len(swk), len(hwk))
        kidx = {k: n for n, k in enumerate(swk)}
        kidx.update({k: 12 + n for n, k in enumerate(hwk)})
        esem = G['esem']; ebase = dict(G['ebase']); dbase = list(G['dbase'])
        with ExitStack() as es:
            block = es.enter_context(nc.Block())

            def body(engname):
                def f(e):
                    seen = {}
                    for o in s.ops:
                        if o['eng'] != engname:
                            continue
                        for wt in o['waits']:
                            if wt[0] == 'dma':
                                n = kidx[wt[1]]; sem = G['dsem'][n]; val = dbase[n] + wt[2]; key = ('d', n)
                            else:
                                d = s.ops[wt[1]]; sem = esem[d['eng']]; val = ebase[d['eng']] + d['cval']; key = ('e', d['eng'])
                            if seen.get(key, 0) >= val:
                                continue
                            seen[key] = val
                            e.wait_ge(sem, val)
                        if o['fn'] is None:
                            continue
                        nm, a_, k_ = o['fn']
                        inst = getattr(e, nm)(*a_, **k_)
                        if o['dma'] is not None:
                            inst.then_inc(G['dsem'][kidx[o['dma']]], 16)
                        elif o['sig']:
                            inst.then_inc(esem[engname], 1)
                return f
            block.tensor(body('pe')); block.scalar(body('act')); block.vector(body('dve'))
            block.gpsimd(body('pool')); block.sync(body('sp'))
        for e_ in ENG:
            G['ebase'][e_] += cnt[e_]
        for k, n in kidx.items():
            G['dbase'][n] += 16 * s.dma_cnt[k]


_uid = [0]


def _pn():
    _uid[0] += 1
    return f"u{_uid[0]}"


def build():
    nc = bass.Bass("TRN2", target_bir_lowering=False)
    STOP = os.environ.get('KSTOP', 'zz')
    DBG = bool(os.environ.get('KDBG'))
    LVL = int(os.environ.get('KLVL', '99'))
    L3 = int(os.environ.get('KL3', '99'))

    def din(name, shape, dt=F32):
        return nc.dram_tensor(name, list(shape), dt, kind="ExternalInput").ap()

    def dout(name, shape, dt=F32):
        return nc.dram_tensor(name, list(shape), dt, kind="ExternalOutput").ap()

    def dscr(name, shape, dt=BF):
        return nc.dram_tensor(name, list(shape), dt, kind=("ExternalOutput" if DBG else "Internal")).ap()

    xT_p = din("xT_p", [2, 128, 8, T]); x_p = din("x_p", [2, T, D])
    xT_s = din("xT_s", [128, 8, 128]); x_s = din("x_s", [128, D])
    cT = din("cT", [128, 8, 6])
    fkT_s = din("fkT_s", [4, 512, P_LEN]); fv_s = din("fv_s", [4, P_LEN, 512]); lfT_s = din("lfT_s", [4, 8, P_LEN])
    dkT_s = din("dkT_s", [4, 512, P_LEN]); dv_s = din("dv_s", [4, P_LEN, 512]); convT_s = din("convT_s", [128, NF, 4, 2])
    w_ada = din("w_ada", [D, 6 * D]); b_adaT = din("b_adaT", [128, 48]); b_ada_g = din("b_ada_g", [1, 2048])
    w_in = din("w_in", [D, 4104]); nb_f = din("b_f", [8, 1]); lam_v = din("lam_v", [1, 256]); subln = din("subln", [128, 1])
    w_o = din("w_o", [D, D]); lnp = din("lnp", [4, D]); w_up = din("w_up", [D, 2 * DFF]); convw = din("convw", [128, NF, 4])
    w_down = din("w_down", [DFF, D])
    ident_d = din("ident", [128, 128], BF); tri_d = din("tri", [128, 128], BF)
    ropeC_p = din("ropeC_p", [128, T]); ropeS_p = din("ropeS_p", [128, T])
    ropeC_s = din("ropeC_s", [128, 128]); ropeS_s = din("ropeS_s", [128, 128])
    sel_d = din("sel", [6, 3, 128])
    ones3_d = din("ones3", [3, T], BF)
    y_p = dout("y_p", [2, T, D]); y_s = dout("y_s", [128, D])
    fkT_o = [dout("fkT_o0", [512, T]), dout("fkT_o1", [512, T]), dout("sfkT_o", [512, 128])]
    fv_o = [dout("fv_o0", [T, 512]), dout("fv_o1", [T, 512]), dout("sfv_o", [128, 512])]
    lfT_o = [dout("lfT_o0", [8, T]), dout("lfT_o1", [8, T]), dout("slfT_o", [8, 128])]
    dkT_o = [dout("dkT_o0", [512, T]), dout("dkT_o1", [512, T]), dout("sdkT_o", [512, 128])]
    dv_o = [dout("dv_o0", [T, 512]), dout("dv_o1", [T, 512]), dout("sdv_o", [128, 512])]
    convT_o = [dout("convT_o0", [128, NF, 1, 2]), dout("convT_o1", [128, NF, 1, 2]), dout("sconvT_o", [128, NF, 4, 2])]
    wi4 = dscr("wi4", [8, 128, 8, 512]); wiF = dscr("wiF", [128, 8, 8]); wo4 = dscr("wo4", [128, 8, D]); wu4 = dscr("wu4", [NF, 128, 8, 256]); wd4 = dscr("wd4", [128, NF, D])
    TG = [T, T, 128]
    QF = [dscr(f"QF{g}", [512, TG[g]]) for g in range(3)]
    KF = [dscr(f"KF{g}", [512, TG[g]]) for g in range(3)]
    QD = [dscr(f"QD{g}", [512, TG[g]]) for g in range(3)]
    KD = [dscr(f"KD{g}", [512, TG[g]]) for g in range(3)]
    VF = [dscr(f"VF{g}", [TG[g], 512]) for g in range(3)]
    VD = [dscr(f"VD{g}", [TG[g], 512]) for g in range(3)]
    CS = [dscr("CS0", [8, 6, T]), dscr("CS1", [8, 6, T])] + [dscr(f"CSs{s}", [8, 6, WK]) for s in range(4)]
    OTOK = [dscr(f"OTOK{g}", [TG[g], D]) for g in range(3)]
    MG = dscr("MG", [6, 2048], F32)
    KaS = [dscr(f"KaS{i}", [512, P_LEN]) for i in range(4)]; KdS = [dscr(f"KdS{i}", [512, P_LEN]) for i in range(4)]
    VfS = [dscr(f"VfS{i}", [P_LEN, 512]) for i in range(4)]; VdS = [dscr(f"VdS{i}", [P_LEN, 512]) for i in range(4)]
    precast = [True]

    with ExitStack() as gs:
        def sb(name, shape, dt=F32):
            return gs.enter_context(nc.sbuf_tensor(name, list(shape), dt))
        Prog.G = dict(esem={e_: gs.enter_context(nc.semaphore(f"ge_{e_}")) for e_ in ENG},
                      dsem=[gs.enter_context(nc.semaphore(f"gd_{n_}")) for n_ in range(40)],
                      ebase={e_: 0 for e_ in ENG}, dbase=[0] * 40)
        ident = sb("ident_sb", [128, 128], BF); tri = sb("tri_sb", [128, 128], BF)
        modF = sb("modF", [128, 48, 6]); sc1p = sb("sc1p", [128, 8, 6]); sc2p = sb("sc2p", [128, 8, 6])
        lnbc = sb("lnbc", [128, 4, D]); neg_lam = sb("neg_lam", [128, 1]); wrs = sb("wrs", [128, 1])
        nbf = sb("nbf", [8, 1]); cw = sb("cw", [128, NF, 4])

        with ExitStack() as st:
            def sb(name, shape, dt=F32):
                _uid[0] += 1
                return st.enter_context(nc.sbuf_tensor(f"{name}_u{_uid[0]}", list(shape), dt))
            P = Prog(nc, "s0")
            cts = sb("cts", [128, 8, 6]); scb = sb("scb", [128, 8, 6], BF)
            modG = sb("modG", [6, 2048])
            wa = [sb(f"wa{i}", [128, 8, 1024], BF) for i in range(2)]
            badT = sb("badT", [128, 48]); bag = sb("bag", [6, 2048])
            lv = sb("lv", [128, 256]); lt = sb("lt", [128, 128]); ls = sb("ls", [128, 2]); le = sb("le", [128, 2])
            subl = sb("subl", [128, 1])
            psA = st.enter_context(nc.psum_tensor("psA_" + _pn(), [128, 512], F32))
            psG = [st.enter_context(nc.psum_tensor(f"psG{i}_" + _pn(), [128, 512], F32)) for i in range(4)]
            for u_ in range(8):
                P.op('pool', I('dma_start', out=wi4[u_], in_=w_in[:, u_ * 512:(u_ + 1) * 512].rearrange("(k p) c -> p k c", p=128)), dma=('wc', u_))
            P.op('pool', I('dma_start', out=wiF[:, :, :], in_=w_in[:, 4096:4104].rearrange("(k p) c -> p k c", p=128)), dma=('wc', 8))
            P.op('pool', I('dma_start', out=wo4[:, :, :], in_=w_o[:, :].rearrange("(k p) c -> p k c", p=128)), dma=('wc', 9))
            P.op('sp', I('dma_start', out=cts[:], in_=cT[:, :, :]), w=['cts'], dma='l0')
            P.op('sp', I('dma_start', out=badT[:], in_=b_adaT[:, :]), w=['badT'], dma='l1')
            P.op('sp', I('dma_start', out=bag[:], in_=b_ada_g[0, :].partition_broadcast(6)), w=['bag'], dma='l2')
            P.op('sp', I('dma_start', out=ident[:], in_=ident_d[:, :]), w=['ident'], dma='l4')
            P.op('sp', I('dma_start', out=tri[:], in_=tri_d[:, :]), w=['tri'], dma='l5')
            P.op('sp', I('dma_start', out=nbf[:], in_=nb_f[:, :]), w=['nbf'], dma='l6')
            P.op('dve', I('tensor_scalar', out=nbf[:], in0=nbf[:], scalar1=-1.0, scalar2=None, op0=ALU.mult), r=['nbf'], w=['nbf'])
            P.op('sp', I('dma_start', out=cw[:], in_=convw[:, :, :]), w=['cw'], dma='l7')
            P.op('sp', I('dma_start', out=subl[:], in_=subln[:, :]), w=['subl'], dma='l8')
            P.op('sp', I('dma_start', out=lv[:], in_=lam_v[0, :].partition_broadcast(128)), w=['lv'], dma='l9')
            for j in range(4):
                P.op('sp', I('dma_start', out=lnbc[:, j, :], in_=lnp[j, :].partition_broadcast(128)), w=['lnbc'], dma='l10')
            P.op('act', I('activation', out=scb[:], in_=cts[:], func=AF.Silu), r=['cts'], w=['scb'])
            for ch in range(6):
                P.op('pool', I('dma_start', out=wa[ch % 2][:], in_=w_ada[:, ch * 1024:(ch + 1) * 1024].rearrange("(k p) c -> p k c", p=128)),
                     w=[('wa', ch % 2)], dma=('wa', ch % 2))
                for jj in range(8):
                    j = ch * 8 + jj
                    for k in range(8):
                        P.op('pe', I('matmul', psA[:, j * 6:(j + 1) * 6], lhsT=wa[ch % 2][:, k, jj * 128:(jj + 1) * 128], rhs=scb[:, k, :], start=(k == 0), stop=(k == 7)),
                             r=[('wa', ch % 2), 'scb'], w=['psA'])
                if ch in (2, 5):
                    gi = 0 if ch == 2 else 1
                    for half in range(2):
                        for k in range(8):
                            P.op('pe', I('matmul', psG[gi * 2 + half][0:6, :], lhsT=scb[:, k, :], rhs=wa[ch % 2][:, k, half * 512:(half + 1) * 512], start=(k == 0), stop=(k == 7)),
                                 r=[('wa', ch % 2), 'scb'], w=[('psG', gi * 2 + half)])
                        P.op('dve', I('tensor_tensor', out=modG[:, gi * 1024 + half * 512: gi * 1024 + (half + 1) * 512], in0=psG[gi * 2 + half][0:6, :],
                                                                               in1=bag[:, gi * 1024 + half * 512: gi * 1024 + (half + 1) * 512], op=ALU.add),
                             r=[('psG', gi * 2 + half), 'bag'], w=['modG'])
            for j in range(6):
                P.op('dve', I('tensor_tensor', out=modF[:, :, j], in0=psA[:, 0:288].rearrange("p (c j) -> p c j", j=6)[:, :, j], in1=badT[:, :], op=ALU.add),
                     r=['psA', 'badT'], w=['modF'])
            P.op('dve', I('tensor_scalar', out=sc1p[:], in0=modF[:, 8:16, :], scalar1=1.0, scalar2=None, op0=ALU.add), r=['modF'], w=['sc1p'])
            P.op('dve', I('tensor_scalar', out=sc2p[:], in0=modF[:, 32:40, :], scalar1=1.0, scalar2=None, op0=ALU.add), r=['modF'], w=['sc2p'])
            P.op('dve', I('tensor_tensor', out=lt[:, 0:64], in0=lv[:, 0:64], in1=lv[:, 64:128], op=ALU.mult), r=['lv'], w=['lt'])
            P.op('dve', I('tensor_tensor', out=lt[:, 64:128], in0=lv[:, 128:192], in1=lv[:, 192:256], op=ALU.mult), r=['lv'], w=['lt'])
            P.op('dve', I('reduce_sum', out=ls[:, :], in_=lt[:, :].rearrange("p (a b) -> p a b", a=2), axis=mybir.AxisListType.X), r=['lt'], w=['ls'])
            P.op('act', I('activation', out=le[:], in_=ls[:], func=AF.Exp), r=['ls'], w=['le'])
            P.op('dve', I('tensor_tensor', out=neg_lam[:], in0=le[:, 1:2], in1=le[:, 0:1], op=ALU.subtract), r=['le'], w=['nl'])
            P.op('dve', I('tensor_scalar', out=neg_lam[:], in0=neg_lam[:], scalar1=-LAM_INIT, scalar2=None, op0=ALU.add), r=['nl'], w=['nl'])
            P.op('dve', I('tensor_scalar', out=wrs[:], in0=subl[:], scalar1=1.0 - LAM_INIT, scalar2=None, op0=ALU.mult), r=['subl'], w=['wrs'])
            P.op('sp', I('dma_start', out=MG[:, :], in_=modG[:]), r=['modG'], dma='mg')
            if DBG:
                dbgF = dout("dbg_modF", [128, 288]); dbgG = dout("dbg_modG", [6, 2048]); dbgM = dout("dbg_misc", [2, 128, 1])
                P.op('sp', I('dma_start', out=dbgF[:, :], in_=modF[:].rearrange("p a b -> p (a b)")), r=['modF'], dma='dbg')
                P.op('sp', I('dma_start', out=dbgG[:, :], in_=modG[:]), r=['modG'], dma='dbg')
                P.op('sp', I('dma_start', out=dbgM[0], in_=neg_lam[:]), r=['nl'], dma='dbg')
                P.op('sp', I('dma_start', out=dbgM[1], in_=wrs[:]), r=['wrs'], dma='dbg')
            P.emit()
        if STOP == 's0':
            return nc

        for g in [int(c_) for c_ in os.environ.get('KG', '012')]:
            Tg = TG[g]; N = min(512, Tg); nblk = Tg // N; ntile = Tg // 128
            with ExitStack() as st:
                def sb(name, shape, dt=F32):
                    _uid[0] += 1
                    return st.enter_context(nc.sbuf_tensor(f"{name}_u{_uid[0]}", list(shape), dt))
                P = Prog(nc, f"s1g{g}")
                uT = sb("uT", [128, 8, Tg], BF)
                xs = [sb(f"xs{i}", [128, 4, N]) for i in range(2)]
                wr = [sb(f"wr{i}", [128, 8, 512], BF) for i in range(4)]
                wff = sb("wff", [128, 8, 8], BF)
                rC = sb("rC", [128, Tg]); rS = sb("rS", [128, Tg])
                t1 = [sb(f"t1_{i}", [128, N]) for i in range(2)]; t2 = [sb(f"t2_{i}", [128, N]) for i in range(2)]
                r32 = [sb(f"r32_{i}", [128, N]) for i in range(2)]
                stg = [sb(f"stg{i}", [128, N], BF) for i in range(2)]
                v32 = [sb(f"v32_{i}", [128, 512]) for i in range(2)]; vb = [sb(f"vb{i}", [128, 512], BF) for i in range(2)]
                lf = sb("lf", [8, Tg]); lfp = sb("lfp", [8, P_LEN if g == 2 else 8]); cum = sb("cum", [8, WK]); r1 = sb("r1", [8, WK]); ones8 = sb("ones8", [8, WK])
                spl = sb("spl", [8, 6, WK], BF)
                ps = [st.enter_context(nc.psum_tensor(f"ps{i}_" + _pn(), [128, 512], F32)) for i in range(8)]
                pctr = [0]

                def nbank():
                    pctr[0] += 1
                    return pctr[0] % 8
                sctr = [0]
                xTsrc = xT_p[g] if g < 2 else xT_s
                bg = []
                if g == int(os.environ.get('KG', '012')[0]):
                    for f_ in range(NF):
                        bg.append(I('dma_start', out=wu4[f_][:, :, 0:128], in_=w_up[:, f_ * 128:(f_ + 1) * 128].rearrange("(k p) c -> p k c", p=128)))
                        bg.append(I('dma_start', out=wu4[f_][:, :, 128:256], in_=w_up[:, DFF + f_ * 128:DFF + (f_ + 1) * 128].rearrange("(k p) c -> p k c", p=128)))
                    for f0 in (0, 11):
                        bg.append(I('dma_start', out=wd4[:, f0:f0 + 11, :], in_=w_down[f0 * 128:(f0 + 11) * 128, :].rearrange("(k p) c -> p k c", p=128)))
                bgn = [0]

                def bgpop():
                    if bg:
                        P.op('pool', bg.pop(0), dma=('bgc', bgn[0] % 8)); bgn[0] += 1
                P.op('pool', I('memset', ones8[:], 1.0), w=['ones8'])
                NWS = 4
                wslot = {}
                wissued = [0]

                def issue_upto(u_):
                    while wissued[0] <= min(u_, 7):
                        uu = wissued[0]; i = uu % NWS
                        P.op('sp', I('dma_start', out=wr[i][:, :, :], in_=wi4[uu]), w=[('wr', i)], dma=('wr', i))
                        wslot[uu] = i; wissued[0] += 1
                if LVL >= 2:
                    issue_upto(1)

                for blk in range(nblk):
                    for hf in range(2):
                        xb = xs[hf]
                        P.op('sp', I('dma_start', out=xb[:], in_=xTsrc[:, hf * 4:(hf + 1) * 4, blk * N:(blk + 1) * N]), w=[('xs', hf)], dma=('xs', hf))
                        for k4 in range(4):
                            k = hf * 4 + k4
                            if g < 2:
                                if k4 % 2 == 0:
                                    P.op('pool', I('tensor_scalar', out=uT[:, k, blk * N:(blk + 1) * N], in0=xb[:, k4, :], scalar1=sc1p[:, k, g:g + 1], scalar2=modF[:, k, g:g + 1], op0=ALU.mult, op1=ALU.add),
                                         r=[('xs', hf)], w=[('uT', blk)])
                                else:
                                    P.op('act', I('activation', out=uT[:, k, blk * N:(blk + 1) * N], in_=xb[:, k4, :], func=AF.Identity, scale=sc1p[:, k, g:g + 1], bias=modF[:, k, g:g + 1]),
                                         r=[('xs', hf)], w=[('uT', blk)])
                            else:
                                for s in range(4):
                                    P.op('pool', I('tensor_scalar', out=uT[:, k, s * 32:(s + 1) * 32], in0=xb[:, k4, s * 32:(s + 1) * 32], scalar1=sc1p[:, k, 2 + s:3 + s], scalar2=modF[:, k, 2 + s:3 + s], op0=ALU.mult, op1=ALU.add),
                                         r=[('xs', hf)], w=[('uT', blk)])
                P.op('sp', I('dma_start', out=rC[:], in_=(ropeC_p if g < 2 else ropeC_s)[:, :]), w=['rC'], dma='rc')
                P.op('sp', I('dma_start', out=rS[:], in_=(ropeS_p if g < 2 else ropeS_s)[:, :]), w=['rS'], dma='rs')
                uTall = [('uT', b) for b in range(nblk)]

                def loadw(c0, ncols=512):
                    uu = c0 // 512
                    issue_upto(uu)
                    return wslot[uu]

                def fm_group(slot, co, blk):
                    b = nbank()
                    for k in range(8):
                        P.op('pe', I('matmul', ps[b][:, 0:N], lhsT=wr[slot][:, k, co:co + 128], rhs=uT[:, k, blk * N:(blk + 1) * N], start=(k == 0), stop=(k == 7)),
                             r=[('wr', slot), ('uT', blk)], w=[('ps', b)])
                    return b
                octr = [0]
                for which, c0 in (((('q', 0), ('k', 512)) if not os.environ.get('KQ') else (('q', 0),)) if LVL >= 2 else ()):
                    slot = loadw(c0)
                    issue_upto(c0 // 512 + 2)
                    for blk in range(nblk):
                        for c in range(4):
                            b = fm_group(slot, c * 128, blk)
                            i = octr[0] % 2; octr[0] += 1
                            if which == 'q':
                                P.op('act', I('activation', out=stg[i][:], in_=ps[b][:, 0:N], func=AF.Identity), r=[('ps', b)], w=[('stg', i)])
                            else:
                                P.op('act', I('activation', out=r32[i][:], in_=ps[b][:, 0:N], func=AF.Identity), r=[('ps', b)], w=[('r32', i)])
                                P.op('pool', I('tensor_copy', out=stg[i][:], in_=r32[i][:]), r=[('r32', i)], w=[('stg', i)])
                                bgpop()
                                P.op('sp', I('dma_start', out=fkT_o[g][c * 128:(c + 1) * 128, blk * N:(blk + 1) * N], in_=r32[i][:]), r=[('r32', i)], dma=('r32o', i))
                            dst = (QF if which == 'q' else KF)[g]
                            P.op('sp', I('dma_start', out=dst[c * 128:(c + 1) * 128, blk * N:(blk + 1) * N], in_=stg[i][:]), r=[('stg', i)], dma=('stgo', i))
                for which, c0 in ((('q', 1024), ('k', 2048)) if LVL >= 3 else ()):
                    sa = loadw(c0); sbw = loadw(c0 + 512)
                    issue_upto(c0 // 512 + 3)
                    for c in range(4):
                        for blk in range(nblk):
                            ba = fm_group(sa, c * 128, blk); bb = fm_group(sbw, c * 128, blk)
                            i = octr[0] % 2; octr[0] += 1
                            P.op('dve', I('tensor_tensor', out=t1[i][:], in0=ps[ba][:, 0:N], in1=rC[:, blk * N:(blk + 1) * N], op=ALU.mult), r=[('ps', ba), 'rC'], w=[('t1', i)])
                            P.op('dve', I('tensor_tensor', out=t2[i][:], in0=ps[bb][:, 0:N], in1=rS[:, blk * N:(blk + 1) * N], op=ALU.mult), r=[('ps', bb), 'rS'], w=[('t2', i)])
                            P.op('pool', I('tensor_tensor', out=r32[i][:], in0=t1[i][:], in1=t2[i][:], op=ALU.add), r=[('t1', i), ('t2', i)], w=[('r32', i)])
                            bgpop()
                            P.op('act', I('activation', out=stg[i][:], in_=r32[i][:], func=AF.Identity), r=[('r32', i)], w=[('stg', i)])
                            dst = (QD if which == 'q' else KD)[g]
                            P.op('sp', I('dma_start', out=dst[c * 128:(c + 1) * 128, blk * N:(blk + 1) * N], in_=stg[i][:]), r=[('stg', i)], dma=('stgo', i))
                            if which == 'k':
                                P.op('sp', I('dma_start', out=dkT_o[g][c * 128:(c + 1) * 128, blk * N:(blk + 1) * N], in_=r32[i][:]), r=[('r32', i)], dma=('r32o', i))
                for c0, vo, vs in (((3072, fv_o[g], VF[g]), (3584, dv_o[g], VD[g])) if LVL >= 4 else ()):
                    slot = loadw(c0)
                    issue_upto(c0 // 512 + 1)
                    for tt in range(ntile):
                        b = nbank()
                        for k in range(8):
                            P.op('pe', I('matmul', ps[b][:, :], lhsT=uT[:, k, tt * 128:(tt + 1) * 128], rhs=wr[slot][:, k, :], start=(k == 0), stop=(k == 7)),
                                 r=[('wr', slot)] + uTall, w=[('ps', b)])
                        i = octr[0] % 2; octr[0] += 1
                        P.op('act', I('activation', out=v32[i][:], in_=ps[b][:, :], func=AF.Identity), r=[('ps', b)], w=[('v32', i)])
                        P.op('pool', I('tensor_copy', out=vb[i][:], in_=v32[i][:]), r=[('v32', i)], w=[('vb', i)])
                        bgpop()
                        P.op('sp', I('dma_start', out=vo[tt * 128:(tt + 1) * 128, :], in_=v32[i][:]), r=[('v32', i)], dma=('v32o', i))
                        P.op('sp', I('dma_start', out=vs[tt * 128:(tt + 1) * 128, :], in_=vb[i][:]), r=[('vb', i)], dma=('vbo', i))
                P.op('sp', I('dma_start', out=wff[:], in_=wiF[:, :, :]), w=['wff'], dma='wff')
                for blk in (range(nblk) if LVL >= 5 else ()):
                    b = nbank()
                    for k in range(8):
                        P.op('pe', I('matmul', ps[b][0:8, 0:N], lhsT=wff[:, k, :], rhs=uT[:, k, blk * N:(blk + 1) * N], start=(k == 0), stop=(k == 7)),
                             r=['wff', ('uT', blk)], w=[('ps', b)])
                    P.op('act', I('activation', out=lf[:, blk * N:(blk + 1) * N], in_=ps[b][0:8, 0:N], func=AF.Exp, bias=nbf[:, 0:1], scale=-1.0), r=[('ps', b)], w=['lf'])
                P.op('act', I('activation', out=lf[:], in_=lf[:], func=AF.Ln, bias=1.0, scale=1.0), r=['lf'], w=['lf'])
                P.op('dve', I('tensor_scalar', out=lf[:], in0=lf[:], scalar1=-1.0, scalar2=None, op0=ALU.mult), r=['lf'], w=['lf'])
                P.op('sp', I('dma_start', out=lfT_o[g][:, :], in_=lf[:]), r=['lf'], dma='lfo')

                def splits(width, csdst):
                    P.op('dve', I('tensor_scalar', out=r1[:, 0:width], in0=cum[:, 0:width], scalar1=8.0, scalar2=None, op0=ALU.mult), r=['cum'], w=['r1'])
                    for j in range(3):
                        P.op('dve', I('tensor_copy', out=spl[:, j, 0:width], in_=r1[:, 0:width]), r=['r1'], w=['spl'])
                        if j < 2:
                            P.op('dve', I('tensor_tensor', out=r1[:, 0:width], in0=r1[:, 0:width], in1=spl[:, j, 0:width], op=ALU.subtract), r=['r1', 'spl'], w=['r1'])
                    P.op('dve', I('tensor_scalar', out=spl[:, 3:6, 0:width], in0=spl[:, 0:3, 0:width], scalar1=-1.0, scalar2=None, op0=ALU.mult), r=['spl'], w=['spl'])
                    P.op('sp', I('dma_start', out=csdst[:, :, 0:width], in_=spl[:, :, 0:width]), r=['spl'], dma='cso')
                if LVL < 6:
                    pass
                elif g < 2:
                    P.op('dve', I('tensor_tensor_scan', out=cum[:, 0:T], data0=ones8[:, 0:T], data1=lf[:, :], initial=0.0, op0=ALU.mult, op1=ALU.add), r=['lf', 'ones8'], w=['cum'])
                    splits(T, CS[g])
                else:
                    for s in range(4):
                        P.op('sp', I('dma_start', out=lfp[:], in_=lfT_s[s]), w=['lfp'], dma='lfp')
                        P.op('dve', I('tensor_tensor_scan', out=cum[:, 0:P_LEN], data0=ones8[:, 0:P_LEN], data1=lfp[:, :], initial=0.0, op0=ALU.mult, op1=ALU.add), r=['lfp', 'ones8', 'spl'], w=['cum'])
                        P.op('dve', I('tensor_tensor_scan', out=cum[:, P_LEN:P_LEN + 32], data0=ones8[:, 0:32], data1=lf[:, s * 32:(s + 1) * 32], initial=cum[:, P_LEN - 1:P_LEN], op0=ALU.mult, op1=ALU.add), r=['lf', 'cum'], w=['cum'])
                        splits(P_LEN + 32, CS[2 + s])
                while bg:
                    bgpop()
                P.emit()
            if STOP == f's1g{g}':
                return nc

            with ExitStack() as st:
                def sb(name, shape, dt=F32):
                    _uid[0] += 1
                    return st.enter_context(nc.sbuf_tensor(f"{name}_u{_uid[0]}", list(shape), dt))
                P = Prog(nc, f"s2g{g}")
                if g < 2:
                    Qa = [sb(f"Qa{i}", [128, Tg], BF) for i in range(2)]; Ka = [sb(f"Ka{i}", [128, WK], BF) for i in range(2)]
                    Qd = [[sb(f"Qd{i}_{j}", [128, Tg], BF) for j in range(2)] for i in range(2)]; Kd = [sb(f"Kd{i}", [128, WK], BF) for i in range(2)]
                    Vf = [sb(f"Vf{i}", [128, 17, 66], BF) for i in range(2)]; Vd = [sb(f"Vd{i}", [128, 17, 130], BF) for i in range(2)]
                else:
                    KaA = [sb(f"KaA{i}", [128, 8, WK], BF) for i in range(2)]; KdA = sb("KdA", [128, 4, WK], BF)
                    QaA = [sb(f"QaA{i}", [128, 8, 32], BF) for i in range(2)]; QzA = [sb(f"QzA{i}", [128, 4, 32], BF) for i in range(2)]
                    Vst = [sb(f"Vst{i}", [128, 16, 512], BF) for i in range(2)]
                    VfA = sb("VfA", [128, 17, 8, 66], BF); VdA = sb("VdA", [128, 17, 4, 130], BF)
                NSB = 4; NPT = 4
                PT = [sb(f"PT{i}", [128, 512], BF) for i in range(NPT)]
                OT = sb("OT", [128, 16 if g < 2 else 4, D], BF)
                rec = [sb(f"rec{i}", [128, 4]) for i in range(2)]; nl = [sb(f"nl{i}", [128, 4]) for i in range(2)]
                a32 = [sb(f"a32_{i}", [128, 4, 128]) for i in range(2)]; d32 = [sb(f"d32_{i}", [128, 4, 128]) for i in range(2)]
                mhalf = sb("mhalf", [128, 4])
                P.op('pool', I('memset', mhalf[:], -0.5), w=['mhalf'])
                junk = sb("junk", [128, 128]); ss = [sb(f"ss{i}", [128, 4]) for i in range(2)]; rstd = [sb(f"rstd{i}", [128, 4]) for i in range(2)]
                Sb = [st.enter_context(nc.psum_tensor(f"Sb{i}_" + _pn(), [128, 512], F32)) for i in range(NSB)]
                Ob = [st.enter_context(nc.psum_tensor(f"Ob{i}_" + _pn(), [128, 512], F32)) for i in range(4)]
                if g < 2:
                    for i in range(2):
                        P.op('pool', I('memset', Qa[i][64:128, :], 0.0), w=[('Qa', i)])
                        P.op('sp', I('dma_start', out=Qa[i][67:70, 0:min(Tg, T)], in_=ones3_d[:, 0:min(Tg, T)]), w=[('Qa', i)], dma=('Qa', i))
                        P.op('pool', I('memset', Ka[i][64:128, :], 1.0), w=[('Ka', i)])
                        P.op('pool', I('memset', Qd[i][0][64:128, :], 0.0), w=[('Qd', i)])
                        P.op('pool', I('memset', Qd[i][1][0:64, :], 0.0), w=[('Qd', i)])
                        P.op('pool', I('memset', Vf[i][:, :, 64:66], 1.0), w=[('Vf', i)])
                        P.op('pool', I('memset', Vd[i][:, :, 128:130], 1.0), w=[('Vd', i)])
                else:
                    for i in range(2):
                        P.op('pool', I('memset', QaA[i][64:128, :, :], 0.0), w=[('Qa', i)])
                        P.op('sp', I('dma_start', out=QaA[i][67:70, :, :], in_=ones3_d[:, 0:256].rearrange("j (h t) -> j h t", h=8)), w=[('Qa', i)], dma=('Qa', i))
                        P.op('pool', I('memset', KaA[i][64:128, :, :], 1.0), w=[('Ka', i)])
                    P.op('pool', I('memset', QzA[0][64:128, :, :], 0.0), w=[('Qd', 0)])
                    P.op('pool', I('memset', QzA[1][0:64, :, :], 0.0), w=[('Qd', 0)])
                    P.op('pool', I('memset', VfA[:, :, :, 64:66], 1.0), w=[('Vf', 0)])
                    P.op('pool', I('memset', VdA[:, :, :, 128:130], 1.0), w=[('Vd', 0)])
                sctr = [0]; pctr = [0]; uctr = [0]
                bg2 = []
                glist = [int(c_) for c_ in os.environ.get('KG', '012')]
                if g < 2 and 2 in glist:
                    mine = [0, 1] if (g == 0 and 1 in glist) else ([2, 3] if g == 1 and 0 in glist else [0, 1, 2, 3])
                    for s_ in mine:
                        bg2.append(I('dma_start', out=KaS[s_][:, :], in_=fkT_s[s_]))
                        bg2.append(I('dma_start', out=VfS[s_][:, :], in_=fv_s[s_]))
                        bg2.append(I('dma_start', out=KdS[s_][:, :], in_=dkT_s[s_]))
                        bg2.append(I('dma_start', out=VdS[s_][:, :], in_=dv_s[s_]))
                bg2n = [0]; bg2c = [0]

                def bg2pop(force=False):
                    bg2c[0] += 1
                    if bg2 and (force or bg2c[0] % 12 == 0):
                        P.op('pool', bg2.pop(0), dma=('bgc', bg2n[0] % 8)); bg2n[0] += 1
                nseq = 1 if g < 2 else 4
                Lq = Tg if g < 2 else 32
                npast = 0 if g < 2 else 16
                for s in range(nseq):
                    cs = CS[g] if g < 2 else CS[2 + s]
                    qc0 = 0 if g < 2 else s * 32
                    qpos0 = 0 if g < 2 else P_LEN
                    nqb = Lq // 512 if g < 2 else 1
                    QB = 512 if g < 2 else 32

                    def attend(kind, h, b, ov=None):
                        subs = (0,) if kind == 'f' else (0, 1)
                        if ov is None:
                            Kt = Ka[b] if kind == 'f' else Kd[b]
                            Qts = [Qa[b]] if kind == 'f' else Qd[b]
                            Vt = Vf[b] if kind == 'f' else Vd[b]
                            kr = ('Ka', b) if kind == 'f' else ('Kd', b)
                            qr = ('Qa', b) if kind == 'f' else ('Qd', b)
                            vr = ('Vf', b) if kind == 'f' else ('Vd', b)
                        else:
                            Kt, Qts, Vt, kr, qr, vr = ov
                        KR = 128
                        VW = 65 if kind == 'f' else 129
                        for qb in range(nqb):
                            u = uctr[0] % 2; uctr[0] += 1
                            for sub in subs:
                                pb0 = 0
                                Qt = Qts[sub]
                                if g < 2:
                                    kts = list(range(0, 4 * qb + 4))
                                else:
                                    kts = list(range(17))
                                if kind == 'f':
                                    obk = [Ob[u * 2]] * 4 if g < 2 else [Ob[u * 2]]
                                    obn = [u * 2] * 4
                                    ocol = [qt * 65 for qt in range(4)]
                                else:
                                    obn = [sub * 2 + qt // 2 for qt in range(4)]
                                    obk = [Ob[n] for n in obn]
                                    ocol = [(qt % 2) * 129 for qt in range(4)]
                                started = set()
                                if g == 2:
                                    sbk = sctr[0] % NSB; sctr[0] += 1
                                    pt = pctr[0] % NPT; pctr[0] += 1
                                    for kt in range(16):
                                        P.op('pe', I('matmul', Sb[sbk][:, kt * 32:(kt + 1) * 32], lhsT=Kt[pb0:pb0 + KR, kt * 128:(kt + 1) * 128], rhs=Qt[pb0:pb0 + KR, 0:32], start=True, stop=True),
                                             r=[kr, qr], w=[('S', sbk)])
                                    P.op('act', I('activation', out=PT[pt][:, :], in_=Sb[sbk][:, :], func=AF.Exp, scale=0.125), r=[('S', sbk)], w=[('PT', pt)])
                                    for kt in range(16):
                                        P.op('pe', I('matmul', obk[0][0:32, ocol[0]:ocol[0] + VW], lhsT=PT[pt][:, kt * 32:(kt + 1) * 32], rhs=Vt[:, kt, 0:VW], start=(kt == 0), stop=False, skip_group_check=True),
                                             r=[('PT', pt), vr], w=[('O', obn[0])])
                                    sbk = sctr[0] % NSB; sctr[0] += 1
                                    pt = pctr[0] % NPT; pctr[0] += 1
                                    P.op('pe', I('matmul', Sb[sbk][0:32, 0:32], lhsT=Kt[pb0:pb0 + KR, P_LEN:P_LEN + 32], rhs=Qt[pb0:pb0 + KR, 0:32], start=True, stop=True),
                                         r=[kr, qr], w=[('S', sbk)])
                                    P.op('act', I('activation', out=PT[pt][0:32, 0:32], in_=Sb[sbk][0:32, 0:32], func=AF.Exp, scale=0.125), r=[('S', sbk)], w=[('PT', pt)])
                                    if kind == 'f':
                                        P.op('pool', I('tensor_tensor', out=PT[pt][0:32, 0:32], in0=PT[pt][0:32, 0:32], in1=tri[0:32, 0:32], op=ALU.mult), r=[('PT', pt)], w=[('PT', pt)])
                                    P.op('pe', I('matmul', obk[0][0:32, ocol[0]:ocol[0] + VW], lhsT=PT[pt][0:32, 0:32], rhs=Vt[0:32, 16, 0:VW], start=False, stop=True, skip_group_check=True),
                                         r=[('PT', pt), vr], w=[('O', obn[0])])
                                else:
                                    recs = []

                                    def front(kt):
                                        j = kt - 4 * qb
                                        qoff = max(j, 0) * 128; nq = 512 - qoff
                                        sbk = sctr[0] % NSB; sctr[0] += 1
                                        pt = pctr[0] % NPT; pctr[0] += 1
                                        P.op('pe', I('matmul', Sb[sbk][:, 0:nq], lhsT=Kt[pb0:pb0 + KR, kt * 128:(kt + 1) * 128], rhs=Qt[pb0:pb0 + KR, qb * 512 + qoff:(qb + 1) * 512], start=True, stop=True),
                                             r=[kr, qr], w=[('S', sbk)])
                                        P.op('act', I('activation', out=PT[pt][:, 0:nq], in_=Sb[sbk][:, 0:nq], func=AF.Exp, scale=0.125), r=[('S', sbk)], w=[('PT', pt)])
                                        if j >= 0:
                                            if kind == 'f':
                                                P.op('pool', I('tensor_tensor', out=PT[pt][:, 0:128], in0=PT[pt][:, 0:128], in1=tri[:, :], op=ALU.mult), r=[('PT', pt)], w=[('PT', pt)])
                                            else:
                                                P.op('pool', I('memset', PT[pt][64:128, 0:64], 0.0), r=[('PT', pt)], w=[('PT', pt)])
                                            bg2pop()
                                        return (kt, j, pt)

                                    def back(rc_):
                                        kt, j, pt = rc_
                                        for qt in range(max(j, 0), 4):
                                            cc = (qt - max(j, 0)) * 128
                                            first = obn[qt] not in started
                                            started.add(obn[qt])
                                            P.op('pe', I('matmul', obk[qt][:, ocol[qt]:ocol[qt] + VW], lhsT=PT[pt][:, cc:cc + 128], rhs=Vt[:, kt, 0:VW], start=first, stop=(kt == 4 * qb + qt), skip_group_check=True),
                                                 r=[('PT', pt), vr], w=[('O', obn[qt])])
                                    LA = 3
                                    for idx, kt in enumerate(kts):
                                        recs.append(front(kt))
                                        if idx >= LA:
                                            back(recs[idx - LA])
                                    for rc_ in recs[max(0, len(kts) - LA):]:
                                        back(rc_)
                                nqt = 4 if g < 2 else 1
                                rows = 128 if g < 2 else 32
                                for qt in range(nqt):
                                    P.op('dve', I('reciprocal', out=rec[u][0:rows, qt:qt + 1], in_=obk[qt][0:rows, ocol[qt] + VW - 1:ocol[qt] + VW]), r=[('O', obn[qt])], w=[('rec', u)])
                                tile0 = qb * 4 if g < 2 else s
                                if kind == 'f':
                                    for qt in range(nqt):
                                        P.op('dve', I('tensor_scalar', out=OT[0:rows, tile0 + qt, h * 64:(h + 1) * 64], in0=obk[qt][0:rows, ocol[qt]:ocol[qt] + 64], scalar1=rec[u][0:rows, qt:qt + 1], scalar2=None, op0=ALU.mult),
                                             r=[('O', obn[qt]), ('rec', u)], w=[('OT', tile0 + qt, kind, h)])
                                elif sub == 0:
                                    for qt in range(nqt):
                                        P.op('dve', I('tensor_scalar', out=a32[u][0:rows, qt, :], in0=obk[qt][0:rows, ocol[qt]:ocol[qt] + 128], scalar1=rec[u][0:rows, qt:qt + 1], scalar2=None, op0=ALU.mult),
                                             r=[('O', obn[qt]), ('rec', u)], w=[('a32', u)])
                                else:
                                    P.op('dve', I('tensor_scalar', out=nl[u][0:rows, 0:nqt], in0=rec[u][0:rows, 0:nqt], scalar1=neg_lam[0:rows, 0:1], scalar2=None, op0=ALU.mult), r=[('rec', u)], w=[('nl', u)])
                                    for qt in range(nqt):
                                        P.op('dve', I('scalar_tensor_tensor', out=d32[u][0:rows, qt, :], in0=obk[qt][0:rows, ocol[qt]:ocol[qt] + 128], scalar=nl[u][0:rows, qt:qt + 1], in1=a32[u][0:rows, qt, :], op0=ALU.mult, op1=ALU.add),
                                             r=[('O', obn[qt]), ('nl', u), ('a32', u)], w=[('d32', u)])
                                        P.op('dve', I('scalar_tensor_tensor', out=junk[0:rows, :], in0=d32[u][0:rows, qt, :], scalar=1.0 / 128.0, in1=d32[u][0:rows, qt, :], op0=ALU.mult, op1=ALU.mult, accum_out=ss[u][0:rows, qt:qt + 1]), r=[('d32', u)], w=['junk', ('ss', u)])
                                    P.op('dve', I('tensor_scalar', out=ss[u][0:rows, 0:nqt], in0=ss[u][0:rows, 0:nqt], scalar1=1e-6, scalar2=None, op0=ALU.add), r=[('ss', u)], w=[('ss', u)])
                                    P.op('pool', I('tensor_tensor', out=rstd[u][0:rows, 0:nqt], in0=ss[u][0:rows, 0:nqt], in1=mhalf[0:rows, 0:nqt], op=ALU.pow), r=[('ss', u)], w=[('rstd', u)])
                                    for qt in range(nqt):
                                        P.op('dve', I('tensor_scalar', out=OT[0:rows, tile0 + qt, 512 + h * 128:512 + (h + 1) * 128], in0=d32[u][0:rows, qt, :], scalar1=rstd[u][0:rows, qt:qt + 1], scalar2=None, op0=ALU.mult),
                                             r=[('d32', u), ('rstd', u)], w=[('OT', tile0 + qt, kind, h)])

                    if g == 2:
                        b = s % 2
                        pre_ = (0 in glist or 1 in glist)

                        def ld_ka(s_):
                            b_ = s_ % 2; cs_ = CS[2 + s_]; c0_ = s_ * 32
                            if pre_:
                                P.op('sp', I('dma_start', out=KaA[b_][0:64, :, 0:P_LEN], in_=KaS[s_][:, :].rearrange("(h d) t -> d h t", d=64)), w=[('Ka', b_)], dma=('Ka', b_))
                            else:
                                P.op('pool', I('dma_start', out=KaA[b_][0:64, :, 0:P_LEN], in_=fkT_s[s_].rearrange("(h d) t -> d h t", d=64)), w=[('Ka', b_)], dma=('Ka', b_))
                            P.op('sp', I('dma_start', out=KaA[b_][0:64, :, P_LEN:P_LEN + 32], in_=KF[g][:, c0_:c0_ + 32].rearrange("(h d) t -> d h t", d=64)), w=[('Ka', b_)], dma=('Ka', b_))
                            P.op('sp', I('dma_start', out=KaA[b_][67:70, :, 0:P_LEN + 32], in_=cs_[:, 3:6, 0:P_LEN + 32].rearrange("h j t -> j h t")), w=[('Ka', b_)], dma=('Ka', b_))
                            P.op('sp', I('dma_start', out=QaA[b_][0:64, :, :], in_=QF[g][:, c0_:c0_ + 32].rearrange("(h d) t -> d h t", d=64)), w=[('Qa', b_)], dma=('Qa', b_))
                            P.op('sp', I('dma_start', out=QaA[b_][64:67, :, :], in_=cs_[:, 0:3, P_LEN:P_LEN + 32].rearrange("h j t -> j h t")), w=[('Qa', b_)], dma=('Qa', b_))

                        def ld_vst(s_, which):
                            if pre_:
                                src_ = VfS if which == 0 else VdS
                                P.op('sp', I('dma_start', out=Vst[which][:, :, :], in_=src_[s_][:, :].rearrange("(t p) c -> p t c", p=128)), w=[('Vst', which)], dma=('Vst', which))
                            else:
                                src_ = fv_s if which == 0 else dv_s
                                P.op('pool', I('dma_start', out=Vst[which][:, :, :], in_=src_[s_].rearrange("(t p) c -> p t c", p=128)), w=[('Vst', which)], dma=('Vst', which))
                        if s == 0:
                            ld_ka(0); ld_vst(0, 0); ld_vst(0, 1)
                        c0 = s * 32
                        P.op('pool', I('tensor_copy', out=VfA[:, 0:16, :, 0:64], in_=Vst[0][:, :, :].rearrange("p t (h c) -> p t h c", h=8)), r=[('Vst', 0)], w=[('Vf', 0)])
                        P.op('sp', I('dma_start', out=VfA[0:32, 16, :, 0:64], in_=VF[g][c0:c0 + 32, :].rearrange("t (h c) -> t h c", h=8)), w=[('Vf', 0)], dma=('Vf', 0))
                        if s + 1 < 4:
                            ld_ka(s + 1); ld_vst(s + 1, 0)
                        if pre_:
                            P.op('sp', I('dma_start', out=KdA[:, :, 0:P_LEN], in_=KdS[s][:, :].rearrange("(h d) t -> d h t", d=128)), w=[('Kd', 0)], dma=('Kd', 0))
                        else:
                            P.op('pool', I('dma_start', out=KdA[:, :, 0:P_LEN], in_=dkT_s[s].rearrange("(h d) t -> d h t", d=128)), w=[('Kd', 0)], dma=('Kd', 0))
                        P.op('sp', I('dma_start', out=KdA[:, :, P_LEN:P_LEN + 32], in_=KD[g][:, c0:c0 + 32].rearrange("(h d) t -> d h t", d=128)), w=[('Kd', 0)], dma=('Kd', 0))
                        qv_ = QD[g][:, c0:c0 + 32].rearrange("(h m d) t -> m d h t", m=2, d=64)
                        P.op('sp', I('dma_start', out=QzA[0][0:64, :, :], in_=qv_[0]), w=[('Qd', 0)], dma=('Qd', 0))
                        P.op('sp', I('dma_start', out=QzA[1][64:128, :, :], in_=qv_[1]), w=[('Qd', 0)], dma=('Qd', 0))
                        P.op('pool', I('tensor_copy', out=VdA[:, 0:16, :, 0:128], in_=Vst[1][:, :, :].rearrange("p t (h c) -> p t h c", h=4)), r=[('Vst', 1)], w=[('Vd', 0)])
                        P.op('sp', I('dma_start', out=VdA[0:32, 16, :, 0:128], in_=VD[g][c0:c0 + 32, :].rearrange("t (h c) -> t h c", h=4)), w=[('Vd', 0)], dma=('Vd', 0))
                        if s + 1 < 4:
                            ld_vst(s + 1, 1)
                        for h in range(8):
                            attend('f', h, b, ov=(KaA[b][:, h, :], [QaA[b][:, h, :]], VfA[:, :, h, :], ('Ka', b), ('Qa', b), ('Vf', 0)))
                        for h in range(4):
                            attend('d', h, 0, ov=(KdA[:, h, :], [QzA[0][:, h, :], QzA[1][:, h, :]], VdA[:, :, h, :], ('Kd', 0), ('Qd', 0), ('Vd', 0)))
                    hctr = 0
                    for kind, nh in ((('d', 4), ('f', 8)) if g < 2 else ()):
                        for h in range(nh):
                            b = hctr % 2; hctr += 1
                            if kind == 'f':
                                P.op('sp', I('dma_start', out=Qa[b][0:64, 0:Lq], in_=QF[g][h * 64:(h + 1) * 64, qc0:qc0 + Lq]), w=[('Qa', b)], dma=('Qa', b))
                                P.op('sp', I('dma_start', out=Qa[b][64:67, 0:Lq], in_=cs[h, 0:3, qpos0:qpos0 + Lq]), w=[('Qa', b)], dma=('Qa', b))
                                if g < 2:
                                    P.op('sp', I('dma_start', out=Ka[b][0:64, 0:T], in_=KF[g][h * 64:(h + 1) * 64, :]), w=[('Ka', b)], dma=('Ka', b))
                                    P.op('sp', I('dma_start', out=Ka[b][67:70, 0:T], in_=cs[h, 3:6, 0:T]), w=[('Ka', b)], dma=('Ka', b))
                                    P.op('sp', I('dma_start', out=Vf[b][:, 0:16, 0:64], in_=VF[g][:, h * 64:(h + 1) * 64].rearrange("(t p) c -> p t c", p=128)), w=[('Vf', b)], dma=('Vf', b))
                                else:
                                    P.op('pool', I('dma_start', out=Ka[b][0:64, 0:P_LEN], in_=fkT_s[s, h * 64:(h + 1) * 64, :]), w=[('Ka', b)], dma=('Ka', b))
                                    P.op('sp', I('dma_start', out=Ka[b][0:64, P_LEN:P_LEN + 32], in_=KF[g][h * 64:(h + 1) * 64, qc0:qc0 + 32]), w=[('Ka', b)], dma=('Ka', b))
                                    P.op('sp', I('dma_start', out=Ka[b][67:70, 0:P_LEN + 32], in_=cs[h, 3:6, 0:P_LEN + 32]), w=[('Ka', b)], dma=('Ka', b))
                                    P.op('pool', I('dma_start', out=Vf[b][:, 0:16, 0:64], in_=fv_s[s, :, h * 64:(h + 1) * 64].rearrange("(t p) c -> p t c", p=128)), w=[('Vf', b)], dma=('Vf', b))
                                    P.op('sp', I('dma_start', out=Vf[b][0:32, 16, 0:64], in_=VF[g][qc0:qc0 + 32, h * 64:(h + 1) * 64]), w=[('Vf', b)], dma=('Vf', b))
                            else:
                                P.op('sp', I('dma_start', out=Qd[b][0][0:64, 0:Lq], in_=QD[g][h * 128:h * 128 + 64, qc0:qc0 + Lq]), w=[('Qd', b)], dma=('Qd', b))
                                P.op('sp', I('dma_start', out=Qd[b][1][64:128, 0:Lq], in_=QD[g][h * 128 + 64:(h + 1) * 128, qc0:qc0 + Lq]), w=[('Qd', b)], dma=('Qd', b))
                                if g < 2:
                                    P.op('sp', I('dma_start', out=Kd[b][:, 0:T], in_=KD[g][h * 128:(h + 1) * 128, :]), w=[('Kd', b)], dma=('Kd', b))
                                    P.op('sp', I('dma_start', out=Vd[b][:, 0:16, 0:128], in_=VD[g][:, h * 128:(h + 1) * 128].rearrange("(t p) c -> p t c", p=128)), w=[('Vd', b)], dma=('Vd', b))
                                else:
                                    P.op('pool', I('dma_start', out=Kd[b][:, 0:P_LEN], in_=dkT_s[s, h * 128:(h + 1) * 128, :]), w=[('Kd', b)], dma=('Kd', b))
                                    P.op('sp', I('dma_start', out=Kd[b][:, P_LEN:P_LEN + 32], in_=KD[g][h * 128:(h + 1) * 128, qc0:qc0 + 32]), w=[('Kd', b)], dma=('Kd', b))
                                    P.op('pool', I('dma_start', out=Vd[b][:, 0:16, 0:128], in_=dv_s[s, :, h * 128:(h + 1) * 128].rearrange("(t p) c -> p t c", p=128)), w=[('Vd', b)], dma=('Vd', b))
                                    P.op('sp', I('dma_start', out=Vd[b][0:32, 16, 0:128], in_=VD[g][qc0:qc0 + 32, h * 128:(h + 1) * 128]), w=[('Vd', b)], dma=('Vd', b))
                            attend(kind, h, b)
                    if g < 2:
                        allot = [('OT', tt, 'f', h) for tt in range(16) for h in range(8)] + [('OT', tt, 'd', h) for tt in range(16) for h in range(4)]
                        P.op('sp', I('dma_start', out=OTOK[g][:, :].rearrange("(t p) c -> p t c", p=128), in_=OT[:, :, :]), r=allot, dma='oto')
                    else:
                        allot = [('OT', s, 'f', h) for h in range(8)] + [('OT', s, 'd', h) for h in range(4)]
                        P.op('sp', I('dma_start', out=OTOK[g][s * 32:(s + 1) * 32, :], in_=OT[0:32, s, :]), r=allot, dma=('oto', s))
                while bg2:
                    bg2pop(force=True)
                P.emit()
            if STOP == f's2g{g}':
                return nc

            with ExitStack() as st:
                def sb(name, shape, dt=F32):
                    _uid[0] += 1
                    return st.enter_context(nc.sbuf_tensor(f"{name}_u{_uid[0]}", list(shape), dt))
                P = Prog(nc, f"s3g{g}")
                nt = N // 128
                nseg = 1 if g < 2 else 4
                L = N // nseg
                wo = sb("wo", [128, 8, D], BF); wd = sb("wd", [128, NF, D], BF)
                wu = [sb(f"wu{i}", [128, 8, 256], BF) for i in range(3)]
                xtokL = [sb(f"xtok{i}", [128, nt, D]) for i in range(2)]; otkL = [sb(f"otk{i}", [128, nt, D], BF) for i in range(2)]
                oT = sb("oT", [128, 8, N], BF); u2T = sb("u2T", [128, 8, N], BF); hT = sb("hT", [128, NF, N], BF)
                x1b = [sb(f"x1b{i}", [128, D], BF) for i in range(2)]; tmpL = [sb(f"tmp{i}", [128, D]) for i in range(2)]
                apad = [sb(f"apad{i}", [128, nseg, L + 2]) for i in range(2)]
                acc = [sb(f"acc{i}", [128, nseg, L]) for i in range(2)]; sg = [sb(f"sg{i}", [128, nseg, L], BF) for i in range(2)]
                halo = sb("halo", [128, NF, nseg, 2])
                gbc = sb("gbc", [128, 2, D])
                bstL = [sb(f"bst{i}", [128, 2, 6]) for i in range(2)]; mvL = [sb(f"mv{i}", [128, 2]) for i in range(2)]; sdL = [sb(f"sd{i}", [128, 1]) for i in range(2)]; nmrL = [sb(f"nmr{i}", [128, 1]) for i in range(2)]
                ps = [st.enter_context(nc.psum_tensor(f"q{i}_" + _pn(), [128, 512], F32)) for i in range(5)]
                tp = [st.enter_context(nc.psum_tensor(f"tp{i}_" + _pn(), [128, 1024], BF)) for i in range(3)]
                pctr = [0]

                def nbank():
                    pctr[0] += 1
                    return pctr[0] % 5
                xsrc = x_p[g] if g < 2 else x_s
                ydst = y_p[g] if g < 2 else y_s
                wctr = [0]

                lnctr = [0]

                def layernorm(lni, xtok, xk, tt=0):
                    pq = lnctr[0] % 2; lnctr[0] += 1
                    bst = bstL[pq]; mv = mvL[pq]; sd = sdL[pq]; nmr = nmrL[pq]
                    for hh in range(2):
                        P.op('dve', I('bn_stats', out=bst[:, hh, :], in_=xtok[:, tt, hh * 512:(hh + 1) * 512]), r=[('xtok', xk, tt)], w=[('bst', pq)])
                    P.op('dve', I('bn_aggr', out=mv[:], in_=bst[:].rearrange("p a b -> p (a b)")), r=[('bst', pq)], w=[('mv', pq)])
                    P.op('act', I('activation', out=sd[:], in_=mv[:, 1:2], func=AF.Sqrt, bias=1e-5, scale=1.0), r=[('mv', pq)], w=[('sd', pq)])
                    P.op('dve', I('reciprocal', out=sd[:], in_=sd[:]), r=[('sd', pq)], w=[('sd', pq)])
                    P.op('dve', I('tensor_scalar', out=nmr[:], in0=mv[:, 0:1], scalar1=sd[:, 0:1], scalar2=-1.0, op0=ALU.mult, op1=ALU.mult), r=[('mv', pq), ('sd', pq)], w=[('nmr', pq)])
                    P.op('act', I('activation', out=xtok[:, tt, :], in_=xtok[:, tt, :], func=AF.Identity, scale=sd[:, 0:1], bias=nmr[:, 0:1]), r=[('xtok', xk, tt), ('sd', pq), ('nmr', pq)], w=[('xtok', xk, tt)])
                    P.op('pool', I('tensor_tensor', out=xtok[:, tt, :], in0=xtok[:, tt, :], in1=lnbc[:, lni, :], op=ALU.mult), r=[('xtok', xk, tt)], w=[('xtok', xk, tt)])
                    P.op('pool', I('tensor_tensor', out=xtok[:, tt, :], in0=xtok[:, tt, :], in1=lnbc[:, lni + 1, :], op=ALU.add), r=[('xtok', xk, tt)], w=[('xtok', xk, tt)])

                def load_blk(blk):
                    xk = blk % 2
                    t0_ = blk * N
                    P.op('sp', I('dma_start', out=xtokL[xk][:], in_=xsrc[t0_:t0_ + N, :].rearrange("(t p) c -> p t c", p=128)), w=[('xtok', xk, tt) for tt in range(nt)], dma=('xtok', xk))
                    P.op('sp', I('dma_start', out=otkL[xk][:], in_=OTOK[g][t0_:t0_ + N, :].rearrange("(t p) c -> p t c", p=128)), w=[('otk', xk)], dma=('otk', xk))
                load_blk(0)
                P.op('sp', I('dma_start', out=wo[:], in_=wo4[:, :, :]), w=['wo'], dma='wo')
                P.op('act', I('activation', out=wo[:, 4:8, :], in_=wo[:, 4:8, :], func=AF.Identity, scale=wrs[:, 0:1]), r=['wo'], w=['wo'])
                if g < 2:
                    P.op('pool', I('memset', halo[:], 0.0), w=['halo'])
                else:
                    P.op('sp', I('dma_start', out=halo[:], in_=convT_s[:, :, :, :]), w=['halo'], dma='halo')
                for gi in range(2):
                    if g < 2:
                        P.op('sp', I('dma_start', out=gbc[:, gi, :], in_=MG[g, gi * 1024:(gi + 1) * 1024].partition_broadcast(128)), w=['gbc'], dma='gbc')
                    else:
                        for s in range(4):
                            P.op('sp', I('dma_start', out=gbc[s * 32:(s + 1) * 32, gi, :], in_=MG[2 + s, gi * 1024:(gi + 1) * 1024].partition_broadcast(32)), w=['gbc'], dma='gbc')
                P.op('sp', I('dma_start', out=wd[:], in_=wd4[:, :, :]), w=['wd'], dma='wd')
                for blk in range(nblk):
                    t0 = blk * N
                    xk = blk % 2
                    xtok = xtokL[xk]; otk = otkL[xk]
                    if blk + 1 < nblk:
                        load_blk(blk + 1)

                    bsA = {}

                    def phaseA(tt):
                        tb = tt % 2
                        tmp = tmpL[tt % 2]; tk = ('tmp', tt % 2)
                        for k in range(8):
                            P.op('pe', I('transpose', tp[tb][:, k * 128:(k + 1) * 128], otk[:, tt, k * 128:(k + 1) * 128], ident[:]), r=[('otk', xk), 'ident'], w=[('tp', tb)])
                        P.op('act', I('activation', out=oT[:, :, tt * 128:(tt + 1) * 128], in_=tp[tb][:, :].rearrange("p (k t) -> p k t", k=8), func=AF.Identity), r=[('tp', tb)], w=[('oT', tt)])
                        bs = []
                        for half in range(2):
                            b = nbank(); bs.append(b)
                            for k in range(8):
                                P.op('pe', I('matmul', ps[b][:, :], lhsT=oT[:, k, tt * 128:(tt + 1) * 128], rhs=wo[:, k, half * 512:(half + 1) * 512], start=(k == 0), stop=(k == 7)),
                                     r=[('oT', tt), 'wo'], w=[('ps', b)])
                        bsA[tt] = bs

                    def phaseL(tt):
                        tmp = tmpL[tt % 2]; tk = ('tmp', tt % 2)
                        bs = bsA[tt]
                        for half in range(2):
                            P.op('dve', I('tensor_tensor', out=tmp[:, half * 512:(half + 1) * 512], in0=ps[bs[half]][:, :], in1=gbc[:, 0, half * 512:(half + 1) * 512], op=ALU.mult), r=[('ps', bs[half]), 'gbc'], w=[tk])
                        P.op('dve', I('scalar_tensor_tensor', out=xtok[:, tt, :], in0=xtok[:, tt, :], scalar=ALPHA, in1=tmp[:], op0=ALU.mult, op1=ALU.add), r=[('xtok', xk, tt), tk], w=[('xtok', xk, tt)])
                        layernorm(0, xtok, xk, tt=tt)
                        P.op('act', I('activation', out=x1b[tt % 2][:], in_=xtok[:, tt, :], func=AF.Identity), r=[('xtok', xk, tt)], w=[('x1b', tt % 2)])

                    def phaseB(tt):
                        tb2 = 2
                        for k in range(8):
                            P.op('pe', I('transpose', tp[tb2][:, k * 128:(k + 1) * 128], x1b[tt % 2][:, k * 128:(k + 1) * 128], ident[:]), r=[('x1b', tt % 2), 'ident'], w=[('tp', tb2)])
                        for k in range(8):
                            if g < 2:
                                P.op('dve', I('tensor_scalar', out=u2T[:, k, tt * 128:(tt + 1) * 128], in0=tp[tb2][:, k * 128:(k + 1) * 128], scalar1=sc2p[:, k, g:g + 1], scalar2=modF[:, 24 + k, g:g + 1], op0=ALU.mult, op1=ALU.add),
                                     r=[('tp', tb2)], w=['u2T'])
                            else:
                                for s in range(4):
                                    P.op('dve', I('tensor_scalar', out=u2T[:, k, s * 32:(s + 1) * 32], in0=tp[tb2][:, k * 128 + s * 32:k * 128 + (s + 1) * 32], scalar1=sc2p[:, k, 2 + s:3 + s], scalar2=modF[:, 24 + k, 2 + s:3 + s], op0=ALU.mult, op1=ALU.add),
                                         r=[('tp', tb2)], w=['u2T'])
                    seq_ = []
                    for tt in range(nt):
                        seq_.append(('A', tt))
                        if tt >= 1:
                            seq_.append(('L', tt - 1))
                        if tt >= 2:
                            seq_.append(('B', tt - 2))
                    seq_ += [('L', nt - 1)]
                    if nt >= 2:
                        seq_ += [('B', nt - 2)]
                    seq_ += [('B', nt - 1)]
                    for kind_, tt_ in seq_:
                        {'A': phaseA, 'L': phaseL, 'B': phaseB}[kind_](tt_)
                    if L3 < 4:
                        continue
                    for f in range(NF):
                        sl = wctr[0] % 3; wctr[0] += 1
                        P.op('sp', I('dma_start', out=wu[sl][:, :, :], in_=wu4[f]), w=[('wu', sl)], dma=('wu', sl))
                        ba = nbank(); bb = nbank()
                        for k in range(8):
                            P.op('pe', I('matmul', ps[ba][:, 0:N], lhsT=wu[sl][:, k, 0:128], rhs=u2T[:, k, :], start=(k == 0), stop=(k == 7)), r=[('wu', sl), 'u2T'], w=[('ps', ba)])
                        for k in range(8):
                            P.op('pe', I('matmul', ps[bb][:, 0:N], lhsT=wu[sl][:, k, 128:256], rhs=u2T[:, k, :], start=(k == 0), stop=(k == 7)), r=[('wu', sl), 'u2T'], w=[('ps', bb)])
                        ab = f % 2
                        P.op('act', I('activation', out=apad[ab][:, :, 2:L + 2], in_=ps[ba][:, 0:N].rearrange("p (s l) -> p s l", s=nseg), func=AF.Identity), r=[('ps', ba)], w=[('apad', ab)])
                        P.op('pool', I('tensor_copy', out=apad[ab][:, :, 0:2], in_=halo[:, f, :, :]), r=[('halo', f)], w=[('apad', ab)])
                        P.op('dve', I('tensor_scalar', out=acc[ab][:], in0=apad[ab][:, :, 0:L], scalar1=cw[:, f, 0:1], scalar2=cw[:, f, 3:4], op0=ALU.mult, op1=ALU.add), r=[('apad', ab)], w=[('acc', ab)])
                        P.op('dve', I('scalar_tensor_tensor', out=acc[ab][:], in0=apad[ab][:, :, 1:L + 1], scalar=cw[:, f, 1:2], in1=acc[ab][:], op0=ALU.mult, op1=ALU.add), r=[('apad', ab), ('acc', ab)], w=[('acc', ab)])
                        P.op('dve', I('scalar_tensor_tensor', out=acc[ab][:], in0=apad[ab][:, :, 2:L + 2], scalar=cw[:, f, 2:3], in1=acc[ab][:], op0=ALU.mult, op1=ALU.add), r=[('apad', ab), ('acc', ab)], w=[('acc', ab)])
                        P.op('pool', I('tensor_copy', out=halo[:, f, :, :], in_=apad[ab][:, :, L:L + 2]), r=[('apad', ab)], w=[('halo', f)])
                        P.op('act', I('activation', out=sg[ab][:], in_=acc[ab][:], func=AF.Silu), r=[('acc', ab)], w=[('sg', ab)])
                        P.op('dve', I('tensor_tensor', out=hT[:, f, :].rearrange("p (s l) -> p s l", s=nseg), in0=sg[ab][:], in1=ps[bb][:, 0:N].rearrange("p (s l) -> p s l", s=nseg), op=ALU.mult), r=[('sg', ab), ('ps', bb)], w=['hT'])
                    if L3 < 5:
                        continue
                    for tt in range(nt):
                        tmp = tmpL[tt % 2]; tk = ('tmp', tt % 2)
                        bs = []
                        for half in range(2):
                            b = nbank(); bs.append(b)
                            for f in range(NF):
                                P.op('pe', I('matmul', ps[b][:, :], lhsT=hT[:, f, tt * 128:(tt + 1) * 128], rhs=wd[:, f, half * 512:(half + 1) * 512], start=(f == 0), stop=(f == NF - 1)),
                                     r=['hT', 'wd'], w=[('ps', b)])
                        for half in range(2):
                            P.op('dve', I('tensor_tensor', out=tmp[:, half * 512:(half + 1) * 512], in0=ps[bs[half]][:, :], in1=gbc[:, 1, half * 512:(half + 1) * 512], op=ALU.mult), r=[('ps', bs[half]), 'gbc'], w=[tk])
                        P.op('dve', I('scalar_tensor_tensor', out=xtok[:, tt, :], in0=xtok[:, tt, :], scalar=ALPHA, in1=tmp[:], op0=ALU.mult, op1=ALU.add), r=[('xtok', xk, tt), tk], w=[('xtok', xk, tt)])
                        layernorm(2, xtok, xk, tt=tt)
                        P.op('sp', I('dma_start', out=ydst[t0 + tt * 128:t0 + (tt + 1) * 128, :], in_=xtok[:, tt, :]), r=[('xtok', xk, tt)], dma=('yo', tt))
                P.op('sp', I('dma_start', out=convT_o[g][:, :, :, :], in_=halo[:]), r=[('halo', f) for f in range(NF)], dma='cvo')
                P.emit()
            if STOP == f's3g{g}':
                return nc
    return nc


def _rope_tables(pos):
    d = 64
    inv = (10000.0 ** (-np.arange(0, d, 2, dtype=np.float32) / d)).astype(np.float32)
    ang = pos.astype(np.float32)[None, :] * inv[:, None]
    cos = np.cos(ang).astype(np.float32); sin = np.sin(ang).astype(np.float32)
    C = np.concatenate([cos, cos, cos, cos], axis=0)
    S = np.concatenate([-sin, sin, -sin, sin], axis=0)
    return np.ascontiguousarray(C), np.ascontiguousarray(S)


_NC = None
_PREP_ONLY = False


def kernel(x_prompt, x_sample, c_prompt, c_sample, cache_fox_k, cache_fox_v, cache_fox_logf,
           cache_diff_k, cache_diff_v, state_ffn_conv, w_ada, b_ada, w_in, b_f, lambda_vecs,
           subln_g, w_o, ln1_g, ln1_b, w_up, conv_w, conv_b, w_down, ln2_g, ln2_b):
    global _NC
    f32 = np.float32
    A = lambda a: np.ascontiguousarray(np.asarray(a, dtype=f32))
    x_prompt = A(x_prompt); x_sample = A(x_sample)
    w_in0 = A(w_in)[0]
    fq = w_in0[:, 0:512]; fk = w_in0[:, 512:1024]; fv = w_in0[:, 1024:1536]; ff = w_in0[:, 1536:1544]
    dq = w_in0[:, 1544:2056]; dk = w_in0[:, 2056:2568]; dv = w_in0[:, 2568:3080]

    def swp(m):
        return m.reshape(D, 8, 2, 32)[:, :, ::-1, :].reshape(D, 512)
    w_in_ext = np.ascontiguousarray(np.concatenate([fq, fk, dq, swp(dq), dk, swp(dk), fv, dv, ff], axis=1))
    b_ada0 = A(b_ada)[0]
    common = {
        "w_ada": A(w_ada)[0], "b_adaT": np.ascontiguousarray(b_ada0.reshape(48, 128).T),
        "b_ada_g": np.ascontiguousarray(np.concatenate([b_ada0[2048:3072], b_ada0[5120:6144]])[None, :]),
        "w_in": w_in_ext, "b_f": np.ascontiguousarray(A(b_f)[0].reshape(8, 1)),
        "lam_v": A(lambda_vecs)[0].reshape(1, 256), "subln": A(subln_g)[0].reshape(128, 1),
        "w_o": A(w_o)[0], "lnp": np.ascontiguousarray(np.stack([A(ln1_g)[0], A(ln1_b)[0], A(ln2_g)[0], A(ln2_b)[0]])),
        "w_up": A(w_up)[0],
        "convw": np.ascontiguousarray(np.concatenate([A(conv_w)[0], A(conv_b)], axis=0).reshape(4, NF, 128).transpose(2, 1, 0)),
        "w_down": A(w_down)[0],
        "ident": np.eye(128, dtype=f32).astype(ml_dtypes.bfloat16),
        "tri": np.triu(np.ones((128, 128), dtype=f32)).astype(ml_dtypes.bfloat16),
    }
    Cp, Sp = _rope_tables(np.arange(T)); Cs, Ss = _rope_tables(P_LEN + np.arange(TS))
    common["ropeC_p"] = Cp; common["ropeS_p"] = Sp
    common["ropeC_s"] = np.ascontiguousarray(np.tile(Cs, (1, 4))); common["ropeS_s"] = np.ascontiguousarray(np.tile(Ss, (1, 4)))
    sel = np.zeros((6, 3, 128), dtype=f32)
    sel[0, 0, :] = 1; sel[1, 1, :] = 1
    for s in range(4):
        sel[2 + s, 2, s * 32:(s + 1) * 32] = 1
    common["sel"] = sel
    common["ones3"] = np.ones((3, T), dtype=f32).astype(ml_dtypes.bfloat16)
    cfk = A(cache_fox_k)[0]; cfv = A(cache_fox_v)[0]; clf = A(cache_fox_logf)[0]
    cdk = A(cache_diff_k)[0]; cdv = A(cache_diff_v)[0]; cst = A(state_ffn_conv)[0]
    c_prompt = A(c_prompt); c_sample = A(c_sample)
    in_maps = []
    for c in range(8):
        ps_ = slice(2 * c, 2 * c + 2); ss_ = slice(4 * c, 4 * c + 4)
        xp = x_prompt[ps_]
        xs = x_sample[ss_].reshape(128, D)
        call = np.concatenate([c_prompt[ps_], c_sample[ss_]], axis=0)
        m = dict(common)
        m["xT_p"] = np.ascontiguousarray(xp.reshape(2, T, 8, 128).transpose(0, 3, 2, 1))
        m["x_p"] = np.ascontiguousarray(xp)
        m["xT_s"] = np.ascontiguousarray(xs.reshape(128, 8, 128).transpose(2, 1, 0))
        m["x_s"] = np.ascontiguousarray(xs)
        m["cT"] = np.ascontiguousarray(call.reshape(6, 8, 128).transpose(2, 1, 0))
        m["fkT_s"] = np.ascontiguousarray(cfk[ss_].reshape(4, P_LEN, 512).transpose(0, 2, 1))
        m["fv_s"] = np.ascontiguousarray(cfv[ss_].reshape(4, P_LEN, 512))
        m["lfT_s"] = np.ascontiguousarray(clf[ss_].transpose(0, 2, 1))
        m["dkT_s"] = np.ascontiguousarray(cdk[ss_].reshape(4, P_LEN, 512).transpose(0, 2, 1))
        m["dv_s"] = np.ascontiguousarray(cdv[ss_].reshape(4, P_LEN, 512))
        m["convT_s"] = np.ascontiguousarray(cst[ss_].reshape(4, 2, NF, 128).transpose(3, 2, 0, 1))
        in_maps.append(m)
    if _PREP_ONLY:
        return in_maps
    if _NC is None:
        _NC = build()
    res = run_bass_kernel_spmd(_NC, in_maps, core_ids=list(range(8)))
    R = res.results
    yp = np.concatenate([r["y_p"] for r in R], axis=0)
    ys = np.concatenate([r["y_s"].reshape(4, TS, D) for r in R], axis=0)

    def catp(n0, n1):
        return [a for r in R for a in (r[n0], r[n1])]
    p_fk = np.stack([a.T.reshape(T, 8, 64) for a in catp("fkT_o0", "fkT_o1")])[None]
    p_fv = np.stack([a.reshape(T, 8, 64) for a in catp("fv_o0", "fv_o1")])[None]
    p_lf = np.stack([a.T for a in catp("lfT_o0", "lfT_o1")])[None]
    p_dk = np.stack([a.T.reshape(T, 8, 64) for a in catp("dkT_o0", "dkT_o1")])[None]
    p_dv = np.stack([a.reshape(T, 4, 128) for a in catp("dv_o0", "dv_o1")])[None]
    p_cv = np.stack([a.reshape(128, NF, 2).transpose(2, 1, 0).reshape(2, DFF) for a in catp("convT_o0", "convT_o1")])[None]
    s_fk = np.concatenate([r["sfkT_o"].T.reshape(4, TS, 8, 64) for r in R], axis=0)[None]
    s_fv = np.concatenate([r["sfv_o"].reshape(4, TS, 8, 64) for r in R], axis=0)[None]
    s_lf = np.concatenate([r["slfT_o"].T.reshape(4, TS, 8) for r in R], axis=0)[None]
    s_dk = np.concatenate([r["sdkT_o"].T.reshape(4, TS, 8, 64) for r in R], axis=0)[None]
    s_dv = np.concatenate([r["sdv_o"].reshape(4, TS, 4, 128) for r in R], axis=0)[None]
    s_cv = np.concatenate([r["sconvT_o"].transpose(2, 3, 1, 0).reshape(4, 2, DFF) for r in R], axis=0)[None]
    outs = (yp, ys, p_fk, p_fv, p_lf, p_dk, p_dv, p_cv, s_fk, s_fv, s_lf, s_dk, s_dv, s_cv)
    return tuple(np.ascontiguousarray(o, dtype=f32) for o in outs)
```
